# Optimizing a Trainium2 kernel written in Bass

```python
import jax, jax.numpy as jnp
from jax import lax
import numpy as np

D_MODEL = 1024
BATCH = 2
SEQ = 8192
DEPTH = 2
DEC_BATCH = 4
DEC_SEQ = 8192
PAST_LEN = 128

GRID_W = 64
D_FF = 2816
EPS = 1e-6
ROPE_THETA = 10000.0

GDN_H = 4
GDN_DK = 128
GDN_DV = 256
GDN_QK = GDN_H * GDN_DK
GDN_V = GDN_H * GDN_DV
GDN_CONV_W = 5
GDN_CHUNK = 64

RET_H = 4
RET_DK = 128
RET_DV = 256
RET_QK = RET_H * RET_DK
RET_V = RET_H * RET_DV
RET_CHUNK = 64

ATT_HQ = 8
ATT_HKV = 2
ATT_HD = 128
ATT_GROUP = ATT_HQ // ATT_HKV
ATT_Q = ATT_HQ * ATT_HD
ATT_KV = ATT_HKV * ATT_HD
Q_BLOCK = 128

N_BRANCH = 3
GDN_CONV_CH = 2 * GDN_QK + GDN_V
IN_SIZES = (GDN_CONV_CH, GDN_V, 2 * GDN_H, 2 * GDN_H,
            RET_QK, RET_QK, RET_V, RET_V,
            ATT_Q, ATT_KV, ATT_KV,
            N_BRANCH * D_MODEL)
N_IN = sum(IN_SIZES)

kernel_name = "hybrid_bidir_gdn_retention_axial_gqa_encoder"


def _split_points(sizes):
    pts, acc = [], 0
    for n in sizes[:-1]:
        acc += n
        pts.append(acc)
    return pts


def rms_norm(x, gain):
    xf = x.astype(jnp.float32)
    y = xf * lax.rsqrt(jnp.mean(xf * xf, axis=-1, keepdims=True) + EPS)
    return (y * gain.astype(jnp.float32)).astype(x.dtype)


def l2_norm(x):
    return x * lax.rsqrt(jnp.sum(x * x, axis=-1, keepdims=True) + EPS)


def swiglu(x, w1, w3, w2):
    return (jax.nn.silu(x @ w1) * (x @ w3)) @ w2


def rope_angles(pos, dim):
    inv = ROPE_THETA ** (-jnp.arange(0, dim, 2, dtype=jnp.float32) / dim)
    ang = pos.astype(jnp.float32)[:, None] * inv[None, :]
    return jnp.cos(ang), jnp.sin(ang)


def apply_rope(x, cos, sin):
    x1, x2 = jnp.split(x, 2, axis=-1)
    c = cos[None, :, None, :].astype(x.dtype)
    s = sin[None, :, None, :].astype(x.dtype)
    return jnp.concatenate([x1 * c - x2 * s, x1 * s + x2 * c], axis=-1)


def apply_axial_rope(x, rows, cols):
    half = x.shape[-1] // 2
    cr, sr = rope_angles(rows, half)
    cc, sc = rope_angles(cols, half)
    return jnp.concatenate([apply_rope(x[..., :half], cr, sr),
                            apply_rope(x[..., half:], cc, sc)], axis=-1)


def flip(t):
    return t[:, ::-1]


def to_chunks(x, c):
    b, s, h = x.shape[:3]
    y = x.reshape((b, s // c, c, h) + x.shape[3:])
    return jnp.moveaxis(y, 3, 1)


def from_chunks(y):
    b, h, n, c, d = y.shape
    return jnp.moveaxis(y, 1, 3).reshape(b, n * c, h, d)


def gated_delta_chunked(q, k, v, beta, g):
    c = GDN_CHUNK
    q, k, v = to_chunks(q, c), to_chunks(k, c), to_chunks(v, c)
    beta, g = to_chunks(beta, c), to_chunks(g, c)
    gc = jnp.cumsum(g, axis=-1)
    tril = jnp.tril(jnp.ones((c, c), dtype=bool))
    strict = jnp.tril(jnp.ones((c, c), dtype=bool), -1)
    decay = jnp.exp(jnp.where(tril, gc[..., :, None] - gc[..., None, :], -jnp.inf))
    kb = k * beta[..., None]
    lower = jnp.where(strict, jnp.einsum('bhnid,bhnjd->bhnij', kb, k) * decay, 0.0)
    a_mat = lower + jnp.eye(c, dtype=lower.dtype)
    rhs = jnp.concatenate([v * beta[..., None], kb * jnp.exp(gc)[..., None]], axis=-1)
    sol = lax.linalg.triangular_solve(a_mat, rhs, left_side=True, lower=True, unit_diagonal=True)
    u, w = sol[..., :GDN_DV], sol[..., GDN_DV:]
    qk = jnp.where(tril, jnp.einsum('bhnid,bhnjd->bhnij', q, k) * decay, 0.0)
    q_dec = q * jnp.exp(gc)[..., None]
    k_dec = k * jnp.exp(gc[..., -1:] - gc)[..., None]
    g_tot = jnp.exp(gc[..., -1])
    xs = tuple(jnp.moveaxis(t, 2, 0) for t in (qk, q_dec, k_dec, u, w, g_tot))

    def step(state, inp):
        qk_c, qd_c, kd_c, u_c, w_c, gt_c = inp
        v_new = u_c - jnp.einsum('bhcd,bhde->bhce', w_c, state)
        o = jnp.einsum('bhcd,bhde->bhce', qd_c, state) + jnp.einsum('bhij,bhje->bhie', qk_c, v_new)
        state = state * gt_c[..., None, None] + jnp.einsum('bhcd,bhce->bhde', kd_c, v_new)
        return state, o

    b, h = q.shape[:2]
    s0 = jnp.zeros((b, h, GDN_DK, GDN_DV), jnp.float32)
    _, o = lax.scan(step, s0, xs)
    return from_chunks(jnp.moveaxis(o, 0, 2))


def retention_chunked(q, k, v, log_gamma):
    c = RET_CHUNK
    q, k, v = to_chunks(q, c), to_chunks(k, c), to_chunks(v, c)
    idx = jnp.arange(c, dtype=jnp.float32)
    rel = idx[:, None] - idx[None, :]
    dmat = jnp.exp(jnp.where(rel >= 0, rel * log_gamma[:, None, None], -jnp.inf))
    intra = jnp.einsum('bhnid,bhnjd->bhnij', q, k) * dmat[:, None]
    o_intra = jnp.einsum('bhnij,bhnje->bhnie', intra, v)
    xi = jnp.exp((idx + 1.0)[None, :] * log_gamma[:, None])
    zeta = jnp.exp((c - 1.0 - idx)[None, :] * log_gamma[:, None])
    g_chunk = jnp.exp(c * log_gamma)
    qx = q * xi[:, None, :, None]
    kz = k * zeta[:, None, :, None]
    xs = (jnp.moveaxis(qx, 2, 0), jnp.moveaxis(kz, 2, 0), jnp.moveaxis(v, 2, 0))

    def step(r, inp):
        qc, kc, vc = inp
        o = jnp.einsum('bhcd,bhde->bhce', qc, r)
        r = r * g_chunk[:, None, None] + jnp.einsum('bhcd,bhce->bhde', kc, vc)
        return r, o

    b, h = q.shape[:2]
    r0 = jnp.zeros((b, h, RET_DK, RET_DV), jnp.float32)
    _, o_cross = lax.scan(step, r0, xs)
    return from_chunks(o_intra + jnp.moveaxis(o_cross, 0, 2))


def block_attention(q, k, v):
    b, s = q.shape[:2]
    nblk = s // Q_BLOCK
    qb = q.reshape(b, nblk, Q_BLOCK, ATT_HKV, ATT_GROUP, ATT_HD).transpose(1, 0, 3, 4, 2, 5)
    scale = ATT_HD ** -0.5

    def one_block(qi):
        sc = jnp.einsum('bkgqd,bskd->bkgqs', qi, k, preferred_element_type=jnp.float32) * scale
        p = jax.nn.softmax(sc, axis=-1).astype(v.dtype)
        return jnp.einsum('bkgqs,bskd->bkgqd', p, v)

    o = lax.map(one_block, qb)
    return o.transpose(1, 0, 4, 2, 3, 5).reshape(b, s, ATT_Q)


def token_mixer(u, lp):
    b, s, _ = u.shape
    dt = u.dtype
    f32 = jnp.float32
    proj = u @ lp['w_in']
    (gdn_qkv, gdn_z, gdn_b, gdn_a, r_q, r_k, r_v, r_g,
     a_q, a_k, a_v, gates) = jnp.split(proj, _split_points(IN_SIZES), axis=-1)

    conv = lax.conv_general_dilated(
        gdn_qkv, lp['gdn_conv'][:, None, :], window_strides=(1,),
        padding=[(GDN_CONV_W // 2, GDN_CONV_W // 2)],
        dimension_numbers=('NWC', 'WIO', 'NWC'), feature_group_count=GDN_CONV_CH)
    conv = jax.nn.silu(conv).astype(f32)
    gq, gk, gv = jnp.split(conv, [GDN_QK, 2 * GDN_QK], axis=-1)
    gq = l2_norm(gq.reshape(b, s, GDN_H, GDN_DK)) * (GDN_DK ** -0.5)
    gk = l2_norm(gk.reshape(b, s, GDN_H, GDN_DK))
    gv = gv.reshape(b, s, GDN_H, GDN_DV)
    beta = jax.nn.sigmoid(gdn_b.astype(f32)).reshape(b, s, 2, GDN_H)
    logdec = -jnp.exp(lp['gdn_A_log'].astype(f32)) * jax.nn.softplus(
        gdn_a.astype(f32).reshape(b, s, 2, GDN_H) + lp['gdn_dt_bias'].astype(f32))
    go = (gated_delta_chunked(gq, gk, gv, beta[:, :, 0], logdec[:, :, 0])
          + flip(gated_delta_chunked(flip(gq), flip(gk), flip(gv),
                                     flip(beta[:, :, 1]), flip(logdec[:, :, 1]))))
    go = go * lax.rsqrt(jnp.mean(go * go, axis=-1, keepdims=True) + EPS) * lp['gdn_norm'].astype(f32)
    go = go.astype(dt) * jax.nn.silu(gdn_z.reshape(b, s, GDN_H, GDN_DV))
    branch_a = go.reshape(b, s, GDN_V) @ lp['w_branch_gdn']

    cos, sin = rope_angles(jnp.arange(s), RET_DK)
    rq = apply_rope(r_q.astype(f32).reshape(b, s, RET_H, RET_DK), cos, sin)
    rk = apply_rope(r_k.astype(f32).reshape(b, s, RET_H, RET_DK), cos, sin) * (RET_DK ** -0.5)
    rv = r_v.astype(f32).reshape(b, s, RET_H, RET_DV)
    lg = jax.nn.log_sigmoid(lp['ret_decay_logit'].astype(f32))
    ro = (retention_chunked(rq, rk, rv, lg[0])
          + flip(retention_chunked(flip(rq), flip(rk), flip(rv), lg[1])))
    mu = jnp.mean(ro, axis=-1, keepdims=True)
    var = jnp.mean(jnp.square(ro - mu), axis=-1, keepdims=True)
    ro = ((ro - mu) * lax.rsqrt(var + EPS)).reshape(b, s, RET_V) * lp['ret_norm'].astype(f32)
    ro = jax.nn.silu(r_g) * ro.astype(dt)
    branch_b = ro @ lp['w_branch_ret']

    n_rows = s // GRID_W
    rows = jnp.repeat(jnp.arange(n_rows), GRID_W)
    cols = jnp.tile(jnp.arange(GRID_W), n_rows)
    aq = apply_axial_rope(rms_norm(a_q.reshape(b, s, ATT_HQ, ATT_HD), lp['attn_q_norm']), rows, cols)
    ak = apply_axial_rope(rms_norm(a_k.reshape(b, s, ATT_HKV, ATT_HD), lp['attn_k_norm']), rows, cols)
    av = a_v.reshape(b, s, ATT_HKV, ATT_HD)
    branch_c = block_attention(aq, ak, av) @ lp['w_branch_attn']

    gt = jax.nn.sigmoid(gates.astype(f32)).astype(dt).reshape(b, s, N_BRANCH, D_MODEL)
    merged = gt[:, :, 0] * branch_a + gt[:, :, 1] * branch_b + gt[:, :, 2] * branch_c
    return merged @ lp['w_out']


def encoder_layer(x, lp):
    x = x + 0.5 * swiglu(rms_norm(x, lp['ffn1_norm']), lp['ffn1_w1'], lp['ffn1_w3'], lp['ffn1_w2'])
    x = x + token_mixer(rms_norm(x, lp['mix_norm']), lp)
    x = x + 0.5 * swiglu(rms_norm(x, lp['ffn2_norm']), lp['ffn2_w1'], lp['ffn2_w3'], lp['ffn2_w2'])
    return x


def setup_inputs(seed: int = 0) -> dict:
    key = jax.random.key(seed)
    ks = jax.random.split(key, 32)
    f32 = jnp.float32
    L, D = DEPTH, D_MODEL

    def nrm(k, shape, fan_in):
        return jax.random.normal(k, shape, f32) * (fan_in ** -0.5)

    def gain(k, shape):
        return 1.0 + 0.02 * jax.random.normal(k, shape, f32)

    dt0 = jnp.exp(jax.random.uniform(ks[8], (L, 2, GDN_H), f32, np.log(1e-3), np.log(1e-1)))
    gamma0 = 1.0 - 2.0 ** (-5.0 - jnp.arange(RET_H, dtype=f32))
    logit0 = jnp.log(gamma0) - jnp.log1p(-gamma0)
    return {
        "x_prompt": jax.random.normal(ks[0], (BATCH, SEQ, D), f32),
        "x_sample": jax.random.normal(ks[1], (DEC_BATCH, DEC_SEQ, D), f32),
        "ffn1_norm": gain(ks[2], (L, D)),
        "ffn1_w1": nrm(ks[3], (L, D, D_FF), D),
        "ffn1_w3": nrm(ks[4], (L, D, D_FF), D),
        "ffn1_w2": nrm(ks[5], (L, D_FF, D), D_FF),
        "mix_norm": gain(ks[6], (L, D)),
        "w_in": nrm(ks[7], (L, D, N_IN), D),
        "gdn_conv": nrm(ks[9], (L, GDN_CONV_W, GDN_CONV_CH), GDN_CONV_W),
        "gdn_A_log": jnp.log(jax.random.uniform(ks[10], (L, 2, GDN_H), f32, 1.0, 16.0)),
        "gdn_dt_bias": dt0 + jnp.log(-jnp.expm1(-dt0)),
        "gdn_norm": gain(ks[11], (L, GDN_DV)),
        "ret_decay_logit": logit0 + 0.1 * jax.random.normal(ks[12], (L, 2, RET_H), f32),
        "ret_norm": gain(ks[13], (L, RET_V)),
        "attn_q_norm": gain(ks[14], (L, ATT_HD)),
        "attn_k_norm": gain(ks[15], (L, ATT_HD)),
        "w_branch_gdn": nrm(ks[16], (L, GDN_V, D), GDN_V),
        "w_branch_ret": nrm(ks[17], (L, RET_V, D), RET_V),
        "w_branch_attn": nrm(ks[18], (L, ATT_Q, D), ATT_Q),
        "w_out": nrm(ks[19], (L, D, D), D),
        "ffn2_norm": gain(ks[20], (L, D)),
        "ffn2_w1": nrm(ks[21], (L, D, D_FF), D),
        "ffn2_w3": nrm(ks[22], (L, D, D_FF), D),
        "ffn2_w2": nrm(ks[23], (L, D_FF, D), D_FF),
    }


def reference(x_prompt, x_sample, ffn1_norm, ffn1_w1, ffn1_w3, ffn1_w2, mix_norm, w_in,
              gdn_conv, gdn_A_log, gdn_dt_bias, gdn_norm, ret_decay_logit, ret_norm,
              attn_q_norm, attn_k_norm, w_branch_gdn, w_branch_ret, w_branch_attn, w_out,
              ffn2_norm, ffn2_w1, ffn2_w3, ffn2_w2):
    layers = []
    for l in range(DEPTH):
        layers.append(dict(
            ffn1_norm=ffn1_norm[l], ffn1_w1=ffn1_w1[l], ffn1_w3=ffn1_w3[l], ffn1_w2=ffn1_w2[l],
            mix_norm=mix_norm[l], w_in=w_in[l], gdn_conv=gdn_conv[l], gdn_A_log=gdn_A_log[l],
            gdn_dt_bias=gdn_dt_bias[l], gdn_norm=gdn_norm[l], ret_decay_logit=ret_decay_logit[l],
            ret_norm=ret_norm[l], attn_q_norm=attn_q_norm[l], attn_k_norm=attn_k_norm[l],
            w_branch_gdn=w_branch_gdn[l], w_branch_ret=w_branch_ret[l],
            w_branch_attn=w_branch_attn[l], w_out=w_out[l],
            ffn2_norm=ffn2_norm[l], ffn2_w1=ffn2_w1[l], ffn2_w3=ffn2_w3[l], ffn2_w2=ffn2_w2[l]))

    y_prompt = x_prompt
    for l in range(DEPTH):
        y_prompt = encoder_layer(y_prompt, layers[l])
    y_sample = x_sample
    for l in range(DEPTH):
        y_sample = encoder_layer(y_sample, layers[l])
    return (y_prompt, y_sample)
```

```python
import contextlib
import numpy as np
import ml_dtypes
import concourse.bass as bass
import concourse.mybir as mybir
from concourse.bass_utils import run_bass_kernel_spmd

F32 = mybir.dt.float32
BF16 = mybir.dt.bfloat16
AF = mybir.ActivationFunctionType
ALU = mybir.AluOpType
AX = mybir.AxisListType

ENGS = ("pe", "act", "dve", "pool", "sp")
NSLOT = 12


class Buf:
    __slots__ = ("name", "w", "r")

    def __init__(self, name):
        self.name = name
        self.w = None
        self.r = []


class Op:
    __slots__ = ("eng", "fn", "dma", "deps", "sig", "sigval", "slot", "dmaval", "epoch")

    def __init__(self, eng, fn, dma):
        self.eng = eng
        self.fn = fn
        self.dma = dma
        self.deps = []
        self.sig = False
        self.sigval = 0
        self.slot = -1
        self.dmaval = 0
        self.epoch = 0


class T:
    __slots__ = ("t", "buf")

    def __init__(self, t, buf):
        self.t = t
        self.buf = buf

    def __getitem__(self, k):
        return self.t[k]


class Rot:
    def __init__(self, tiles):
        self.tiles = tiles
        self.i = 0

    def next(self):
        t = self.tiles[self.i % len(self.tiles)]
        self.i += 1
        return t


class FW:
    def __init__(self, nc):
        self.nc = nc
        self.ops = {e: [] for e in ENGS}
        self.ndma = {e: 0 for e in ENGS}
        self.bufs = []
        self.last = {e: None for e in ENGS}
        self.recent_dma = {e: [] for e in ENGS}
        self.base = (nc.sbuf_base + 63) // 64 * 64
        self.top = nc.sbuf_top
        self.ptr = self.base
        self.uid = 0
        self.epoch = 0

    def reset_arena(self, keep=None):
        self.ptr = self.base if keep is None else keep

    def sb(self, name, shape, dtype=F32):
        esz = 4 if dtype == F32 else 2
        n = 1
        for s in shape[1:]:
            n *= s
        nbytes = (n * esz + 63) // 64 * 64
        assert self.ptr + nbytes <= self.top, "SBUF arena overflow at %s (%d)" % (name, self.ptr + nbytes - self.top)
        self.uid += 1
        t = self.nc.alloc_sbuf_tensor_at("%s_%d" % (name, self.uid), list(shape), dtype, offset=self.ptr)
        self.ptr += nbytes
        b = Buf(name)
        self.bufs.append(b)
        return T(t, b)

    def ps(self, name, shape, dtype=F32):
        t = self.nc.alloc_psum_tensor(name, list(shape), dtype)
        b = Buf(name)
        self.bufs.append(b)
        return T(t, b)

    def _rec(self, op, reads, writes):
        raw, other = [], []
        for t in reads:
            b = t.buf
            if b.w is not None:
                raw.append(b.w)
        for t in writes:
            b = t.buf
            if b.w is not None:
                other.append(b.w)
            other.extend(b.r)
        seen = set()
        for lst, is_raw in ((raw, True), (other, False)):
            for d in lst:
                if d is op or id(d) in seen:
                    continue
                if (not d.dma) and (not op.dma) and d.eng == op.eng:
                    if op.eng == "pe" or not is_raw:
                        continue
                seen.add(id(d))
                op.deps.append(d)
                if not d.dma:
                    d.sig = True
        op.epoch = self.epoch
        for t in reads:
            r = t.buf.r
            if not op.dma:
                r[:] = [x for x in r if x.dma or x.eng != op.eng]
            r.append(op)
        for t in writes:
            t.buf.w = op
            t.buf.r = []
        self.ops[op.eng].append(op)
        if not op.dma:
            self.last[op.eng] = op
        return op

    def op(self, eng, fn, reads=(), writes=()):
        return self._rec(Op(eng, fn, False), reads, writes)

    def dma(self, eng, out, in_, reads=(), writes=()):
        o = Op(eng, (out, in_), True)
        n = self.ndma[eng]
        self.ndma[eng] = n + 1
        o.slot = n % NSLOT
        o.dmaval = 16 * (n // NSLOT + 1)
        self._rec(o, reads, writes)
        rd = self.recent_dma[eng]
        rd.append(o)
        if len(rd) > NSLOT:
            rd.pop(0)
        return o

    def barrier(self):
        lastc = [self.last[e] for e in ENGS if self.last[e] is not None]
        dmas = [d for e in ENGS for d in self.recent_dma[e]]
        for e in ENGS:
            o = Op(e, None, False)
            o.epoch = self.epoch
            for d in lastc:
                if d.eng != e:
                    o.deps.append(d)
                    d.sig = True
            o.deps.extend(dmas)
            self.ops[e].append(o)
        for b in self.bufs:
            b.w = None
            b.r = []
        self.epoch += 1

    def emit(self):
        nc = self.nc
        used = set()
        for e in ENGS:
            c = {}
            for o in self.ops[e]:
                if (not o.dma) and o.sig:
                    c[o.epoch] = c.get(o.epoch, 0) + 1
                    o.sigval = c[o.epoch]
                    used.add((e, o.epoch))
        with contextlib.ExitStack() as st:
            csem = {k: st.enter_context(nc.semaphore("c_%s_%d" % k)) for k in sorted(used)}
            dsem = {e: [st.enter_context(nc.semaphore("d_%s%d" % (e, i))) for i in range(NSLOT)]
                    for e in ENGS if self.ndma[e] > 0}
            block = st.enter_context(nc.Block())

            def run(e, eng):
                known = {}
                for o in self.ops[e]:
                    waits = {}
                    for d in o.deps:
                        if d.dma:
                            key = ("d", d.eng, d.slot)
                            v = d.dmaval
                        else:
                            key = ("c", d.eng, d.epoch)
                            v = d.sigval
                        if known.get(key, 0) >= v:
                            continue
                        if waits.get(key, 0) < v:
                            waits[key] = v
                    if o.dma and o.dmaval > 16:
                        key = ("d", e, o.slot)
                        v = o.dmaval - 16
                        if known.get(key, 0) < v and waits.get(key, 0) < v:
                            waits[key] = v
                    for key, v in waits.items():
                        sem = csem[(key[1], key[2])] if key[0] == "c" else dsem[key[1]][key[2]]
                        eng.wait_ge(sem, v)
                        known[key] = v
                    if o.dma:
                        out, in_ = o.fn
                        eng.dma_start(out=out, in_=in_).then_inc(dsem[e][o.slot], 16)
                    elif o.fn is not None:
                        ins = o.fn(eng)
                        if o.sig:
                            ins.then_inc(csem[(e, o.epoch)], 1)
                    elif o.sig:
                        eng.nop().then_inc(csem[(e, o.epoch)], 1)

            @block.tensor
            def _(eng):
                run("pe", eng)

            @block.scalar
            def _(eng):
                run("act", eng)

            @block.vector
            def _(eng):
                run("dve", eng)

            @block.gpsimd
            def _(eng):
                run("pool", eng)

            @block.sync
            def _(eng):
                run("sp", eng)

D = 1024
DC = 8
FF = 2816
FC = 22
NIN = 10768
EPS = 1e-6
QKV0, Z0, B0, A0, RQ0, RK0, RV0, RG0, AQ0, AK0, AV0, GT0 = (
    0, 2048, 3072, 3080, 3088, 3600, 4112, 5136, 6160, 7184, 7440, 7696)
BIG = 30000.0
WNAMES = ("ffn1_w1", "ffn1_w3", "ffn1_w2", "w_in", "w_branch_gdn", "w_branch_ret",
          "w_branch_attn", "w_out", "ffn2_w1", "ffn2_w3", "ffn2_w2")
WSHAPES = {"ffn1_w1": (D, FF), "ffn1_w3": (D, FF), "ffn1_w2": (FF, D), "w_in": (D, NIN),
           "w_branch_gdn": (D, D), "w_branch_ret": (D, D), "w_branch_attn": (D, D), "w_out": (D, D),
           "ffn2_w1": (D, FF), "ffn2_w3": (D, FF), "ffn2_w2": (FF, D)}
MUL, ADD, SUB = ALU.mult, ALU.add, ALU.subtract


class K:
    def __init__(self, nc, S, L, dbg=(), phases=None):
        self.nc = nc
        self.S = S
        self.L = L
        self.NT = S // 512
        self.fw = FW(nc)
        self.dbg = set(dbg)
        self.phases = phases
        fw = self.fw
        di = lambda n, s, dt=F32: nc.dram_tensor(n, list(s), dt, kind="ExternalInput").ap()
        self.x_in = di("x", (S, D))
        self.y_out = nc.dram_tensor("y", [S, D], F32, kind="ExternalOutput").ap()
        self.wsrc = {n: di(n, (L,) + WSHAPES[n]) for n in WNAMES}
        self.gains_d = di("gains_fm", (L, 128, 3, 8))
        self.conv_d = di("conv_fm", (L, 128, 16, 5))
        self.gdnnorm_d = di("gdnnorm_rep", (L, 128, 256))
        self.retnorm_d = di("retnorm_rep", (L, 128, 1024))
        self.qkcol_d = di("qk_gain_col", (L, 128, 2))
        self.qkrep_d = di("qk_gain_rep", (L, 128, 2, 128))
        self.alog_d = di("alog_rep", (L, 128, 4, 8))
        self.dtb_d = di("dtb_rep", (L, 128, 4, 8))
        self.rlogit_d = di("rlogit_rep", (L, 128, 8))
        self.cf_d = di("c_f32", (128, 9, 128))
        self.cb_d = di("c_bf16", (128, 2, 128), BF16)
        self.rope_d = di("rope_tab", (6, 128, S))
        self.scr = {}
        sc = self.scratch
        self.wbf = {n: sc("bf_" + n, (L,) + WSHAPES[n], BF16) for n in WNAMES}
        sc("xT", (DC, 128, S)); sc("x1T", (DC, 128, S))
        sc("qkv_pre", (16, 128, S + 4))
        sc("zs", (S, 1024), BF16); sc("gb", (S, 16))
        sc("rqT", (4, 128, S), BF16); sc("rkT", (4, 128, S), BF16)
        sc("rv", (S, 1024), BF16); sc("rgs", (S, 1024), BF16)
        sc("aqT", (8, 128, S), BF16); sc("akT", (2, 128, S), BF16); sc("av", (S, 256), BF16)
        sc("gatesT", (24, 128, S), BF16)
        sc("ofwd", (S, 1024)); sc("ofwd_r", (S, 1024))
        sc("gqT", (4, 128, S), BF16); sc("gkT", (4, 128, S), BF16); sc("gvT", (8, 128, S), BF16)
        sc("goT", (8, 128, S), BF16); sc("roT", (8, 128, S), BF16); sc("aoT", (8, 128, S), BF16)
        self.PP = [nc.alloc_psum_tensor("PP%d" % i, [128, 2, 512], F32) for i in range(4)]
        self.P = []
        for i in range(8):
            b = Buf("P%d" % i)
            fw.bufs.append(b)
            self.P.append(T(self.PP[i // 2][:, i % 2, :], b))
        self.build()

    def scratch(self, name, shape, dt=F32):
        kind = "ExternalOutput" if name in self.dbg else "Internal"
        t = self.nc.dram_tensor(name, list(shape), dt, kind=kind).ap()
        self.scr[name] = t
        return t

    def dump(self, name, tile, ap=None, dt=F32):
        if ("dbg_" + name) not in self.dbg or ("dbg_" + name) in self.scr:
            return
        ap = tile[:] if ap is None else ap
        d = self.nc.dram_tensor("dbg_" + name, [int(x) for x in ap.shape], dt, kind="ExternalOutput").ap()
        self.scr["dbg_" + name] = d
        self.fw.dma("pool", d, ap, reads=[tile])

    def mm(self, out, lhsT, rhs, reads, writes, start=True, stop=True):
        self.fw.op("pe", lambda e: e.matmul(out, lhsT, rhs, start=start, stop=stop), reads, writes)

    def act(self, out, in_, func, reads, writes, **kw):
        self.fw.op("act", lambda e: e.activation(out, in_, func, **kw), reads, writes)

    def amul(self, out, in_, m, reads, writes):
        self.fw.op("act", lambda e: e.mul(out, in_, m), reads, writes)

    def acopy(self, out, in_, reads, writes):
        self.fw.op("act", lambda e: e.copy(out, in_), reads, writes)

    def vcopy(self, out, in_, reads, writes):
        self.fw.op("dve", lambda e: e.tensor_copy(out, in_), reads, writes)

    def tt(self, out, a, b, op, reads, writes):
        self.fw.op("dve", lambda e: e.tensor_tensor(out, a, b, op=op), reads, writes)

    def ts(self, out, a, s1, op0, reads, writes):
        self.fw.op("dve", lambda e: e.tensor_scalar(out, a, s1, None, op0=op0), reads, writes)

    def stt(self, out, a, s, b, op0, op1, reads, writes):
        self.fw.op("dve", lambda e: e.scalar_tensor_tensor(out, a, s, b, op0=op0, op1=op1), reads, writes)

    def recip(self, out, in_, reads, writes):
        self.fw.op("dve", lambda e: e.reciprocal(out, in_), reads, writes)

    def memset(self, out, val, writes):
        self.fw.op("dve", lambda e: e.memset(out, val), (), writes)

    def build(self):
        fw = self.fw
        self.cf = fw.sb("cf", [128, 9, 128])
        self.cb = fw.sb("cb", [128, 2, 128], BF16)
        fw.dma("sp", self.cf[:], self.cf_d, writes=[self.cf])
        fw.dma("sp", self.cb[:], self.cb_d, writes=[self.cb])
        self.keep = fw.ptr
        self.identf = self.cf[:, 0, :]
        self.onesf = self.cf[:, 1, :]
        self.identb = self.cb[:, 0, :]
        self.onesb = self.cb[:, 1, :]
        ph = self.phases
        on = lambda p: ph is None or p in ph
        if on("cast"):
            self.cast_weights()
            fw.barrier()
        for l in range(self.L):
            if on("A"):
                self.phase_A(l)
                fw.barrier()
            if ph is None:
                for dr in (0, 1):
                    fw.reset_arena(self.keep)
                    live = [self.sweep(l, "gdn", dr, reset=False, lean=(dr == 1)),
                            self.sweep(l, "ret", dr, reset=False, lean=(dr == 1))]
                    while live:
                        for g in list(live):
                            try:
                                next(g)
                            except StopIteration:
                                live.remove(g)
                    fw.barrier()
            else:
                for kind in ("gdn", "ret"):
                    for dr in (0, 1):
                        if on(kind) or on(kind + str(dr)):
                            for _ in self.sweep(l, kind, dr):
                                pass
                            fw.barrier()
            if on("att"):
                self.attention(l)
                fw.barrier()
            if on("E"):
                self.phase_E(l)
                fw.barrier()
        fw.emit()

    def cast_weights(self):
        fw = self.fw
        for l in range(self.L):
            for n in WNAMES:
                R, C = WSHAPES[n]
                for r0 in range(0, R, 128):
                    fw.dma("pool", self.wbf[n][l, r0:r0 + 128, :], self.wsrc[n][l, r0:r0 + 128, :])
        z = self.cf[:, 8, 0:2]
        for c in range(16):
            fw.dma("sp", self.scr["qkv_pre"][c, :, 0:2], z, reads=[self.cf])
            fw.dma("sp", self.scr["qkv_pre"][c, :, self.S + 2:self.S + 4], z, reads=[self.cf])

    def rmsnorm(self, xT, gains, gi, xn, sqrot, rsd, ps):
        for c in range(DC):
            sq = sqrot.next()
            self.act(sq[:], xT[:, c, :], AF.Square, [xT], [sq])
            self.mm(ps[:], self.onesf, sq[:], [sq, self.cf], [ps], start=(c == 0), stop=(c == DC - 1))
        self.act(rsd[:], ps[:], AF.Sqrt, [ps], [rsd], bias=EPS, scale=1.0 / D)
        self.recip(rsd[:], rsd[:], [rsd], [rsd])
        for c in range(DC):
            self.stt(xn[:, c, :], xT[:, c, :], gains[:, gi, c:c + 1], rsd[:], MUL, MUL, [xT, rsd, gains], [xn])

    def ffn(self, l, pre, xT, xn, gT, w13rot, w2rot, srot, prot):
        fw = self.fw
        W1, W3, W2 = self.wbf[pre + "_w1"], self.wbf[pre + "_w3"], self.wbf[pre + "_w2"]
        for g in range(FC // 2):
            w = w13rot.next()
            f0 = g * 256
            fw.dma("sp", w[:, 0], W1[l, :, f0:f0 + 256].rearrange("(k p) f -> p k f", p=128), writes=[w])
            fw.dma("sp", w[:, 1], W3[l, :, f0:f0 + 256].rearrange("(k p) f -> p k f", p=128), writes=[w])
            for j in range(2):
                fc = g * 2 + j
                p1 = prot.next()
                p3 = prot.next()
                for k in range(DC):
                    for which, pp in ((0, p1), (1, p3)):
                        self.mm(pp[:], w[:, which, k, j * 128:(j + 1) * 128], xn[:, k, :], [w, xn], [pp],
                                start=(k == 0), stop=(k == DC - 1))
                s = srot.next()
                self.act(s[:], p1[:], AF.Silu, [p1], [s])
                self.tt(gT[:, fc, :], s[:], p3[:], MUL, [s, p3], [gT])
        for g in range(4):
            w = w2rot.next()
            d0 = g * 256
            fw.dma("sp", w[:], W2[l, :, d0:d0 + 256].rearrange("(k p) f -> p k f", p=128), writes=[w])
            pps = [prot.next(), prot.next()]
            for k in range(FC):
                for j in range(2):
                    self.mm(pps[j][:], w[:, k, j * 128:(j + 1) * 128], gT[:, k, :], [w, gT], [pps[j]], start=(k == 0), stop=(k == FC - 1))
            for j in range(2):
                dc = g * 2 + j
                self.stt(xT[:, dc, :], pps[j][:], 0.5, xT[:, dc, :], MUL, ADD, [pps[j], xT], [xT])

    def phase_A(self, l):
        fw, S, scr = self.fw, self.S, self.scr
        fw.reset_arena(self.keep)
        gains = fw.sb("gains", [128, 3, 8])
        fw.dma("sp", gains[:], self.gains_d[l], writes=[gains])
        qkcol = fw.sb("qkcol", [128, 2])
        fw.dma("sp", qkcol[:], self.qkcol_d[l], writes=[qkcol])
        alog = fw.sb("alog", [128, 4, 8]); dtb = fw.sb("dtb", [128, 4, 8])
        fw.dma("sp", alog[:], self.alog_d[l], writes=[alog])
        fw.dma("sp", dtb[:], self.dtb_d[l], writes=[dtb])
        self.act(alog[:], alog[:], AF.Exp, [alog], [alog])
        self.ts(alog[:], alog[:], -1.0, MUL, [alog], [alog])
        xT = fw.sb("xT", [128, DC, 512])
        xn = fw.sb("xn", [128, DC, 512], BF16)
        gT = fw.sb("gT", [128, FC, 512], BF16)
        w13rot = Rot([fw.sb("w13_%d" % i, [128, 2, DC, 256], BF16) for i in range(2)])
        w2rot = Rot([fw.sb("w2_%d" % i, [128, FC, 256], BF16) for i in range(2)])
        wgrot = Rot([fw.sb("wg_%d" % i, [128, DC, 512], BF16) for i in range(2)])
        wsm = fw.sb("wsm", [128, DC, 16], BF16)
        srot = Rot([fw.sb("s_%d" % i, [128, 512]) for i in range(3)])
        sqrot = Rot([fw.sb("sq_%d" % i, [128, 512]) for i in range(3)])
        rsd = fw.sb("rsd", [128, 512])
        rsd2 = Rot([fw.sb("rsd2_%d" % i, [128, 512]) for i in range(2)])
        fst = Rot([fw.sb("fst_%d" % i, [128, 512]) for i in range(4)])
        bst = Rot([fw.sb("bst_%d" % i, [128, 512], BF16) for i in range(4)])
        tm = Rot([fw.sb("tm_%d" % i, [128, 4, 1024], BF16) for i in range(2)])
        tmv = fw.sb("tmv", [128, 4, 256], BF16)
        gbs = fw.sb("gbs", [128, 4, 16])
        gtmp = fw.sb("gtmp", [128, 4, 8])
        rope = fw.sb("rope", [128, 6, 512])
        xs = Rot([fw.sb("xs_%d" % i, [128, 512]) for i in range(3)])
        t1r = Rot([fw.sb("t1_%d" % i, [128, 512]) for i in range(2)])
        t2r = Rot([fw.sb("t2_%d" % i, [128, 512]) for i in range(2)])
        xin = fw.sb("xin", [128, 4, 256]) if l == 0 else None
        prot = Rot(self.P)
        Wi = self.wbf["w_in"]
        fw.dma("sp", wsm[:], Wi[l, :, B0:B0 + 16].rearrange("(k p) f -> p k f", p=128), writes=[wsm])

        def wload(c0, n):
            w = wgrot.next()
            fw.dma("sp", w[:, :, 0:n], Wi[l, :, c0:c0 + n].rearrange("(k p) f -> p k f", p=128), writes=[w])
            return w

        def fm_chunk(w, j, pp):
            for k in range(DC):
                self.mm(pp[:], w[:, k, j * 128:(j + 1) * 128], xn[:, k, :], [w, xn], [pp], start=(k == 0), stop=(k == DC - 1))

        def tm_block(w, n, blk, pp):
            for k in range(DC):
                self.mm(pp[:, 0:n], xn[:, k, blk * 128:(blk + 1) * 128], w[:, k, 0:n], [w, xn], [pp], start=(k == 0), stop=(k == DC - 1))

        def do_rope(src, ci, si, pairs, out):
            t1 = t1r.next(); t2 = t2r.next()
            self.tt(t1[:], src[:], rope[:, ci, :], MUL, [src, rope], [t1])
            for (dl, sl, n) in pairs:
                self.tt(t2[dl:dl + n, :], src[sl:sl + n, :], rope[sl:sl + n, si, :], MUL, [src, rope], [t2])
            self.tt(out[:], t1[:], t2[:], ADD, [t1, t2], [out])

        PR = [(0, 64, 64), (64, 0, 64)]
        PA = [(0, 32, 32), (32, 0, 32), (64, 96, 32), (96, 64, 32)]

        for tt in range(self.NT):
            t0 = tt * 512
            if l == 0:
                for dq in range(4):
                    fw.dma("sp", xin[:], self.x_in[t0:t0 + 512, dq * 256:(dq + 1) * 256].rearrange("(n p) d -> p n d", p=128),
                           writes=[xin])
                    for j in range(2):
                        pp = prot.next()
                        for blk in range(4):
                            self.mm(pp[:, blk * 128:(blk + 1) * 128], xin[:, blk, j * 128:(j + 1) * 128], self.identf, [xin, self.cf], [pp])
                        c = dq * 2 + j
                        if j:
                            self.acopy(xT[:, c, :], pp[:], [pp], [xT])
                        else:
                            self.vcopy(xT[:, c, :], pp[:], [pp], [xT])
            else:
                fw.dma("sp", xT[:], scr["xT"][:, :, t0:t0 + 512].rearrange("c p t -> p c t"), writes=[xT])
            fw.dma("sp", rope[:], self.rope_d[:, :, t0:t0 + 512].rearrange("r p t -> p r t"), writes=[rope])
            self.rmsnorm(xT, gains, 0, xn, sqrot, rsd, prot.next())
            self.ffn(l, "ffn1", xT, xn, gT, w13rot, w2rot, srot, prot)
            fw.dma("pool", scr["x1T"][:, :, t0:t0 + 512].rearrange("c p t -> p c t"), xT[:], reads=[xT])
            self.rmsnorm(xT, gains, 1, xn, sqrot, rsd, prot.next())
            for g in range(4):
                w = wload(QKV0 + g * 512, 512)
                for j in range(4):
                    pp = prot.next(); fm_chunk(w, j, pp)
                    st = fst.next()
                    if j % 2:
                        self.acopy(st[:], pp[:], [pp], [st])
                    else:
                        self.vcopy(st[:], pp[:], [pp], [st])
                    fw.dma("pool", scr["qkv_pre"][g * 4 + j, :, 2 + t0:2 + t0 + 512], st[:], reads=[st])
            for (c0, dst, silu) in ((Z0, "zs", True), (RV0, "rv", False), (RG0, "rgs", True)):
                tmt = tm.next()
                for g in range(2):
                    w = wload(c0 + g * 512, 512)
                    for blk in range(4):
                        pp = prot.next(); tm_block(w, 512, blk, pp)
                        if silu:
                            self.act(tmt[:, blk, g * 512:(g + 1) * 512], pp[:], AF.Silu, [pp], [tmt])
                        else:
                            self.vcopy(tmt[:, blk, g * 512:(g + 1) * 512], pp[:], [pp], [tmt])
                fw.dma("pool", scr[dst][t0:t0 + 512, :].rearrange("(n p) c -> p n c", p=128), tmt[:], reads=[tmt])
            w = wload(AV0, 256)
            for blk in range(4):
                pp = prot.next(); tm_block(w, 256, blk, pp)
                self.vcopy(tmv[:, blk, :], pp[:, 0:256], [pp], [tmv])
            fw.dma("pool", scr["av"][t0:t0 + 512, :].rearrange("(n p) c -> p n c", p=128), tmv[:], reads=[tmv])
            pp = prot.next()
            for blk in range(4):
                for k in range(DC):
                    self.mm(pp[:, blk * 16:(blk + 1) * 16], xn[:, k, blk * 128:(blk + 1) * 128], wsm[:, k, :], [wsm, xn], [pp],
                            start=(k == 0), stop=(k == DC - 1))
            pv = pp[:, 0:64].rearrange("p (n c) -> p n c", c=16)
            self.act(gbs[:, :, 0:8], pv[:, :, 0:8], AF.Sigmoid, [pp], [gbs])
            self.acopy(gtmp[:], pv[:, :, 8:16], [pp], [gtmp])
            self.tt(gtmp[:], gtmp[:], dtb[:], ADD, [gtmp, dtb], [gtmp])
            self.act(gtmp[:], gtmp[:], AF.Exp, [gtmp], [gtmp])
            self.act(gtmp[:], gtmp[:], AF.Ln, [gtmp], [gtmp], bias=1.0)
            self.tt(gbs[:, :, 8:16], gtmp[:], alog[:], MUL, [gtmp, alog], [gbs])
            fw.dma("pool", scr["gb"][t0:t0 + 512, :].rearrange("(n p) c -> p n c", p=128), gbs[:], reads=[gbs])
            for (c0, dst, ci) in ((RQ0, "rqT", 0), (RK0, "rkT", 2)):
                w = wload(c0, 512)
                for j in range(4):
                    pp = prot.next(); fm_chunk(w, j, pp)
                    x_sb = xs.next()
                    self.acopy(x_sb[:], pp[:], [pp], [x_sb])
                    o = bst.next()
                    do_rope(x_sb, ci, ci + 1, PR, o)
                    fw.dma("pool", scr[dst][j, :, t0:t0 + 512], o[:], reads=[o])
            jobs = []
            for g in range(2):
                jobs += [("aqT", AQ0 + g * 512, g * 4 + j, j, 0) for j in range(4)]
            jobs += [("akT", AK0, j, j, 1) for j in range(2)]
            wcur = {}
            pend = None

            def stage2(job, x_sb, sq):
                dst, c0, oc, j, gi = job
                p2 = prot.next()
                self.mm(p2[:], self.onesf, sq[:], [sq, self.cf], [p2])
                r = rsd2.next()
                self.act(r[:], p2[:], AF.Sqrt, [p2], [r], bias=EPS, scale=1.0 / 128)
                self.recip(r[:], r[:], [r], [r])
                self.stt(x_sb[:], x_sb[:], qkcol[:, gi:gi + 1], r[:], MUL, MUL, [x_sb, r, qkcol], [x_sb])
                o = bst.next()
                do_rope(x_sb, 4, 5, PA, o)
                fw.dma("pool", scr[dst][oc, :, t0:t0 + 512], o[:], reads=[o])

            for job in jobs:
                dst, c0, oc, j, gi = job
                if c0 not in wcur:
                    wcur[c0] = wload(c0, 512 if gi == 0 else 256)
                w = wcur[c0]
                pp = prot.next(); fm_chunk(w, j, pp)
                sq = sqrot.next(); x_sb = xs.next()
                self.act(sq[:], pp[:], AF.Square, [pp], [sq])
                self.acopy(x_sb[:], pp[:], [pp], [x_sb])
                if pend is not None:
                    stage2(*pend)
                pend = (job, x_sb, sq)
            stage2(*pend)
            for g in range(6):
                w = wload(GT0 + g * 512, 512)
                for j in range(4):
                    pp = prot.next(); fm_chunk(w, j, pp)
                    o = bst.next()
                    self.act(o[:], pp[:], AF.Sigmoid, [pp], [o])
                    fw.dma("pool", scr["gatesT"][g * 4 + j, :, t0:t0 + 512], o[:], reads=[o])

    def sweep(self, l, kind, dr, reset=True, lean=False):
        fw, S, scr = self.fw, self.S, self.scr
        if reset:
            fw.reset_arena(self.keep)
        gdn = kind == "gdn"
        OFW = "ofwd" if gdn else "ofwd_r"
        cf, cb = self.cf, self.cb
        tri, pos, neg = cf[:, 2 + dr, :], cf[:, 4 + dr, :], cf[:, 6 + dr, :]
        identb, identf, onesf = self.identb, self.identf, self.onesf
        prot = Rot(self.P)
        H = range(4)
        GB = 2
        St = [fw.sb("S%d" % h, [128, 256]) for h in H]
        Sb = [fw.sb("Sb%d" % h, [128, 256], BF16) for h in H]
        for h in H:
            self.memset(St[h][:], 0.0, [St[h]])
            self.memset(Sb[h][:], 0.0, [Sb[h]])
        R2 = lambda n, shp, dt=F32, k=2: Rot([fw.sb("%s_%d" % (n, i), shp, dt) for i in range(k)])
        BH = lambda n, shp, dt=F32: [[fw.sb("%s_%d_%d" % (n, b, h), shp, dt) for h in H] for b in range(GB)]
        NB = GB + 1
        gcs, negc, egc, kds, egt, dtmp = (R2(n, [128, 4], F32, NB) for n in ("gcs", "negc", "egc", "kds", "egt", "dtmp"))
        Gbr = R2("Gb", [128, 128], F32, 4)
        gsbr = R2("gsb", [128, 8], F32, NB)
        ETb, ATb, kdb = BH("ET", [128, 128]), BH("AT", [128, 128], BF16), BH("kd", [128, 128], BF16)
        tmpBr = R2("tmpB", [128, 256], F32, 4)
        NBUF = 1 if lean else 2
        o_r = R2("o_sb", [128, 1024], F32, NBUF)
        if gdn:
            negb, bge = R2("negb", [128, 4], F32, NB), R2("bge", [128, 4], F32, NB)
            EAb = BH("EA", [128, 128])
            kbgb, vbb = BH("kbg", [128, 128], BF16), BH("vb", [128, 256], BF16)
            Qb = [BH("Qa", [128, 128]), BH("Qb", [128, 128])]
            QTb = [BH("QTa", [128, 128]), BH("QTb", [128, 128])]
            PTb_ = [BH("PTa", [128, 128]), BH("PTb", [128, 128])]
            PTh = BH("PTh", [128, 128], BF16)
            usb, wTb = BH("us", [128, 256]), BH("wT", [128, 128], BF16)
            vnr = [R2("vn%d" % h, [128, 256], BF16) for h in H]
            if dr == 0:
                xq = fw.sb("xq", [128, 16, 516])
                convw = fw.sb("convw", [128, 16, 5])
                fw.dma("sp", convw[:], self.conv_d[l], writes=[convw])
                accr = R2("acc", [128, 512], F32, 4)
                sqr = R2("sq", [128, 512], F32, 4)
                rsr = R2("rs", [128, 512], F32, 4)
            vT = fw.sb("vT", [128, 8, 512], BF16)
            gbt = fw.sb("gbt", [128, 4, 16])
        else:
            rvt = fw.sb("rvt", [128, 4, 1024], BF16)
            gl = fw.sb("gl", [128, 8])
            fw.dma("sp", gl[:], self.rlogit_d[l], writes=[gl])
            self.act(gl[:], gl[:], AF.Exp, [gl], [gl], scale=-1.0)
            self.act(gl[:], gl[:], AF.Ln, [gl], [gl], bias=1.0)
            self.ts(gl[:], gl[:], -1.0, MUL, [gl], [gl])
        qT = fw.sb("qT", [128, 4, 512], BF16)
        kT = fw.sb("kT", [128, 4, 512], BF16)
        if dr == 1:
            ofw = R2("ofw", [128, 1024], F32, NBUF)
            osum = fw.sb("osum", [128, 1024])
            gate = R2("gate", [128, 1024], BF16, NBUF)
            yb = R2("yb", [128, 1024], BF16, NBUF)
            yT = R2("yT", [128, 8, 512], BF16, NBUF)
            nrm = fw.sb("nrm", [128, 256 if gdn else 1024])
            fw.dma("sp", nrm[:], (self.gdnnorm_d if gdn else self.retnorm_d)[l], writes=[nrm])
            junk = R2("junk", [128, 256], F32, 2)
            st4 = R2("st4", [128, 4], F32, 4)
            cen = [fw.sb("cen%d" % h, [128, 256]) for h in H] if not gdn else None

        def decay_prep(g, gbuf, bi):
            pg = prot.next()
            self.mm(pg[:, 0:4], tri, g, [gbuf, cf], [pg])
            self.mm(pg[:, 4:8], onesf, g, [gbuf, cf], [pg])
            c_gcs, c_negc, c_egc, c_kds, c_egt, c_d = (r.next() for r in (gcs, negc, egc, kds, egt, dtmp))
            gsb = gsbr.next()
            self.acopy(gsb[:], pg[:, 0:8], [pg], [gsb])
            self.vcopy(c_gcs[:], gsb[:, 0:4], [gsb], [c_gcs])
            self.act(c_egc[:], gsb[:, 0:4], AF.Exp, [gsb], [c_egc])
            self.tt(c_d[:], gsb[:, 4:8], gsb[:, 0:4], SUB, [gsb], [c_d])
            self.act(c_kds[:], c_d[:], AF.Exp, [c_d], [c_kds])
            self.act(c_egt[:], gsb[:, 4:8], AF.Exp, [gsb], [c_egt])
            self.ts(c_negc[:], gsb[:, 0:4], -1.0, MUL, [gsb], [c_negc])
            Gbs = []
            for h in H:
                Gb = Gbr.next()
                self.ts(Gb[:], onesf, g[:, h:h + 1], MUL, [gbuf, cf], [Gb])
                Gbs.append(Gb)
            pcs, pas = [], []
            for h in H:
                pc = prot.next()
                self.mm(pc[:, 0:128], Gbs[h][:], tri, [Gbs[h], cf], [pc], start=True, stop=False)
                self.mm(pc[:, 0:128], identf, neg, [cf], [pc], start=False, stop=True)
                if gdn:
                    self.mm(pc[:, 128:256], Gbs[h][:], tri, [Gbs[h], cf], [pc], start=True, stop=False)
                    self.mm(pc[:, 128:256], identf, pos, [cf], [pc], start=False, stop=True)
                pcs.append(pc)
            for h in H:
                self.act(ETb[bi][h][:], pcs[h][:, 0:128], AF.Exp, [pcs[h], c_negc], [ETb[bi][h]], bias=c_negc[:, h:h + 1])
                if gdn:
                    self.act(EAb[bi][h][:], pcs[h][:, 128:256], AF.Exp, [pcs[h], c_gcs], [EAb[bi][h]], bias=c_gcs[:, h:h + 1], scale=-1.0)
            return dict(gcs=c_gcs, egc=c_egc, kds=c_kds, egt=c_egt, ET=ETb[bi], EA=EAb[bi] if gdn else None)

        if not gdn:
            dec_c = decay_prep(gl[:, dr * 4:dr * 4 + 4], gl, 0)

        def norm2(c, sl, sq):
            p2 = prot.next()
            self.mm(p2[:], onesf, sq[:], [sq, cf], [p2])
            r = rsr.next()
            self.act(r[:], p2[:], AF.Sqrt, [p2], [r], bias=EPS)
            self.recip(r[:], r[:], [r], [r])
            dst = qT if c < 4 else kT
            sc_ = (128.0 ** -0.5) if c < 4 else 1.0
            self.stt(dst[:, c % 4, :], sl[:], sc_, r[:], MUL, MUL, [sl, r], [dst])

        import os
        STOP = int(os.environ.get("SW_STOP", "99"))
        tiles = list(range(self.NT))
        blks = list(range(4))
        if dr == 1:
            tiles.reverse(); blks.reverse()
        for tt in tiles:
            t0 = tt * 512
            if gdn and dr == 1:
                fw.dma("sp", qT[:], scr["gqT"][:, :, t0:t0 + 512].rearrange("c p t -> p c t"), writes=[qT])
                fw.dma("sp", kT[:], scr["gkT"][:, :, t0:t0 + 512].rearrange("c p t -> p c t"), writes=[kT])
                fw.dma("sp", vT[:], scr["gvT"][:, :, t0:t0 + 512].rearrange("c p t -> p c t"), writes=[vT])
                fw.dma("sp", gbt[:], scr["gb"][t0:t0 + 512, :].rearrange("(n p) c -> p n c", p=128), writes=[gbt])
            elif gdn:
                for c4 in range(0, 16, 2):
                    fw.dma("sp", xq[:, c4:c4 + 2, :], scr["qkv_pre"][c4:c4 + 2, :, t0:t0 + 516].rearrange("c p t -> p c t"), writes=[xq])
                fw.dma("sp", gbt[:], scr["gb"][t0:t0 + 512, :].rearrange("(n p) c -> p n c", p=128), writes=[gbt])
                for cg in range(0, 16, 4):
                    accs = [accr.next() for _ in range(4)]
                    for i in range(4):
                        c = cg + i
                        self.amul(accs[i][:], xq[:, c, 0:512], convw[:, c, 0:1], [xq, convw], [accs[i]])
                    for w in range(1, 5):
                        for i in range(4):
                            c = cg + i
                            self.stt(accs[i][:], xq[:, c, w:w + 512], convw[:, c, w:w + 1], accs[i][:], MUL, ADD, [xq, convw, accs[i]], [accs[i]])
                    if cg >= 8:
                        for i in range(4):
                            self.act(vT[:, cg + i - 8, :], accs[i][:], AF.Silu, [accs[i]], [vT])
                    else:
                        sqs = []
                        for i in range(4):
                            self.act(accs[i][:], accs[i][:], AF.Silu, [accs[i]], [accs[i]])
                        for i in range(4):
                            sq = sqr.next()
                            self.act(sq[:], accs[i][:], AF.Square, [accs[i]], [sq])
                            sqs.append(sq)
                        for i in range(4):
                            norm2(cg + i, accs[i], sqs[i])
                fw.dma("pool", scr["gqT"][:, :, t0:t0 + 512].rearrange("c p t -> p c t"), qT[:], reads=[qT])
                fw.dma("pool", scr["gkT"][:, :, t0:t0 + 512].rearrange("c p t -> p c t"), kT[:], reads=[kT])
                fw.dma("pool", scr["gvT"][:, :, t0:t0 + 512].rearrange("c p t -> p c t"), vT[:], reads=[vT])
            else:
                fw.dma("sp", qT[:], scr["rqT"][:, :, t0:t0 + 512].rearrange("c p t -> p c t"), writes=[qT])
                fw.dma("sp", kT[:], scr["rkT"][:, :, t0:t0 + 512].rearrange("c p t -> p c t"), writes=[kT])
                fw.dma("sp", rvt[:], scr["rv"][t0:t0 + 512, :].rearrange("(n p) c -> p n c", p=128), writes=[rvt])
            if dr == 1:
                yTt = yT.next()
            for g0 in range(0, 4, GB):
                grp = blks[g0:g0 + GB]
                BHs = [(bi, blk, h) for bi, blk in enumerate(grp) for h in H]
                bsl = {blk: slice(blk * 128, (blk + 1) * 128) for blk in grp}
                dec, beta, c_negb, c_bge = {}, {}, {}, {}
                for bi, blk in enumerate(grp):
                    if gdn:
                        dec[blk] = decay_prep(gbt[:, blk, 8 + dr * 4:12 + dr * 4], gbt, bi)
                        beta[blk] = gbt[:, blk, dr * 4:dr * 4 + 4]
                        c_negb[blk], c_bge[blk] = negb.next(), bge.next()
                        self.ts(c_negb[blk][:], beta[blk], -1.0, MUL, [gbt], [c_negb[blk]])
                        self.tt(c_bge[blk][:], beta[blk], dec[blk]["egc"][:], MUL, [gbt, dec[blk]["egc"]], [c_bge[blk]])
                    else:
                        dec[blk] = dec_c
                AT, kd, kbg, vb, Q, QT, PT = {}, {}, {}, {}, {}, {}, {}
                for sub in range(0, len(BHs), 4):
                    SUBB = BHs[sub:sub + 4]
                    ps1 = {}
                    for (bi, blk, h) in SUBB:
                        bs = bsl[blk]
                        qb, kb = qT[:, h, bs], kT[:, h, bs]
                        pa = prot.next()
                        self.mm(pa[:, 0:128], kb, qb, [qT, kT], [pa])
                        if gdn:
                            self.mm(pa[:, 128:256], kb, kb, [kT], [pa])
                        pb = prot.next()
                        self.mm(pb[:, 0:128], kb, identb, [kT, cb], [pb])
                        if gdn:
                            for j in range(2):
                                self.mm(pb[:, 128 + j * 128:256 + j * 128], vT[:, 2 * h + j, bs], identb, [vT, cb], [pb])
                        ps1[(blk, h)] = (pa, pb)
                    for (bi, blk, h) in SUBB:
                        pa, pb = ps1[(blk, h)]
                        d = dec[blk]
                        AT[(blk, h)] = ATb[bi][h]
                        self.tt(ATb[bi][h][:], pa[:, 0:128], d["ET"][h][:], MUL, [pa, d["ET"][h]], [ATb[bi][h]])
                        kd[(blk, h)] = kdb[bi][h]
                        self.amul(kdb[bi][h][:], pb[:, 0:128], d["kds"][:, h:h + 1], [pb, d["kds"]], [kdb[bi][h]])
                        if gdn:
                            kbg[(blk, h)] = kbgb[bi][h]
                            self.amul(kbgb[bi][h][:], pb[:, 0:128], c_bge[blk][:, h:h + 1], [pb, c_bge[blk]], [kbgb[bi][h]])
                            vb[(blk, h)] = vbb[bi][h]
                            self.amul(vbb[bi][h][:], pb[:, 128:384], beta[blk][:, h:h + 1], [pb, gbt], [vbb[bi][h]])
                            Q[(blk, h)] = Qb[0][bi][h]
                            self.stt(Qb[0][bi][h][:], pa[:, 128:256], c_negb[blk][:, h:h + 1], d["EA"][h][:], MUL, MUL,
                                     [pa, c_negb[blk], d["EA"][h]], [Qb[0][bi][h]])
                yield
                if gdn:
                    psn = {}
                    for (bi, blk, h) in BHs:
                        pn = prot.next()
                        self.mm(pn[:, 0:128], Q[(blk, h)][:], identf, [Q[(blk, h)], cf], [pn])
                        psn[(blk, h)] = pn
                    for (bi, blk, h) in BHs:
                        pn = psn[(blk, h)]
                        QT[(blk, h)] = QTb[0][bi][h]; PT[(blk, h)] = PTb_[0][bi][h]
                        self.acopy(QT[(blk, h)][:], pn[:, 0:128], [pn], [QT[(blk, h)]])
                        self.tt(PT[(blk, h)][:], QT[(blk, h)][:], identf, ADD, [QT[(blk, h)], cf], [PT[(blk, h)]])
                    yield
                    for lev in range(1, 7):
                        pi = lev % 2
                        psq = {}
                        for (bi, blk, h) in BHs:
                            k_ = (blk, h)
                            pq = prot.next()
                            self.mm(pq[:, 0:128], QT[k_][:], Q[k_][:], [QT[k_], Q[k_]], [pq])
                            if lev < 6:
                                self.mm(pq[:, 128:256], Q[k_][:], QT[k_][:], [QT[k_], Q[k_]], [pq])
                            psq[k_] = pq
                        for n_, (bi, blk, h) in enumerate(BHs):
                            k_ = (blk, h)
                            pq = psq[k_]
                            cp = self.acopy if (n_ % 2 == 0 or lev == 6) else self.vcopy
                            Qn = Qb[pi][bi][h]
                            cp(Qn[:], pq[:, 0:128], [pq], [Qn])
                            if lev < 6:
                                QTn = QTb[pi][bi][h]
                                cp(QTn[:], pq[:, 128:256], [pq], [QTn])
                                QT[k_] = QTn
                            Q[k_] = Qn
                        psp = {}
                        for (bi, blk, h) in BHs:
                            k_ = (blk, h)
                            pp = prot.next()
                            self.mm(pp[:, 0:128], Q[k_][:], PT[k_][:], [Q[k_], PT[k_]], [pp])
                            psp[k_] = pp
                        for (bi, blk, h) in BHs:
                            k_ = (blk, h)
                            PTn = PTb_[pi][bi][h]
                            self.tt(PTn[:], psp[k_][:, 0:128], PT[k_][:], ADD, [psp[k_], PT[k_]], [PTn])
                            PT[k_] = PTn
                        yield
                    for n_, (bi, blk, h) in enumerate(BHs):
                        k_ = (blk, h)
                        if n_ % 2:
                            self.acopy(PTh[bi][h][:], PT[k_][:], [PT[k_]], [PTh[bi][h]])
                        else:
                            self.vcopy(PTh[bi][h][:], PT[k_][:], [PT[k_]], [PTh[bi][h]])
                    psu = {}
                    for (bi, blk, h) in BHs:
                        k_ = (blk, h)
                        pu = prot.next()
                        self.mm(pu[:, 0:256], PTh[bi][h][:], vb[k_][:], [PTh[bi][h], vb[k_]], [pu])
                        self.mm(pu[:, 256:384], kbg[k_][:], PTh[bi][h][:], [PTh[bi][h], kbg[k_]], [pu])
                        psu[k_] = pu
                    us, wT = {}, {}
                    for (bi, blk, h) in BHs:
                        k_ = (blk, h)
                        us[k_] = usb[bi][h]; wT[k_] = wTb[bi][h]
                        cp = self.acopy if h % 2 else self.vcopy
                        cp(us[k_][:], psu[k_][:, 0:256], [psu[k_]], [us[k_]])
                        cp(wT[k_][:], psu[k_][:, 256:384], [psu[k_]], [wT[k_]])
                yield
                for bi, blk in enumerate(grp):
                    if bi:
                        yield
                    bs = bsl[blk]
                    tok0 = t0 + blk * 128
                    d = dec[blk]
                    if dr == 1:
                        c_ofw = ofw.next()
                        fw.dma("sp", c_ofw[:], scr[OFW][tok0:tok0 + 128, :], writes=[c_ofw])
                        c_gate = gate.next()
                        fw.dma("sp", c_gate[:], scr["zs" if gdn else "rgs"][tok0:tok0 + 128, :], writes=[c_gate])
                    o_sb = o_r.next()
                    vnew = {}
                    if gdn:
                        pws = {}
                        for h in H:
                            pws[h] = prot.next()
                            self.mm(pws[h][:, 0:256], wT[(blk, h)][:], Sb[h][:], [wT[(blk, h)], Sb[h]], [pws[h]])
                        for h in H:
                            vn = vnr[h].next()
                            self.tt(vn[:], us[(blk, h)][:], pws[h][:, 0:256], SUB, [us[(blk, h)], pws[h]], [vn])
                            vnew[h] = (vn[:], vn)
                    else:
                        for h in H:
                            vnew[h] = (rvt[:, blk, h * 256:(h + 1) * 256], rvt)
                    poA, poB = {}, {}
                    for h in H:
                        vap, vbuf = vnew[h]
                        poA[h] = prot.next()
                        self.mm(poA[h][:, 0:256], qT[:, h, bs], Sb[h][:], [qT, Sb[h]], [poA[h]])
                        self.mm(poA[h][:, 256:512], kd[(blk, h)][:], vap, [kd[(blk, h)], vbuf], [poA[h]])
                        poB[h] = prot.next()
                        self.mm(poB[h][:, 0:256], AT[(blk, h)][:], vap, [AT[(blk, h)], vbuf], [poB[h]])
                    for h in H:
                        self.stt(St[h][:], St[h][:], d["egt"][:, h:h + 1], poA[h][:, 256:512], MUL, ADD, [St[h], poA[h], d["egt"]], [St[h]])
                        self.acopy(Sb[h][:], St[h][:], [St[h]], [Sb[h]])
                    for h in H:
                        tB = tmpBr.next()
                        self.acopy(tB[:], poB[h][:, 0:256], [poB[h]], [tB])
                        self.stt(o_sb[:, h * 256:(h + 1) * 256], poA[h][:, 0:256], d["egc"][:, h:h + 1], tB[:], MUL, ADD,
                                 [poA[h], tB, d["egc"]], [o_sb])
                    if dr == 0:
                        fw.dma("pool", scr[OFW][tok0:tok0 + 128, :], o_sb[:], reads=[o_sb])
                        continue
                    self.tt(osum[:], o_sb[:], c_ofw[:], ADD, [o_sb, c_ofw], [osum])
                    c_yb = yb.next()
                    s1, s2 = st4.next(), st4.next()
                    srcs = {}
                    if gdn:
                        for h in H:
                            srcs[h] = (osum, osum[:, h * 256:(h + 1) * 256])
                    else:
                        for h in H:
                            jk = junk.next()
                            self.act(jk[:], osum[:, h * 256:(h + 1) * 256], AF.Copy, [osum], [jk, s1], accum_out=s1[:, h:h + 1])
                        self.ts(s1[:], s1[:], -1.0 / 256, MUL, [s1], [s1])
                        for h in H:
                            self.ts(cen[h][:], osum[:, h * 256:(h + 1) * 256], s1[:, h:h + 1], ADD, [osum, s1], [cen[h]])
                            srcs[h] = (cen[h], cen[h][:])
                    for h in H:
                        jk = junk.next()
                        self.act(jk[:], srcs[h][1], AF.Square, [srcs[h][0]], [jk, s2], accum_out=s2[:, h:h + 1])
                    self.act(s2[:], s2[:], AF.Sqrt, [s2], [s2], bias=EPS, scale=1.0 / 256)
                    self.recip(s2[:], s2[:], [s2], [s2])
                    tBs = {}
                    for h in H:
                        hs = slice(h * 256, (h + 1) * 256)
                        nap = nrm[:, 0:256] if gdn else nrm[:, hs]
                        tBs[h] = tmpBr.next()
                        self.stt(tBs[h][:], srcs[h][1], s2[:, h:h + 1], nap, MUL, MUL, [srcs[h][0], s2, nrm], [tBs[h]])
                    for h in H:
                        hs = slice(h * 256, (h + 1) * 256)
                        self.tt(c_yb[:, hs], tBs[h][:], c_gate[:, hs], MUL, [tBs[h], c_gate], [c_yb])
                    pps = []
                    for g in range(2):
                        pp = prot.next()
                        for j in range(4):
                            c = g * 4 + j
                            self.mm(pp[:, j * 128:(j + 1) * 128], c_yb[:, c * 128:(c + 1) * 128], identb, [c_yb, cb], [pp])
                        pps.append(pp)
                    for g in range(2):
                        src3 = pps[g][:, 0:512].rearrange("p (c t) -> p c t", t=128)
                        if g:
                            self.acopy(yTt[:, g * 4:g * 4 + 4, bs], src3, [pps[g]], [yTt])
                        else:
                            self.vcopy(yTt[:, g * 4:g * 4 + 4, bs], src3, [pps[g]], [yTt])
                yield
            if dr == 1:
                fw.dma("pool", scr["goT" if gdn else "roT"][:, :, t0:t0 + 512].rearrange("c p t -> p c t"), yTt[:], reads=[yTt])
        yield

    def attention(self, l):
        fw, S, scr = self.fw, self.S, self.scr
        fw.reset_arena(self.keep)
        NK = S // 128
        qkrep = fw.sb("qkrep", [128, 2, 128])
        fw.dma("sp", qkrep[:], self.qkrep_d[l], writes=[qkrep])
        mx = fw.sb("mx", [128, 2]); nb = fw.sb("nb", [128, 1])
        self.act(qkrep[:], qkrep[:], AF.Abs, [qkrep], [qkrep])
        fw.op("dve", lambda e: e.reduce_max(mx[:], qkrep[:], axis=AX.X), [qkrep], [mx])
        self.stt(nb[:], mx[:, 0:1], -(128.0 ** 0.5), mx[:, 1:2], MUL, MUL, [mx], [nb])
        KT = fw.sb("KT", [128, S], BF16)
        Vt = fw.sb("Vt", [128, NK, 128], BF16)
        qrot = Rot([fw.sb("qt%d" % i, [128, 512], BF16) for i in range(3)])
        prob = Rot([fw.sb("pr%d" % i, [128, 2, 512], BF16) for i in range(4)])
        psr = Rot([fw.sb("prs%d" % i, [128, 512], BF16) for i in range(4)])
        rdn = Rot([fw.sb("rdn%d" % i, [128, 512]) for i in range(2)])
        orot = Rot([fw.sb("ot%d" % i, [128, 512], BF16) for i in range(2)])
        P = self.P
        sgrp = [(self.PP[0], P[0], P[1]), (self.PP[1], P[2], P[3]), (self.PP[2], P[4], P[5])]
        porot = Rot(P[6:7]); pdrot = Rot(P[7:8])
        sc = 128.0 ** -0.5
        assert NK % 2 == 0
        NP = NK // 2
        LA = 2
        for g in range(2):
            fw.dma("sp", KT[:], scr["akT"][g], writes=[KT])
            nstep = min(NK, 8)
            for n0 in range(0, NK, nstep):
                fw.dma("sp", Vt[:, n0:n0 + nstep, :],
                       scr["av"][n0 * 128:(n0 + nstep) * 128, g * 128:(g + 1) * 128].rearrange("(n p) e -> p n e", p=128), writes=[Vt])
            jobs = [(hq, tq, kp) for hq in range(4 * g, 4 * g + 4) for tq in range(self.NT) for kp in range(NP)]
            state = {}

            def tile_state(hq, tq):
                key = (hq, tq)
                if key not in state:
                    qt = qrot.next()
                    fw.dma("sp", qt[:], scr["aqT"][hq, :, tq * 512:tq * 512 + 512], writes=[qt])
                    state[key] = (qt, porot.next(), pdrot.next())
                return state[key]

            pend = []
            denq = []
            for i in range(len(jobs) + LA):
                if i < len(jobs):
                    hq, tq, kp = jobs[i]
                    qt, po, pd = tile_state(hq, tq)
                    PPt, b0, b1 = sgrp[i % 3]
                    for j, bk in ((0, b0), (1, b1)):
                        kt = 2 * kp + j
                        self.mm(bk[:], KT[:, kt * 128:(kt + 1) * 128], qt[:], [KT, qt], [bk])
                    pr = prob.next()
                    self.act(pr[:], PPt[:], AF.Exp, [b0, b1, nb], [pr], bias=nb[:, 0:1], scale=sc)
                    pend.append((hq, tq, kp, pr, po, pd))
                if i >= LA:
                    hq, tq, kp, pr, po, pd = pend.pop(0)
                    prs = psr.next()
                    self.tt(prs[:], pr[:, 0, :], pr[:, 1, :], ADD, [pr], [prs])
                    for j in range(2):
                        kt = 2 * kp + j
                        self.mm(po[:], Vt[:, kt, :], pr[:, j, :], [Vt, pr], [po], start=(kt == 0), stop=(kt == NK - 1))
                    if denq:
                        dprs, dpd, dkp = denq.pop(0)
                        self.mm(dpd[:], self.onesb, dprs[:], [self.cb, dprs], [dpd], start=(dkp == 0), stop=(dkp == NP - 1))
                    denq.append((prs, pd, kp))
                    if kp == NP - 1:
                        dprs, dpd, dkp = denq.pop(0)
                        self.mm(dpd[:], self.onesb, dprs[:], [self.cb, dprs], [dpd], start=(dkp == 0), stop=(dkp == NP - 1))
                    if kp == NP - 1:
                        r = rdn.next()
                        self.recip(r[:], pd[:], [pd], [r])
                        ot = orot.next()
                        self.tt(ot[:], po[:], r[:], MUL, [po, r], [ot])
                        fw.dma("pool", scr["aoT"][hq, :, tq * 512:tq * 512 + 512], ot[:], reads=[ot])
                        del state[(hq, tq)]

    def phase_E(self, l):
        fw, S, scr = self.fw, self.S, self.scr
        fw.reset_arena(self.keep)
        last = l == self.L - 1
        gains = fw.sb("gains", [128, 3, 8])
        fw.dma("sp", gains[:], self.gains_d[l], writes=[gains])
        xT = fw.sb("xT", [128, DC, 512])
        xn = fw.sb("xn", [128, DC, 512], BF16)
        gT = fw.sb("gT", [128, FC, 512], BF16)
        w13rot = Rot([fw.sb("w13_%d" % i, [128, 2, DC, 256], BF16) for i in range(2)])
        w2rot = Rot([fw.sb("w2_%d" % i, [128, FC, 256], BF16) for i in range(2)])
        wbr = Rot([fw.sb("wbr_%d" % i, [128, DC, 256], BF16) for i in range(4)])
        srot = Rot([fw.sb("s_%d" % i, [128, 512]) for i in range(3)])
        sqrot = Rot([fw.sb("sq_%d" % i, [128, 512]) for i in range(3)])
        rsd = fw.sb("rsd", [128, 512])
        bins = [fw.sb("bin_%d" % i, [128, DC, 512], BF16) for i in range(3)]
        gtr = Rot([fw.sb("gt_%d" % i, [128, 3, 2, 512], BF16) for i in range(2)])
        merged = fw.sb("merged", [128, DC, 512], BF16)
        m1r = Rot([fw.sb("m1_%d" % i, [128, 512]) for i in range(2)])
        m2r = Rot([fw.sb("m2_%d" % i, [128, 512]) for i in range(2)])
        ytr = Rot([fw.sb("yt_%d" % i, [128, 1024]) for i in range(2)]) if last else None
        prot = Rot(self.P)
        bnames = ("w_branch_gdn", "w_branch_ret", "w_branch_attn")
        bsrc = ("goT", "roT", "aoT")
        for tt in range(self.NT):
            t0 = tt * 512
            fw.dma("sp", xT[:], scr["x1T"][:, :, t0:t0 + 512].rearrange("c p t -> p c t"), writes=[xT])
            for b in range(3):
                fw.dma("sp", bins[b][:], scr[bsrc[b]][:, :, t0:t0 + 512].rearrange("c p t -> p c t"), writes=[bins[b]])
            for g in range(4):
                ws = []
                for b in range(3):
                    w = wbr.next()
                    fw.dma("sp", w[:], self.wbf[bnames[b]][l, :, g * 256:(g + 1) * 256].rearrange("(k p) f -> p k f", p=128), writes=[w])
                    ws.append(w)
                gt = gtr.next()
                for b in range(3):
                    fw.dma("sp", gt[:, b], scr["gatesT"][b * 8 + 2 * g:b * 8 + 2 * g + 2, :, t0:t0 + 512].rearrange("c p t -> p c t"), writes=[gt])
                for j in range(2):
                    dc = 2 * g + j
                    pb = []
                    for b in range(3):
                        pp = prot.next()
                        for k in range(DC):
                            self.mm(pp[:], ws[b][:, k, j * 128:(j + 1) * 128], bins[b][:, k, :], [ws[b], bins[b]], [pp],
                                    start=(k == 0), stop=(k == DC - 1))
                        pb.append(pp)
                    m1, m2 = m1r.next(), m2r.next()
                    self.tt(m1[:], pb[0][:], gt[:, 0, j, :], MUL, [pb[0], gt], [m1])
                    self.tt(m2[:], pb[1][:], gt[:, 1, j, :], MUL, [pb[1], gt], [m2])
                    self.tt(m1[:], m1[:], m2[:], ADD, [m1, m2], [m1])
                    self.tt(m2[:], pb[2][:], gt[:, 2, j, :], MUL, [pb[2], gt], [m2])
                    self.tt(merged[:, dc, :], m1[:], m2[:], ADD, [m1, m2], [merged])
            for g in range(4):
                w = wbr.next()
                fw.dma("sp", w[:], self.wbf["w_out"][l, :, g * 256:(g + 1) * 256].rearrange("(k p) f -> p k f", p=128), writes=[w])
                for j in range(2):
                    dc = 2 * g + j
                    pp = prot.next()
                    for k in range(DC):
                        self.mm(pp[:], w[:, k, j * 128:(j + 1) * 128], merged[:, k, :], [w, merged], [pp], start=(k == 0), stop=(k == DC - 1))
                    self.tt(xT[:, dc, :], xT[:, dc, :], pp[:], ADD, [xT, pp], [xT])
            self.rmsnorm(xT, gains, 2, xn, sqrot, rsd, prot.next())
            self.ffn(l, "ffn2", xT, xn, gT, w13rot, w2rot, srot, prot)
            if not last:
                fw.dma("pool", scr["xT"][:, :, t0:t0 + 512].rearrange("c p t -> p c t"), xT[:], reads=[xT])
            else:
                for blk in range(4):
                    yt = ytr.next()
                    for g in range(2):
                        pp = prot.next()
                        for j in range(4):
                            c = g * 4 + j
                            self.mm(pp[:, j * 128:(j + 1) * 128], xT[:, c, blk * 128:(blk + 1) * 128], self.identf, [xT, self.cf], [pp])
                        if g:
                            self.acopy(yt[:, g * 512:(g + 1) * 512], pp[:], [pp], [yt])
                        else:
                            self.vcopy(yt[:, g * 512:(g + 1) * 512], pp[:], [pp], [yt])
                    fw.dma("pool", self.y_out[t0 + blk * 128:t0 + (blk + 1) * 128, :], yt[:], reads=[yt])


def _rope_tables(S):
    f32 = np.float32
    theta = f32(10000.0)

    def angles(pos, dim):
        inv = theta ** (-(np.arange(0, dim, 2, dtype=f32)) / f32(dim))
        ang = pos.astype(f32)[:, None] * inv[None, :].astype(f32)
        return np.cos(ang).astype(f32), np.sin(ang).astype(f32)

    tab = np.zeros((6, 128, S), f32)
    c, s = angles(np.arange(S), 128)
    tab[0, :64], tab[0, 64:] = c.T, c.T
    tab[1, :64], tab[1, 64:] = s.T, -s.T
    k = f32(128.0 ** -0.5)
    tab[2], tab[3] = tab[0] * k, tab[1] * k
    rows = np.repeat(np.arange(S // 64), 64)
    cols = np.tile(np.arange(64), S // 64)
    cr, sr = angles(rows, 64)
    cc, sc = angles(cols, 64)
    tab[4, 0:32], tab[4, 32:64], tab[4, 64:96], tab[4, 96:128] = cr.T, cr.T, cc.T, cc.T
    tab[5, 0:32], tab[5, 32:64], tab[5, 64:96], tab[5, 96:128] = sr.T, -sr.T, sc.T, -sc.T
    return tab


def _consts():
    f32 = np.float32
    p = np.arange(128)[:, None]
    f = np.arange(128)[None, :]
    cf = np.zeros((128, 9, 128), f32)
    cf[:, 0] = np.eye(128, dtype=f32)
    cf[:, 1] = 1.0
    cf[:, 2] = (p <= f)
    cf[:, 3] = (p >= f)
    cf[:, 4] = BIG * (p <= f)
    cf[:, 5] = BIG * (p >= f)
    cf[:, 6] = -BIG * (f < p)
    cf[:, 7] = -BIG * (f > p)
    cb = np.zeros((128, 2, 128), f32)
    cb[:, 0] = np.eye(128, dtype=f32)
    cb[:, 1] = 1.0
    return cf, cb.astype(ml_dtypes.bfloat16)


def _small_params(inp, L):
    f32 = np.float32
    A = lambda k: np.asarray(inp[k], dtype=f32)
    fm = lambda v: np.ascontiguousarray(v.reshape(L, 8, 128).transpose(0, 2, 1))
    out = {}
    out["gains_fm"] = np.ascontiguousarray(np.stack([fm(A("ffn1_norm")), fm(A("mix_norm")), fm(A("ffn2_norm"))], axis=2))
    out["conv_fm"] = np.ascontiguousarray(A("gdn_conv").reshape(L, 5, 16, 128).transpose(0, 3, 2, 1))
    rep = lambda v: np.ascontiguousarray(np.broadcast_to(v[:, None], (L, 128) + v.shape[1:]))
    out["gdnnorm_rep"] = rep(A("gdn_norm"))
    out["retnorm_rep"] = rep(A("ret_norm"))
    out["qk_gain_col"] = np.ascontiguousarray(np.stack([A("attn_q_norm"), A("attn_k_norm")], axis=2))
    out["qk_gain_rep"] = rep(np.stack([A("attn_q_norm"), A("attn_k_norm")], axis=1))
    out["alog_rep"] = rep(np.broadcast_to(A("gdn_A_log").reshape(L, 1, 8), (L, 4, 8)))
    out["dtb_rep"] = rep(np.broadcast_to(A("gdn_dt_bias").reshape(L, 1, 8), (L, 4, 8)))
    out["rlogit_rep"] = rep(A("ret_decay_logit").reshape(L, 8))
    return out


_CACHE = {}


def _program(S, L, dbg=(), phases=None):
    key = (S, L, tuple(dbg), None if phases is None else tuple(phases))
    if key not in _CACHE:
        nc = bass.Bass("TRN2", target_bir_lowering=False)
        K(nc, S, L, dbg, phases)
        _CACHE[key] = nc
    return _CACHE[key]


def run_cores(seqs, inp, L, dbg=(), phases=None, n_cores=None):
    S = seqs[0].shape[0]
    nc = _program(S, L, dbg, phases)
    cf, cb = _consts()
    common = {n: np.ascontiguousarray(np.asarray(inp[n], dtype=np.float32)) for n in WNAMES}
    common.update(_small_params(inp, L))
    common["c_f32"] = cf
    common["c_bf16"] = cb
    common["rope_tab"] = _rope_tables(S)
    in_maps = []
    for s in seqs:
        m = dict(common)
        m["x"] = np.ascontiguousarray(np.asarray(s, dtype=np.float32))
        in_maps.append(m)
    res = run_bass_kernel_spmd(nc, in_maps, core_ids=list(range(len(seqs))))
    return res.results


def kernel(**inputs):
    xp = np.asarray(inputs["x_prompt"], dtype=np.float32)
    xs = np.asarray(inputs["x_sample"], dtype=np.float32)
    L = np.asarray(inputs["w_in"]).shape[0]
    seqs = [xp[i] for i in range(xp.shape[0])] + [xs[i] for i in range(xs.shape[0])]
    n_real = len(seqs)
    while len(seqs) < 8:
        seqs.append(seqs[0])
    res = run_cores(seqs, inputs, L)
    ys = [np.asarray(res[i]["y"], dtype=np.float32) for i in range(n_real)]
    y_prompt = np.stack(ys[:xp.shape[0]], axis=0)
    y_sample = np.stack(ys[xp.shape[0]:], axis=0)
    return (y_prompt, y_sample)
```

```python
import contextlib
import numpy as np
import ml_dtypes
import concourse.bass as bass
import concourse.mybir as mybir
from concourse.bass_utils import run_bass_kernel_spmd

F32 = mybir.dt.float32
BF16 = mybir.dt.bfloat16
AF = mybir.ActivationFunctionType
ALU = mybir.AluOpType
AX = mybir.AxisListType

ENGS = ("pe", "act", "dve", "pool", "sp")
NSLOT = 12


class Buf:
    __slots__ = ("name", "w", "r")

    def __init__(self, name):
        self.name = name
        self.w = None
        self.r = []


class Op:
    __slots__ = ("eng", "fn", "dma", "deps", "sig", "sigval", "slot", "dmaval", "epoch")

    def __init__(self, eng, fn, dma):
        self.eng = eng
        self.fn = fn
        self.dma = dma
        self.deps = []
        self.sig = False
        self.sigval = 0
        self.slot = -1
        self.dmaval = 0
        self.epoch = 0


class T:
    __slots__ = ("t", "buf")

    def __init__(self, t, buf):
        self.t = t
        self.buf = buf

    def __getitem__(self, k):
        return self.t[k]


class Rot:
    def __init__(self, tiles):
        self.tiles = tiles
        self.i = 0

    def next(self):
        t = self.tiles[self.i % len(self.tiles)]
        self.i += 1
        return t


class FW:
    def __init__(self, nc):
        self.nc = nc
        self.ops = {e: [] for e in ENGS}
        self.ndma = {e: 0 for e in ENGS}
        self.bufs = []
        self.last = {e: None for e in ENGS}
        self.recent_dma = {e: [] for e in ENGS}
        self.base = (nc.sbuf_base + 63) // 64 * 64
        self.top = nc.sbuf_top
        self.ptr = self.base
        self.uid = 0
        self.epoch = 0

    def reset_arena(self, keep=None):
        self.ptr = self.base if keep is None else keep

    def sb(self, name, shape, dtype=F32):
        esz = 4 if dtype == F32 else 2
        n = 1
        for s in shape[1:]:
            n *= s
        nbytes = (n * esz + 63) // 64 * 64
        assert self.ptr + nbytes <= self.top, "SBUF arena overflow at %s (%d)" % (name, self.ptr + nbytes - self.top)
        self.uid += 1
        t = self.nc.alloc_sbuf_tensor_at("%s_%d" % (name, self.uid), list(shape), dtype, offset=self.ptr)
        self.ptr += nbytes
        b = Buf(name)
        self.bufs.append(b)
        return T(t, b)

    def ps(self, name, shape, dtype=F32):
        t = self.nc.alloc_psum_tensor(name, list(shape), dtype)
        b = Buf(name)
        self.bufs.append(b)
        return T(t, b)

    def _rec(self, op, reads, writes):
        raw, other = [], []
        for t in reads:
            b = t.buf
            if b.w is not None:
                raw.append(b.w)
        for t in writes:
            b = t.buf
            if b.w is not None:
                other.append(b.w)
            other.extend(b.r)
        seen = set()
        for lst, is_raw in ((raw, True), (other, False)):
            for d in lst:
                if d is op or id(d) in seen:
                    continue
                if (not d.dma) and (not op.dma) and d.eng == op.eng:
                    if op.eng == "pe" or not is_raw:
                        continue
                seen.add(id(d))
                op.deps.append(d)
                if not d.dma:
                    d.sig = True
        op.epoch = self.epoch
        for t in reads:
            r = t.buf.r
            if not op.dma:
                r[:] = [x for x in r if x.dma or x.eng != op.eng]
            r.append(op)
        for t in writes:
            t.buf.w = op
            t.buf.r = []
        self.ops[op.eng].append(op)
        if not op.dma:
            self.last[op.eng] = op
        return op

    def op(self, eng, fn, reads=(), writes=()):
        return self._rec(Op(eng, fn, False), reads, writes)

    def dma(self, eng, out, in_, reads=(), writes=()):
        o = Op(eng, (out, in_), True)
        n = self.ndma[eng]
        self.ndma[eng] = n + 1
        o.slot = n % NSLOT
        o.dmaval = 16 * (n // NSLOT + 1)
        self._rec(o, reads, writes)
        rd = self.recent_dma[eng]
        rd.append(o)
        if len(rd) > NSLOT:
            rd.pop(0)
        return o

    def barrier(self):
        lastc = [self.last[e] for e in ENGS if self.last[e] is not None]
        dmas = [d for e in ENGS for d in self.recent_dma[e]]
        for e in ENGS:
            o = Op(e, None, False)
            o.epoch = self.epoch
            for d in lastc:
                if d.eng != e:
                    o.deps.append(d)
                    d.sig = True
            o.deps.extend(dmas)
            self.ops[e].append(o)
        for b in self.bufs:
            b.w = None
            b.r = []
        self.epoch += 1

    def emit(self):
        nc = self.nc
        used = set()
        for e in ENGS:
            c = {}
            for o in self.ops[e]:
                if (not o.dma) and o.sig:
                    c[o.epoch] = c.get(o.epoch, 0) + 1
                    o.sigval = c[o.epoch]
                    used.add((e, o.epoch))
        with contextlib.ExitStack() as st:
            csem = {k: st.enter_context(nc.semaphore("c_%s_%d" % k)) for k in sorted(used)}
            dsem = {e: [st.enter_context(nc.semaphore("d_%s%d" % (e, i))) for i in range(NSLOT)]
                    for e in ENGS if self.ndma[e] > 0}
            block = st.enter_context(nc.Block())

            def run(e, eng):
                known = {}
                for o in self.ops[e]:
                    waits = {}
                    for d in o.deps:
                        if d.dma:
                            key = ("d", d.eng, d.slot)
                            v = d.dmaval
                        else:
                            key = ("c", d.eng, d.epoch)
                            v = d.sigval
                        if known.get(key, 0) >= v:
                            continue
                        if waits.get(key, 0) < v:
                            waits[key] = v
                    if o.dma and o.dmaval > 16:
                        key = ("d", e, o.slot)
                        v = o.dmaval - 16
                        if known.get(key, 0) < v and waits.get(key, 0) < v:
                            waits[key] = v
                    for key, v in waits.items():
                        sem = csem[(key[1], key[2])] if key[0] == "c" else dsem[key[1]][key[2]]
                        eng.wait_ge(sem, v)
                        known[key] = v
                    if o.dma:
                        out, in_ = o.fn
                        eng.dma_start(out=out, in_=in_).then_inc(dsem[e][o.slot], 16)
                    elif o.fn is not None:
                        ins = o.fn(eng)
                        if o.sig:
                            ins.then_inc(csem[(e, o.epoch)], 1)
                    elif o.sig:
                        eng.nop().then_inc(csem[(e, o.epoch)], 1)

            @block.tensor
            def _(eng):
                run("pe", eng)

            @block.scalar
            def _(eng):
                run("act", eng)

            @block.vector
            def _(eng):
                run("dve", eng)

            @block.gpsimd
            def _(eng):
                run("pool", eng)

            @block.sync
            def _(eng):
                run("sp", eng)

D = 1024
DC = 8
FF = 2816
FC = 22
NIN = 10768
EPS = 1e-6
QKV0, Z0, B0, A0, RQ0, RK0, RV0, RG0, AQ0, AK0, AV0, GT0 = (
    0, 2048, 3072, 3080, 3088, 3600, 4112, 5136, 6160, 7184, 7440, 7696)
BIG = 30000.0
WNAMES = ("ffn1_w1", "ffn1_w3", "ffn1_w2", "w_in", "w_branch_gdn", "w_branch_ret",
          "w_branch_attn", "w_out", "ffn2_w1", "ffn2_w3", "ffn2_w2")
WSHAPES = {"ffn1_w1": (D, FF), "ffn1_w3": (D, FF), "ffn1_w2": (FF, D), "w_in": (D, NIN),
           "w_branch_gdn": (D, D), "w_branch_ret": (D, D), "w_branch_attn": (D, D), "w_out": (D, D),
           "ffn2_w1": (D, FF), "ffn2_w3": (D, FF), "ffn2_w2": (FF, D)}
MUL, ADD, SUB = ALU.mult, ALU.add, ALU.subtract


class K:
    def __init__(self, nc, S, L, dbg=(), phases=None):
        self.nc = nc
        self.S = S
        self.L = L
        self.NT = S // 512
        self.fw = FW(nc)
        self.dbg = set(dbg)
        self.phases = phases
        fw = self.fw
        di = lambda n, s, dt=F32: nc.dram_tensor(n, list(s), dt, kind="ExternalInput").ap()
        self.x_in = di("x", (S, D))
        self.y_out = nc.dram_tensor("y", [S, D], F32, kind="ExternalOutput").ap()
        self.wsrc = {n: di(n, (L,) + WSHAPES[n]) for n in WNAMES}
        self.gains_d = di("gains_fm", (L, 128, 3, 8))
        self.conv_d = di("conv_fm", (L, 128, 16, 5))
        self.gdnnorm_d = di("gdnnorm_rep", (L, 128, 256))
        self.retnorm_d = di("retnorm_rep", (L, 128, 1024))
        self.qkcol_d = di("qk_gain_col", (L, 128, 2))
        self.qkrep_d = di("qk_gain_rep", (L, 128, 2, 128))
        self.alog_d = di("alog_rep", (L, 128, 4, 8))
        self.dtb_d = di("dtb_rep", (L, 128, 4, 8))
        self.rlogit_d = di("rlogit_rep", (L, 128, 8))
        self.cf_d = di("c_f32", (128, 9, 128))
        self.cb_d = di("c_bf16", (128, 2, 128), BF16)
        self.rope_d = di("rope_tab", (6, 128, S))
        self.scr = {}
        sc = self.scratch
        self.wbf = {n: sc("bf_" + n, (L,) + WSHAPES[n], BF16) for n in WNAMES}
        sc("xT", (DC, 128, S)); sc("x1T", (DC, 128, S))
        sc("qkv_pre", (16, 128, S + 4))
        sc("zs", (S, 1024), BF16); sc("gb", (S, 16))
        sc("rqT", (4, 128, S), BF16); sc("rkT", (4, 128, S), BF16)
        sc("rv", (S, 1024), BF16); sc("rgs", (S, 1024), BF16)
        sc("aqT", (8, 128, S), BF16); sc("akT", (2, 128, S), BF16); sc("av", (S, 256), BF16)
        sc("gatesT", (24, 128, S), BF16)
        sc("ofwd", (S, 1024)); sc("ofwd_r", (S, 1024))
        sc("gqT", (4, 128, S), BF16); sc("gkT", (4, 128, S), BF16); sc("gvT", (8, 128, S), BF16)
        sc("goT", (8, 128, S), BF16); sc("roT", (8, 128, S), BF16); sc("aoT", (8, 128, S), BF16)
        self.PP = [nc.alloc_psum_tensor("PP%d" % i, [128, 2, 512], F32) for i in range(4)]
        self.P = []
        for i in range(8):
            b = Buf("P%d" % i)
            fw.bufs.append(b)
            self.P.append(T(self.PP[i // 2][:, i % 2, :], b))
        self.build()

    def scratch(self, name, shape, dt=F32):
        kind = "ExternalOutput" if name in self.dbg else "Internal"
        t = self.nc.dram_tensor(name, list(shape), dt, kind=kind).ap()
        self.scr[name] = t
        return t

    def dump(self, name, tile, ap=None, dt=F32):
        if ("dbg_" + name) not in self.dbg or ("dbg_" + name) in self.scr:
            return
        ap = tile[:] if ap is None else ap
        d = self.nc.dram_tensor("dbg_" + name, [int(x) for x in ap.shape], dt, kind="ExternalOutput").ap()
        self.scr["dbg_" + name] = d
        self.fw.dma("pool", d, ap, reads=[tile])

    def mm(self, out, lhsT, rhs, reads, writes, start=True, stop=True):
        self.fw.op("pe", lambda e: e.matmul(out, lhsT, rhs, start=start, stop=stop), reads, writes)

    def act(self, out, in_, func, reads, writes, **kw):
        self.fw.op("act", lambda e: e.activation(out, in_, func, **kw), reads, writes)

    def amul(self, out, in_, m, reads, writes):
        self.fw.op("act", lambda e: e.mul(out, in_, m), reads, writes)

    def acopy(self, out, in_, reads, writes):
        self.fw.op("act", lambda e: e.copy(out, in_), reads, writes)

    def vcopy(self, out, in_, reads, writes):
        self.fw.op("dve", lambda e: e.tensor_copy(out, in_), reads, writes)

    def tt(self, out, a, b, op, reads, writes):
        self.fw.op("dve", lambda e: e.tensor_tensor(out, a, b, op=op), reads, writes)

    def ptt(self, out, a, b, op, reads, writes):
        self.fw.op("pool", lambda e: e.tensor_tensor(out, a, b, op=op), reads, writes)

    def ts(self, out, a, s1, op0, reads, writes):
        self.fw.op("dve", lambda e: e.tensor_scalar(out, a, s1, None, op0=op0), reads, writes)

    def stt(self, out, a, s, b, op0, op1, reads, writes):
        self.fw.op("dve", lambda e: e.scalar_tensor_tensor(out, a, s, b, op0=op0, op1=op1), reads, writes)

    def recip(self, out, in_, reads, writes):
        self.fw.op("dve", lambda e: e.reciprocal(out, in_), reads, writes)

    def memset(self, out, val, writes):
        self.fw.op("dve", lambda e: e.memset(out, val), (), writes)

    def build(self):
        fw = self.fw
        self.cf = fw.sb("cf", [128, 9, 128])
        self.cb = fw.sb("cb", [128, 2, 128], BF16)
        fw.dma("sp", self.cf[:], self.cf_d, writes=[self.cf])
        fw.dma("sp", self.cb[:], self.cb_d, writes=[self.cb])
        self.keep = fw.ptr
        self.identf = self.cf[:, 0, :]
        self.onesf = self.cf[:, 1, :]
        self.identb = self.cb[:, 0, :]
        self.onesb = self.cb[:, 1, :]
        ph = self.phases
        on = lambda p: ph is None or p in ph
        if on("cast"):
            self.cast_weights()
            fw.barrier()
        for l in range(self.L):
            if on("A"):
                self.phase_A(l)
                fw.barrier()
            if ph is None:
                fw.reset_arena(self.keep)
                g1 = self.sweep(l, "gdn", 0, reset=False)
                g2 = self.sweep(l, "ret", 0, reset=False)
                live = [g1, g2]
                while live:
                    for g in list(live):
                        try:
                            next(g)
                        except StopIteration:
                            live.remove(g)
                fw.barrier()
                for kind in ("gdn", "ret"):
                    for _ in self.sweep(l, kind, 1):
                        pass
                    fw.barrier()
            else:
                for kind in ("gdn", "ret"):
                    for dr in (0, 1):
                        if on(kind) or on(kind + str(dr)):
                            for _ in self.sweep(l, kind, dr):
                                pass
                            fw.barrier()
            if on("att"):
                self.attention(l)
                fw.barrier()
            if on("E"):
                self.phase_E(l)
                fw.barrier()
        fw.emit()

    def cast_weights(self):
        fw = self.fw
        for l in range(self.L):
            for n in WNAMES:
                R, C = WSHAPES[n]
                for r0 in range(0, R, 128):
                    fw.dma("pool", self.wbf[n][l, r0:r0 + 128, :], self.wsrc[n][l, r0:r0 + 128, :])
        z = self.cf[:, 8, 0:2]
        for c in range(16):
            fw.dma("sp", self.scr["qkv_pre"][c, :, 0:2], z, reads=[self.cf])
            fw.dma("sp", self.scr["qkv_pre"][c, :, self.S + 2:self.S + 4], z, reads=[self.cf])

    def rmsnorm(self, xT, gains, gi, xn, sqrot, rsd, ps):
        for c in range(DC):
            sq = sqrot.next()
            self.act(sq[:], xT[:, c, :], AF.Square, [xT], [sq])
            self.mm(ps[:], self.onesf, sq[:], [sq, self.cf], [ps], start=(c == 0), stop=(c == DC - 1))
        self.act(rsd[:], ps[:], AF.Ln, [ps], [rsd], bias=EPS, scale=1.0 / D)
        self.act(rsd[:], rsd[:], AF.Exp, [rsd], [rsd], scale=-0.5)
        for c in range(DC):
            self.stt(xn[:, c, :], xT[:, c, :], gains[:, gi, c:c + 1], rsd[:], MUL, MUL, [xT, rsd, gains], [xn])

    def ffn(self, l, pre, xT, xn, gT, w13rot, w2rot, srot, prot):
        fw = self.fw
        W1, W3, W2 = self.wbf[pre + "_w1"], self.wbf[pre + "_w3"], self.wbf[pre + "_w2"]
        for g in range(FC // 2):
            w = w13rot.next()
            f0 = g * 256
            fw.dma("sp", w[:, 0], W1[l, :, f0:f0 + 256].rearrange("(k p) f -> p k f", p=128), writes=[w])
            fw.dma("sp", w[:, 1], W3[l, :, f0:f0 + 256].rearrange("(k p) f -> p k f", p=128), writes=[w])
            for j in range(2):
                fc = g * 2 + j
                p1 = prot.next()
                p3 = prot.next()
                for k in range(DC):
                    for which, pp in ((0, p1), (1, p3)):
                        self.mm(pp[:], w[:, which, k, j * 128:(j + 1) * 128], xn[:, k, :], [w, xn], [pp],
                                start=(k == 0), stop=(k == DC - 1))
                s = srot.next()
                self.act(s[:], p1[:], AF.Silu, [p1], [s])
                self.tt(gT[:, fc, :], s[:], p3[:], MUL, [s, p3], [gT])
        for g in range(4):
            w = w2rot.next()
            d0 = g * 256
            fw.dma("sp", w[:], W2[l, :, d0:d0 + 256].rearrange("(k p) f -> p k f", p=128), writes=[w])
            pps = [prot.next(), prot.next()]
            for k in range(FC):
                for j in range(2):
                    self.mm(pps[j][:], w[:, k, j * 128:(j + 1) * 128], gT[:, k, :], [w, gT], [pps[j]], start=(k == 0), stop=(k == FC - 1))
            for j in range(2):
                dc = g * 2 + j
                self.stt(xT[:, dc, :], pps[j][:], 0.5, xT[:, dc, :], MUL, ADD, [pps[j], xT], [xT])

    def phase_A(self, l):
        fw, S, scr = self.fw, self.S, self.scr
        fw.reset_arena(self.keep)
        gains = fw.sb("gains", [128, 3, 8])
        fw.dma("sp", gains[:], self.gains_d[l], writes=[gains])
        qkcol = fw.sb("qkcol", [128, 2])
        fw.dma("sp", qkcol[:], self.qkcol_d[l], writes=[qkcol])
        alog = fw.sb("alog", [128, 4, 8]); dtb = fw.sb("dtb", [128, 4, 8])
        fw.dma("sp", alog[:], self.alog_d[l], writes=[alog])
        fw.dma("sp", dtb[:], self.dtb_d[l], writes=[dtb])
        self.act(alog[:], alog[:], AF.Exp, [alog], [alog])
        self.ts(alog[:], alog[:], -1.0, MUL, [alog], [alog])
        xT = fw.sb("xT", [128, DC, 512])
        xn = fw.sb("xn", [128, DC, 512], BF16)
        gT = fw.sb("gT", [128, FC, 512], BF16)
        w13rot = Rot([fw.sb("w13_%d" % i, [128, 2, DC, 256], BF16) for i in range(2)])
        w2rot = Rot([fw.sb("w2_%d" % i, [128, FC, 256], BF16) for i in range(2)])
        wgrot = Rot([fw.sb("wg_%d" % i, [128, DC, 512], BF16) for i in range(2)])
        wsm = fw.sb("wsm", [128, DC, 16], BF16)
        srot = Rot([fw.sb("s_%d" % i, [128, 512]) for i in range(3)])
        sqrot = Rot([fw.sb("sq_%d" % i, [128, 512]) for i in range(3)])
        rsd = fw.sb("rsd", [128, 512])
        rsd2 = Rot([fw.sb("rsd2_%d" % i, [128, 512]) for i in range(2)])
        fst = Rot([fw.sb("fst_%d" % i, [128, 512]) for i in range(4)])
        bst = Rot([fw.sb("bst_%d" % i, [128, 512], BF16) for i in range(4)])
        tm = Rot([fw.sb("tm_%d" % i, [128, 4, 1024], BF16) for i in range(2)])
        tmv = fw.sb("tmv", [128, 4, 256], BF16)
        gbs = fw.sb("gbs", [128, 4, 16])
        gtmp = fw.sb("gtmp", [128, 4, 8])
        rope = fw.sb("rope", [128, 6, 512])
        xs = Rot([fw.sb("xs_%d" % i, [128, 512]) for i in range(3)])
        t1r = Rot([fw.sb("t1_%d" % i, [128, 512]) for i in range(2)])
        t2r = Rot([fw.sb("t2_%d" % i, [128, 512]) for i in range(2)])
        xin = fw.sb("xin", [128, 4, 256]) if l == 0 else None
        prot = Rot(self.P)
        Wi = self.wbf["w_in"]
        fw.dma("sp", wsm[:], Wi[l, :, B0:B0 + 16].rearrange("(k p) f -> p k f", p=128), writes=[wsm])

        def wload(c0, n):
            w = wgrot.next()
            fw.dma("sp", w[:, :, 0:n], Wi[l, :, c0:c0 + n].rearrange("(k p) f -> p k f", p=128), writes=[w])
            return w

        def fm_chunk(w, j, pp):
            for k in range(DC):
                self.mm(pp[:], w[:, k, j * 128:(j + 1) * 128], xn[:, k, :], [w, xn], [pp], start=(k == 0), stop=(k == DC - 1))

        def tm_block(w, n, blk, pp):
            for k in range(DC):
                self.mm(pp[:, 0:n], xn[:, k, blk * 128:(blk + 1) * 128], w[:, k, 0:n], [w, xn], [pp], start=(k == 0), stop=(k == DC - 1))

        def do_rope(src, ci, si, pairs, out):
            t1 = t1r.next(); t2 = t2r.next()
            self.tt(t1[:], src[:], rope[:, ci, :], MUL, [src, rope], [t1])
            for (dl, sl, n) in pairs:
                self.tt(t2[dl:dl + n, :], src[sl:sl + n, :], rope[sl:sl + n, si, :], MUL, [src, rope], [t2])
            self.ptt(out[:], t1[:], t2[:], ADD, [t1, t2], [out])

        PR = [(0, 64, 64), (64, 0, 64)]
        PA = [(0, 32, 32), (32, 0, 32), (64, 96, 32), (96, 64, 32)]

        for tt in range(self.NT):
            t0 = tt * 512
            if l == 0:
                for dq in range(4):
                    fw.dma("sp", xin[:], self.x_in[t0:t0 + 512, dq * 256:(dq + 1) * 256].rearrange("(n p) d -> p n d", p=128),
                           writes=[xin])
                    for j in range(2):
                        pp = prot.next()
                        for blk in range(4):
                            self.mm(pp[:, blk * 128:(blk + 1) * 128], xin[:, blk, j * 128:(j + 1) * 128], self.identf, [xin, self.cf], [pp])
                        c = dq * 2 + j
                        if j:
                            self.acopy(xT[:, c, :], pp[:], [pp], [xT])
                        else:
                            self.vcopy(xT[:, c, :], pp[:], [pp], [xT])
            else:
                fw.dma("sp", xT[:], scr["xT"][:, :, t0:t0 + 512].rearrange("c p t -> p c t"), writes=[xT])
            fw.dma("sp", rope[:], self.rope_d[:, :, t0:t0 + 512].rearrange("r p t -> p r t"), writes=[rope])
            self.rmsnorm(xT, gains, 0, xn, sqrot, rsd, prot.next())
            self.ffn(l, "ffn1", xT, xn, gT, w13rot, w2rot, srot, prot)
            fw.dma("pool", scr["x1T"][:, :, t0:t0 + 512].rearrange("c p t -> p c t"), xT[:], reads=[xT])
            self.rmsnorm(xT, gains, 1, xn, sqrot, rsd, prot.next())
            for g in range(4):
                w = wload(QKV0 + g * 512, 512)
                for j in range(4):
                    pp = prot.next(); fm_chunk(w, j, pp)
                    st = fst.next()
                    if j % 2:
                        self.acopy(st[:], pp[:], [pp], [st])
                    else:
                        self.vcopy(st[:], pp[:], [pp], [st])
                    fw.dma("pool", scr["qkv_pre"][g * 4 + j, :, 2 + t0:2 + t0 + 512], st[:], reads=[st])
            for (c0, dst, silu) in ((Z0, "zs", True), (RV0, "rv", False), (RG0, "rgs", True)):
                tmt = tm.next()
                for g in range(2):
                    w = wload(c0 + g * 512, 512)
                    for blk in range(4):
                        pp = prot.next(); tm_block(w, 512, blk, pp)
                        if silu:
                            self.act(tmt[:, blk, g * 512:(g + 1) * 512], pp[:], AF.Silu, [pp], [tmt])
                        else:
                            self.vcopy(tmt[:, blk, g * 512:(g + 1) * 512], pp[:], [pp], [tmt])
                fw.dma("pool", scr[dst][t0:t0 + 512, :].rearrange("(n p) c -> p n c", p=128), tmt[:], reads=[tmt])
            w = wload(AV0, 256)
            for blk in range(4):
                pp = prot.next(); tm_block(w, 256, blk, pp)
                self.vcopy(tmv[:, blk, :], pp[:, 0:256], [pp], [tmv])
            fw.dma("pool", scr["av"][t0:t0 + 512, :].rearrange("(n p) c -> p n c", p=128), tmv[:], reads=[tmv])
            pp = prot.next()
            for blk in range(4):
                for k in range(DC):
                    self.mm(pp[:, blk * 16:(blk + 1) * 16], xn[:, k, blk * 128:(blk + 1) * 128], wsm[:, k, :], [wsm, xn], [pp],
                            start=(k == 0), stop=(k == DC - 1))
            pv = pp[:, 0:64].rearrange("p (n c) -> p n c", c=16)
            self.act(gbs[:, :, 0:8], pv[:, :, 0:8], AF.Sigmoid, [pp], [gbs])
            self.acopy(gtmp[:], pv[:, :, 8:16], [pp], [gtmp])
            self.tt(gtmp[:], gtmp[:], dtb[:], ADD, [gtmp, dtb], [gtmp])
            self.act(gtmp[:], gtmp[:], AF.Exp, [gtmp], [gtmp])
            self.act(gtmp[:], gtmp[:], AF.Ln, [gtmp], [gtmp], bias=1.0)
            self.tt(gbs[:, :, 8:16], gtmp[:], alog[:], MUL, [gtmp, alog], [gbs])
            fw.dma("pool", scr["gb"][t0:t0 + 512, :].rearrange("(n p) c -> p n c", p=128), gbs[:], reads=[gbs])
            for (c0, dst, ci) in ((RQ0, "rqT", 0), (RK0, "rkT", 2)):
                w = wload(c0, 512)
                for j in range(4):
                    pp = prot.next(); fm_chunk(w, j, pp)
                    x_sb = xs.next()
                    self.acopy(x_sb[:], pp[:], [pp], [x_sb])
                    o = bst.next()
                    do_rope(x_sb, ci, ci + 1, PR, o)
                    fw.dma("pool", scr[dst][j, :, t0:t0 + 512], o[:], reads=[o])
            jobs = []
            for g in range(2):
                jobs += [("aqT", AQ0 + g * 512, g * 4 + j, j, 0) for j in range(4)]
            jobs += [("akT", AK0, j, j, 1) for j in range(2)]
            wcur = {}
            pend = None

            def stage2(job, x_sb, sq):
                dst, c0, oc, j, gi = job
                p2 = prot.next()
                self.mm(p2[:], self.onesf, sq[:], [sq, self.cf], [p2])
                r = rsd2.next()
                self.act(r[:], p2[:], AF.Ln, [p2], [r], bias=EPS, scale=1.0 / 128)
                self.act(r[:], r[:], AF.Exp, [r], [r], scale=-0.5)
                self.stt(x_sb[:], x_sb[:], qkcol[:, gi:gi + 1], r[:], MUL, MUL, [x_sb, r, qkcol], [x_sb])
                o = bst.next()
                do_rope(x_sb, 4, 5, PA, o)
                fw.dma("pool", scr[dst][oc, :, t0:t0 + 512], o[:], reads=[o])

            for job in jobs:
                dst, c0, oc, j, gi = job
                if c0 not in wcur:
                    wcur[c0] = wload(c0, 512 if gi == 0 else 256)
                w = wcur[c0]
                pp = prot.next(); fm_chunk(w, j, pp)
                sq = sqrot.next(); x_sb = xs.next()
                self.act(sq[:], pp[:], AF.Square, [pp], [sq])
                self.acopy(x_sb[:], pp[:], [pp], [x_sb])
                if pend is not None:
                    stage2(*pend)
                pend = (job, x_sb, sq)
            stage2(*pend)
            for g in range(6):
                w = wload(GT0 + g * 512, 512)
                for j in range(4):
                    pp = prot.next(); fm_chunk(w, j, pp)
                    o = bst.next()
                    self.act(o[:], pp[:], AF.Sigmoid, [pp], [o])
                    fw.dma("pool", scr["gatesT"][g * 4 + j, :, t0:t0 + 512], o[:], reads=[o])

    def sweep(self, l, kind, dr, reset=True):
        fw, S, scr = self.fw, self.S, self.scr
        if reset:
            fw.reset_arena(self.keep)
        gdn = kind == "gdn"
        OFW = "ofwd" if gdn else "ofwd_r"
        cf, cb = self.cf, self.cb
        tri, pos, neg = cf[:, 2 + dr, :], cf[:, 4 + dr, :], cf[:, 6 + dr, :]
        identb, identf, onesf = self.identb, self.identf, self.onesf
        prot = Rot(self.P)
        H = range(4)
        GB = 2
        St = [fw.sb("S%d" % h, [128, 256]) for h in H]
        Sb = [fw.sb("Sb%d" % h, [128, 256], BF16) for h in H]
        for h in H:
            self.memset(St[h][:], 0.0, [St[h]])
            self.memset(Sb[h][:], 0.0, [Sb[h]])
        R2 = lambda n, shp, dt=F32, k=2: Rot([fw.sb("%s_%d" % (n, i), shp, dt) for i in range(k)])
        BH = lambda n, shp, dt=F32: [[fw.sb("%s_%d_%d" % (n, b, h), shp, dt) for h in H] for b in range(GB)]
        NB = GB + 1
        gcs, negc, egc, kds, egt, dtmp = (R2(n, [128, 4], F32, NB) for n in ("gcs", "negc", "egc", "kds", "egt", "dtmp"))
        Gbr = R2("Gb", [128, 128], F32, 4)
        gsbr = R2("gsb", [128, 8], F32, NB)
        ETb, ATb, kdb = BH("ET", [128, 128]), BH("AT", [128, 128], BF16), BH("kd", [128, 128], BF16)
        tmpBr = R2("tmpB", [128, 256], F32, 4)
        o_r = R2("o_sb", [128, 1024], F32, 2)
        if gdn:
            negb, bge = R2("negb", [128, 4], F32, NB), R2("bge", [128, 4], F32, NB)
            EAb = BH("EA", [128, 128])
            kbgb, vbb = BH("kbg", [128, 128], BF16), BH("vb", [128, 256], BF16)
            Qb = [BH("Qa", [128, 128]), BH("Qb", [128, 128])]
            QTb = [BH("QTa", [128, 128]), BH("QTb", [128, 128])]
            PTb_ = [BH("PTa", [128, 128]), BH("PTb", [128, 128])]
            PTh = BH("PTh", [128, 128], BF16)
            usb, wTb = BH("us", [128, 256]), BH("wT", [128, 128], BF16)
            vnr = [R2("vn%d" % h, [128, 256], BF16) for h in H]
            if dr == 0:
                xq = fw.sb("xq", [128, 16, 516])
                convw = fw.sb("convw", [128, 16, 5])
                fw.dma("sp", convw[:], self.conv_d[l], writes=[convw])
                accr = R2("acc", [128, 512], F32, 4)
                sqr = R2("sq", [128, 512], F32, 4)
                rsr = R2("rs", [128, 512], F32, 4)
            vT = fw.sb("vT", [128, 8, 512], BF16)
            gbt = fw.sb("gbt", [128, 4, 16])
        else:
            rvt = fw.sb("rvt", [128, 4, 1024], BF16)
            gl = fw.sb("gl", [128, 8])
            fw.dma("sp", gl[:], self.rlogit_d[l], writes=[gl])
            self.act(gl[:], gl[:], AF.Exp, [gl], [gl], scale=-1.0)
            self.act(gl[:], gl[:], AF.Ln, [gl], [gl], bias=1.0)
            self.ts(gl[:], gl[:], -1.0, MUL, [gl], [gl])
        qT = fw.sb("qT", [128, 4, 512], BF16)
        kT = fw.sb("kT", [128, 4, 512], BF16)
        if dr == 1:
            ofw = R2("ofw", [128, 1024], F32, 2)
            osum = fw.sb("osum", [128, 1024])
            gate = R2("gate", [128, 1024], BF16, 2)
            yb = R2("yb", [128, 1024], BF16, 2)
            yT = R2("yT", [128, 8, 512], BF16, 2)
            nrm = fw.sb("nrm", [128, 256 if gdn else 1024])
            fw.dma("sp", nrm[:], (self.gdnnorm_d if gdn else self.retnorm_d)[l], writes=[nrm])
            junk = R2("junk", [128, 256], F32, 2)
            st4 = R2("st4", [128, 4], F32, 4)
            cen = [fw.sb("cen%d" % h, [128, 256]) for h in H] if not gdn else None

        def decay_prep(g, gbuf, bi):
            pg = prot.next()
            self.mm(pg[:, 0:4], tri, g, [gbuf, cf], [pg])
            self.mm(pg[:, 4:8], onesf, g, [gbuf, cf], [pg])
            c_gcs, c_negc, c_egc, c_kds, c_egt, c_d = (r.next() for r in (gcs, negc, egc, kds, egt, dtmp))
            gsb = gsbr.next()
            self.acopy(gsb[:], pg[:, 0:8], [pg], [gsb])
            self.vcopy(c_gcs[:], gsb[:, 0:4], [gsb], [c_gcs])
            self.act(c_egc[:], gsb[:, 0:4], AF.Exp, [gsb], [c_egc])
            self.tt(c_d[:], gsb[:, 4:8], gsb[:, 0:4], SUB, [gsb], [c_d])
            self.act(c_kds[:], c_d[:], AF.Exp, [c_d], [c_kds])
            self.act(c_egt[:], gsb[:, 4:8], AF.Exp, [gsb], [c_egt])
            self.ts(c_negc[:], gsb[:, 0:4], -1.0, MUL, [gsb], [c_negc])
            Gbs = []
            for h in H:
                Gb = Gbr.next()
                self.ts(Gb[:], onesf, g[:, h:h + 1], MUL, [gbuf, cf], [Gb])
                Gbs.append(Gb)
            pcs, pas = [], []
            for h in H:
                pc = prot.next()
                self.mm(pc[:, 0:128], Gbs[h][:], tri, [Gbs[h], cf], [pc], start=True, stop=False)
                self.mm(pc[:, 0:128], identf, neg, [cf], [pc], start=False, stop=True)
                if gdn:
                    self.mm(pc[:, 128:256], Gbs[h][:], tri, [Gbs[h], cf], [pc], start=True, stop=False)
                    self.mm(pc[:, 128:256], identf, pos, [cf], [pc], start=False, stop=True)
                pcs.append(pc)
            for h in H:
                self.act(ETb[bi][h][:], pcs[h][:, 0:128], AF.Exp, [pcs[h], c_negc], [ETb[bi][h]], bias=c_negc[:, h:h + 1])
                if gdn:
                    self.act(EAb[bi][h][:], pcs[h][:, 128:256], AF.Exp, [pcs[h], c_gcs], [EAb[bi][h]], bias=c_gcs[:, h:h + 1], scale=-1.0)
            return dict(gcs=c_gcs, egc=c_egc, kds=c_kds, egt=c_egt, ET=ETb[bi], EA=EAb[bi] if gdn else None)

        if not gdn:
            dec_c = decay_prep(gl[:, dr * 4:dr * 4 + 4], gl, 0)

        def norm2(c, sl, sq):
            p2 = prot.next()
            self.mm(p2[:], onesf, sq[:], [sq, cf], [p2])
            r = rsr.next()
            self.act(r[:], p2[:], AF.Ln, [p2], [r], bias=EPS)
            self.act(r[:], r[:], AF.Exp, [r], [r], scale=-0.5)
            dst = qT if c < 4 else kT
            sc_ = (128.0 ** -0.5) if c < 4 else 1.0
            self.stt(dst[:, c % 4, :], sl[:], sc_, r[:], MUL, MUL, [sl, r], [dst])

        import os
        STOP = int(os.environ.get("SW_STOP", "99"))
        tiles = list(range(self.NT))
        blks = list(range(4))
        if dr == 1:
            tiles.reverse(); blks.reverse()
        for tt in tiles:
            t0 = tt * 512
            if gdn and dr == 1:
                fw.dma("sp", qT[:], scr["gqT"][:, :, t0:t0 + 512].rearrange("c p t -> p c t"), writes=[qT])
                fw.dma("sp", kT[:], scr["gkT"][:, :, t0:t0 + 512].rearrange("c p t -> p c t"), writes=[kT])
                fw.dma("sp", vT[:], scr["gvT"][:, :, t0:t0 + 512].rearrange("c p t -> p c t"), writes=[vT])
                fw.dma("sp", gbt[:], scr["gb"][t0:t0 + 512, :].rearrange("(n p) c -> p n c", p=128), writes=[gbt])
            elif gdn:
                for c4 in range(0, 16, 2):
                    fw.dma("sp", xq[:, c4:c4 + 2, :], scr["qkv_pre"][c4:c4 + 2, :, t0:t0 + 516].rearrange("c p t -> p c t"), writes=[xq])
                fw.dma("sp", gbt[:], scr["gb"][t0:t0 + 512, :].rearrange("(n p) c -> p n c", p=128), writes=[gbt])
                for cg in range(0, 16, 4):
                    accs = [accr.next() for _ in range(4)]
                    for i in range(4):
                        c = cg + i
                        self.amul(accs[i][:], xq[:, c, 0:512], convw[:, c, 0:1], [xq, convw], [accs[i]])
                    for w in range(1, 5):
                        for i in range(4):
                            c = cg + i
                            self.stt(accs[i][:], xq[:, c, w:w + 512], convw[:, c, w:w + 1], accs[i][:], MUL, ADD, [xq, convw, accs[i]], [accs[i]])
                    if cg >= 8:
                        for i in range(4):
                            self.act(vT[:, cg + i - 8, :], accs[i][:], AF.Silu, [accs[i]], [vT])
                    else:
                        sqs = []
                        for i in range(4):
                            self.act(accs[i][:], accs[i][:], AF.Silu, [accs[i]], [accs[i]])
                        for i in range(4):
                            sq = sqr.next()
                            self.act(sq[:], accs[i][:], AF.Square, [accs[i]], [sq])
                            sqs.append(sq)
                        for i in range(4):
                            norm2(cg + i, accs[i], sqs[i])
                fw.dma("pool", scr["gqT"][:, :, t0:t0 + 512].rearrange("c p t -> p c t"), qT[:], reads=[qT])
                fw.dma("pool", scr["gkT"][:, :, t0:t0 + 512].rearrange("c p t -> p c t"), kT[:], reads=[kT])
                fw.dma("pool", scr["gvT"][:, :, t0:t0 + 512].rearrange("c p t -> p c t"), vT[:], reads=[vT])
            else:
                fw.dma("sp", qT[:], scr["rqT"][:, :, t0:t0 + 512].rearrange("c p t -> p c t"), writes=[qT])
                fw.dma("sp", kT[:], scr["rkT"][:, :, t0:t0 + 512].rearrange("c p t -> p c t"), writes=[kT])
                fw.dma("sp", rvt[:], scr["rv"][t0:t0 + 512, :].rearrange("(n p) c -> p n c", p=128), writes=[rvt])
            if dr == 1:
                yTt = yT.next()
            for g0 in range(0, 4, GB):
                grp = blks[g0:g0 + GB]
                BHs = [(bi, blk, h) for bi, blk in enumerate(grp) for h in H]
                bsl = {blk: slice(blk * 128, (blk + 1) * 128) for blk in grp}
                dec, beta, c_negb, c_bge = {}, {}, {}, {}
                for bi, blk in enumerate(grp):
                    if gdn:
                        dec[blk] = decay_prep(gbt[:, blk, 8 + dr * 4:12 + dr * 4], gbt, bi)
                        beta[blk] = gbt[:, blk, dr * 4:dr * 4 + 4]
                        c_negb[blk], c_bge[blk] = negb.next(), bge.next()
                        self.ts(c_negb[blk][:], beta[blk], -1.0, MUL, [gbt], [c_negb[blk]])
                        self.tt(c_bge[blk][:], beta[blk], dec[blk]["egc"][:], MUL, [gbt, dec[blk]["egc"]], [c_bge[blk]])
                    else:
                        dec[blk] = dec_c
                AT, kd, kbg, vb, Q, QT, PT = {}, {}, {}, {}, {}, {}, {}
                for sub in range(0, len(BHs), 4):
                    SUBB = BHs[sub:sub + 4]
                    ps1 = {}
                    for (bi, blk, h) in SUBB:
                        bs = bsl[blk]
                        qb, kb = qT[:, h, bs], kT[:, h, bs]
                        pa = prot.next()
                        self.mm(pa[:, 0:128], kb, qb, [qT, kT], [pa])
                        if gdn:
                            self.mm(pa[:, 128:256], kb, kb, [kT], [pa])
                        pb = prot.next()
                        self.mm(pb[:, 0:128], kb, identb, [kT, cb], [pb])
                        if gdn:
                            for j in range(2):
                                self.mm(pb[:, 128 + j * 128:256 + j * 128], vT[:, 2 * h + j, bs], identb, [vT, cb], [pb])
                        ps1[(blk, h)] = (pa, pb)
                    for (bi, blk, h) in SUBB:
                        pa, pb = ps1[(blk, h)]
                        d = dec[blk]
                        AT[(blk, h)] = ATb[bi][h]
                        self.tt(ATb[bi][h][:], pa[:, 0:128], d["ET"][h][:], MUL, [pa, d["ET"][h]], [ATb[bi][h]])
                        kd[(blk, h)] = kdb[bi][h]
                        self.amul(kdb[bi][h][:], pb[:, 0:128], d["kds"][:, h:h + 1], [pb, d["kds"]], [kdb[bi][h]])
                        if gdn:
                            kbg[(blk, h)] = kbgb[bi][h]
                            self.amul(kbgb[bi][h][:], pb[:, 0:128], c_bge[blk][:, h:h + 1], [pb, c_bge[blk]], [kbgb[bi][h]])
                            vb[(blk, h)] = vbb[bi][h]
                            self.amul(vbb[bi][h][:], pb[:, 128:384], beta[blk][:, h:h + 1], [pb, gbt], [vbb[bi][h]])
                            Q[(blk, h)] = Qb[0][bi][h]
                            self.stt(Qb[0][bi][h][:], pa[:, 128:256], c_negb[blk][:, h:h + 1], d["EA"][h][:], MUL, MUL,
                                     [pa, c_negb[blk], d["EA"][h]], [Qb[0][bi][h]])
                if gdn:
                    psn = {}
                    for (bi, blk, h) in BHs:
                        pn = prot.next()
                        self.mm(pn[:, 0:128], Q[(blk, h)][:], identf, [Q[(blk, h)], cf], [pn])
                        psn[(blk, h)] = pn
                    for (bi, blk, h) in BHs:
                        pn = psn[(blk, h)]
                        QT[(blk, h)] = QTb[0][bi][h]; PT[(blk, h)] = PTb_[0][bi][h]
                        self.acopy(QT[(blk, h)][:], pn[:, 0:128], [pn], [QT[(blk, h)]])
                        self.tt(PT[(blk, h)][:], QT[(blk, h)][:], identf, ADD, [QT[(blk, h)], cf], [PT[(blk, h)]])
                    for lev in range(1, 7):
                        pi = lev % 2
                        psq = {}
                        for (bi, blk, h) in BHs:
                            k_ = (blk, h)
                            pq = prot.next()
                            self.mm(pq[:, 0:128], QT[k_][:], Q[k_][:], [QT[k_], Q[k_]], [pq])
                            if lev < 6:
                                self.mm(pq[:, 128:256], Q[k_][:], QT[k_][:], [QT[k_], Q[k_]], [pq])
                            psq[k_] = pq
                        for n_, (bi, blk, h) in enumerate(BHs):
                            k_ = (blk, h)
                            pq = psq[k_]
                            cp = self.acopy if (n_ % 2 == 0 or lev == 6) else self.vcopy
                            Qn = Qb[pi][bi][h]
                            cp(Qn[:], pq[:, 0:128], [pq], [Qn])
                            if lev < 6:
                                QTn = QTb[pi][bi][h]
                                cp(QTn[:], pq[:, 128:256], [pq], [QTn])
                                QT[k_] = QTn
                            Q[k_] = Qn
                        psp = {}
                        for (bi, blk, h) in BHs:
                            k_ = (blk, h)
                            pp = prot.next()
                            self.mm(pp[:, 0:128], Q[k_][:], PT[k_][:], [Q[k_], PT[k_]], [pp])
                            psp[k_] = pp
                        for (bi, blk, h) in BHs:
                            k_ = (blk, h)
                            PTn = PTb_[pi][bi][h]
                            self.tt(PTn[:], psp[k_][:, 0:128], PT[k_][:], ADD, [psp[k_], PT[k_]], [PTn])
                            PT[k_] = PTn
                    for n_, (bi, blk, h) in enumerate(BHs):
                        k_ = (blk, h)
                        if n_ % 2:
                            self.acopy(PTh[bi][h][:], PT[k_][:], [PT[k_]], [PTh[bi][h]])
                        else:
                            self.vcopy(PTh[bi][h][:], PT[k_][:], [PT[k_]], [PTh[bi][h]])
                    psu = {}
                    for (bi, blk, h) in BHs:
                        k_ = (blk, h)
                        pu = prot.next()
                        self.mm(pu[:, 0:256], PTh[bi][h][:], vb[k_][:], [PTh[bi][h], vb[k_]], [pu])
                        self.mm(pu[:, 256:384], kbg[k_][:], PTh[bi][h][:], [PTh[bi][h], kbg[k_]], [pu])
                        psu[k_] = pu
                    us, wT = {}, {}
                    for (bi, blk, h) in BHs:
                        k_ = (blk, h)
                        us[k_] = usb[bi][h]; wT[k_] = wTb[bi][h]
                        cp = self.acopy if h % 2 else self.vcopy
                        cp(us[k_][:], psu[k_][:, 0:256], [psu[k_]], [us[k_]])
                        cp(wT[k_][:], psu[k_][:, 256:384], [psu[k_]], [wT[k_]])
                for bi, blk in enumerate(grp):
                    bs = bsl[blk]
                    tok0 = t0 + blk * 128
                    d = dec[blk]
                    if dr == 1:
                        c_ofw = ofw.next()
                        fw.dma("sp", c_ofw[:], scr[OFW][tok0:tok0 + 128, :], writes=[c_ofw])
                        c_gate = gate.next()
                        fw.dma("sp", c_gate[:], scr["zs" if gdn else "rgs"][tok0:tok0 + 128, :], writes=[c_gate])
                    o_sb = o_r.next()
                    vnew = {}
                    if gdn:
                        pws = {}
                        for h in H:
                            pws[h] = prot.next()
                            self.mm(pws[h][:, 0:256], wT[(blk, h)][:], Sb[h][:], [wT[(blk, h)], Sb[h]], [pws[h]])
                        for h in H:
                            vn = vnr[h].next()
                            self.tt(vn[:], us[(blk, h)][:], pws[h][:, 0:256], SUB, [us[(blk, h)], pws[h]], [vn])
                            vnew[h] = (vn[:], vn)
                    else:
                        for h in H:
                            vnew[h] = (rvt[:, blk, h * 256:(h + 1) * 256], rvt)
                    poA, poB = {}, {}
                    for h in H:
                        vap, vbuf = vnew[h]
                        poA[h] = prot.next()
                        self.mm(poA[h][:, 0:256], qT[:, h, bs], Sb[h][:], [qT, Sb[h]], [poA[h]])
                        self.mm(poA[h][:, 256:512], kd[(blk, h)][:], vap, [kd[(blk, h)], vbuf], [poA[h]])
                        poB[h] = prot.next()
                        self.mm(poB[h][:, 0:256], AT[(blk, h)][:], vap, [AT[(blk, h)], vbuf], [poB[h]])
                    for h in H:
                        self.stt(St[h][:], St[h][:], d["egt"][:, h:h + 1], poA[h][:, 256:512], MUL, ADD, [St[h], poA[h], d["egt"]], [St[h]])
                        self.acopy(Sb[h][:], St[h][:], [St[h]], [Sb[h]])
                    for h in H:
                        tB = tmpBr.next()
                        self.acopy(tB[:], poB[h][:, 0:256], [poB[h]], [tB])
                        self.stt(o_sb[:, h * 256:(h + 1) * 256], poA[h][:, 0:256], d["egc"][:, h:h + 1], tB[:], MUL, ADD,
                                 [poA[h], tB, d["egc"]], [o_sb])
                    if dr == 0:
                        fw.dma("pool", scr[OFW][tok0:tok0 + 128, :], o_sb[:], reads=[o_sb])
                        continue
                    self.tt(osum[:], o_sb[:], c_ofw[:], ADD, [o_sb, c_ofw], [osum])
                    c_yb = yb.next()
                    s1, s2 = st4.next(), st4.next()
                    srcs = {}
                    if gdn:
                        for h in H:
                            srcs[h] = (osum, osum[:, h * 256:(h + 1) * 256])
                    else:
                        for h in H:
                            jk = junk.next()
                            self.act(jk[:], osum[:, h * 256:(h + 1) * 256], AF.Copy, [osum], [jk, s1], accum_out=s1[:, h:h + 1])
                        self.ts(s1[:], s1[:], -1.0 / 256, MUL, [s1], [s1])
                        for h in H:
                            self.ts(cen[h][:], osum[:, h * 256:(h + 1) * 256], s1[:, h:h + 1], ADD, [osum, s1], [cen[h]])
                            srcs[h] = (cen[h], cen[h][:])
                    for h in H:
                        jk = junk.next()
                        self.act(jk[:], srcs[h][1], AF.Square, [srcs[h][0]], [jk, s2], accum_out=s2[:, h:h + 1])
                    self.act(s2[:], s2[:], AF.Sqrt, [s2], [s2], bias=EPS, scale=1.0 / 256)
                    self.recip(s2[:], s2[:], [s2], [s2])
                    tBs = {}
                    for h in H:
                        hs = slice(h * 256, (h + 1) * 256)
                        nap = nrm[:, 0:256] if gdn else nrm[:, hs]
                        tBs[h] = tmpBr.next()
                        self.stt(tBs[h][:], srcs[h][1], s2[:, h:h + 1], nap, MUL, MUL, [srcs[h][0], s2, nrm], [tBs[h]])
                    for h in H:
                        hs = slice(h * 256, (h + 1) * 256)
                        self.tt(c_yb[:, hs], tBs[h][:], c_gate[:, hs], MUL, [tBs[h], c_gate], [c_yb])
                    pps = []
                    for g in range(2):
                        pp = prot.next()
                        for j in range(4):
                            c = g * 4 + j
                            self.mm(pp[:, j * 128:(j + 1) * 128], c_yb[:, c * 128:(c + 1) * 128], identb, [c_yb, cb], [pp])
                        pps.append(pp)
                    for g in range(2):
                        src3 = pps[g][:, 0:512].rearrange("p (c t) -> p c t", t=128)
                        if g:
                            self.acopy(yTt[:, g * 4:g * 4 + 4, bs], src3, [pps[g]], [yTt])
                        else:
                            self.vcopy(yTt[:, g * 4:g * 4 + 4, bs], src3, [pps[g]], [yTt])
                yield
            if dr == 1:
                fw.dma("pool", scr["goT" if gdn else "roT"][:, :, t0:t0 + 512].rearrange("c p t -> p c t"), yTt[:], reads=[yTt])
        yield

    def attention(self, l):
        fw, S, scr = self.fw, self.S, self.scr
        fw.reset_arena(self.keep)
        NK = S // 128
        qkrep = fw.sb("qkrep", [128, 2, 128])
        fw.dma("sp", qkrep[:], self.qkrep_d[l], writes=[qkrep])
        mx = fw.sb("mx", [128, 2]); nb = fw.sb("nb", [128, 1])
        self.act(qkrep[:], qkrep[:], AF.Abs, [qkrep], [qkrep])
        fw.op("dve", lambda e: e.reduce_max(mx[:], qkrep[:], axis=AX.X), [qkrep], [mx])
        self.stt(nb[:], mx[:, 0:1], -(128.0 ** 0.5), mx[:, 1:2], MUL, MUL, [mx], [nb])
        KT = fw.sb("KT", [128, S], BF16)
        Vt = fw.sb("Vt", [128, NK, 128], BF16)
        qrot = Rot([fw.sb("qt%d" % i, [128, 512], BF16) for i in range(3)])
        prob = Rot([fw.sb("pr%d" % i, [128, 2, 512], BF16) for i in range(4)])
        psr = Rot([fw.sb("prs%d" % i, [128, 512], BF16) for i in range(4)])
        rdn = Rot([fw.sb("rdn%d" % i, [128, 512]) for i in range(2)])
        orot = Rot([fw.sb("ot%d" % i, [128, 512], BF16) for i in range(2)])
        P = self.P
        sgrp = [(self.PP[0], P[0], P[1]), (self.PP[1], P[2], P[3]), (self.PP[2], P[4], P[5])]
        porot = Rot(P[6:7]); pdrot = Rot(P[7:8])
        sc = 128.0 ** -0.5
        assert NK % 2 == 0
        NP = NK // 2
        LA = 2
        for g in range(2):
            fw.dma("sp", KT[:], scr["akT"][g], writes=[KT])
            nstep = min(NK, 8)
            for n0 in range(0, NK, nstep):
                fw.dma("sp", Vt[:, n0:n0 + nstep, :],
                       scr["av"][n0 * 128:(n0 + nstep) * 128, g * 128:(g + 1) * 128].rearrange("(n p) e -> p n e", p=128), writes=[Vt])
            jobs = [(hq, tq, kp) for hq in range(4 * g, 4 * g + 4) for tq in range(self.NT) for kp in range(NP)]
            state = {}

            def tile_state(hq, tq):
                key = (hq, tq)
                if key not in state:
                    qt = qrot.next()
                    fw.dma("sp", qt[:], scr["aqT"][hq, :, tq * 512:tq * 512 + 512], writes=[qt])
                    state[key] = (qt, porot.next(), pdrot.next())
                return state[key]

            pend = []
            denq = []
            for i in range(len(jobs) + LA):
                if i < len(jobs):
                    hq, tq, kp = jobs[i]
                    qt, po, pd = tile_state(hq, tq)
                    PPt, b0, b1 = sgrp[i % 3]
                    for j, bk in ((0, b0), (1, b1)):
                        kt = 2 * kp + j
                        self.mm(bk[:], KT[:, kt * 128:(kt + 1) * 128], qt[:], [KT, qt], [bk])
                    pr = prob.next()
                    self.act(pr[:], PPt[:], AF.Exp, [b0, b1, nb], [pr], bias=nb[:, 0:1], scale=sc)
                    pend.append((hq, tq, kp, pr, po, pd))
                if i >= LA:
                    hq, tq, kp, pr, po, pd = pend.pop(0)
                    prs = psr.next()
                    self.tt(prs[:], pr[:, 0, :], pr[:, 1, :], ADD, [pr], [prs])
                    for j in range(2):
                        kt = 2 * kp + j
                        self.mm(po[:], Vt[:, kt, :], pr[:, j, :], [Vt, pr], [po], start=(kt == 0), stop=(kt == NK - 1))
                    if denq:
                        dprs, dpd, dkp = denq.pop(0)
                        self.mm(dpd[:], self.onesb, dprs[:], [self.cb, dprs], [dpd], start=(dkp == 0), stop=(dkp == NP - 1))
                    denq.append((prs, pd, kp))
                    if kp == NP - 1:
                        dprs, dpd, dkp = denq.pop(0)
                        self.mm(dpd[:], self.onesb, dprs[:], [self.cb, dprs], [dpd], start=(dkp == 0), stop=(dkp == NP - 1))
                    if kp == NP - 1:
                        r = rdn.next()
                        self.recip(r[:], pd[:], [pd], [r])
                        ot = orot.next()
                        self.tt(ot[:], po[:], r[:], MUL, [po, r], [ot])
                        fw.dma("pool", scr["aoT"][hq, :, tq * 512:tq * 512 + 512], ot[:], reads=[ot])
                        del state[(hq, tq)]

    def phase_E(self, l):
        fw, S, scr = self.fw, self.S, self.scr
        fw.reset_arena(self.keep)
        last = l == self.L - 1
        gains = fw.sb("gains", [128, 3, 8])
        fw.dma("sp", gains[:], self.gains_d[l], writes=[gains])
        xT = fw.sb("xT", [128, DC, 512])
        xn = fw.sb("xn", [128, DC, 512], BF16)
        gT = fw.sb("gT", [128, FC, 512], BF16)
        w13rot = Rot([fw.sb("w13_%d" % i, [128, 2, DC, 256], BF16) for i in range(2)])
        w2rot = Rot([fw.sb("w2_%d" % i, [128, FC, 256], BF16) for i in range(2)])
        wbr = Rot([fw.sb("wbr_%d" % i, [128, DC, 256], BF16) for i in range(4)])
        srot = Rot([fw.sb("s_%d" % i, [128, 512]) for i in range(3)])
        sqrot = Rot([fw.sb("sq_%d" % i, [128, 512]) for i in range(3)])
        rsd = fw.sb("rsd", [128, 512])
        bins = [fw.sb("bin_%d" % i, [128, DC, 512], BF16) for i in range(3)]
        gtr = Rot([fw.sb("gt_%d" % i, [128, 3, 2, 512], BF16) for i in range(2)])
        merged = fw.sb("merged", [128, DC, 512], BF16)
        m1r = Rot([fw.sb("m1_%d" % i, [128, 512]) for i in range(2)])
        m2r = Rot([fw.sb("m2_%d" % i, [128, 512]) for i in range(2)])
        ytr = Rot([fw.sb("yt_%d" % i, [128, 1024]) for i in range(2)]) if last else None
        prot = Rot(self.P)
        bnames = ("w_branch_gdn", "w_branch_ret", "w_branch_attn")
        bsrc = ("goT", "roT", "aoT")
        for tt in range(self.NT):
            t0 = tt * 512
            fw.dma("sp", xT[:], scr["x1T"][:, :, t0:t0 + 512].rearrange("c p t -> p c t"), writes=[xT])
            for b in range(3):
                fw.dma("sp", bins[b][:], scr[bsrc[b]][:, :, t0:t0 + 512].rearrange("c p t -> p c t"), writes=[bins[b]])
            for g in range(4):
                ws = []
                for b in range(3):
                    w = wbr.next()
                    fw.dma("sp", w[:], self.wbf[bnames[b]][l, :, g * 256:(g + 1) * 256].rearrange("(k p) f -> p k f", p=128), writes=[w])
                    ws.append(w)
                gt = gtr.next()
                for b in range(3):
                    fw.dma("sp", gt[:, b], scr["gatesT"][b * 8 + 2 * g:b * 8 + 2 * g + 2, :, t0:t0 + 512].rearrange("c p t -> p c t"), writes=[gt])
                for j in range(2):
                    dc = 2 * g + j
                    pb = []
                    for b in range(3):
                        pp = prot.next()
                        for k in range(DC):
                            self.mm(pp[:], ws[b][:, k, j * 128:(j + 1) * 128], bins[b][:, k, :], [ws[b], bins[b]], [pp],
                                    start=(k == 0), stop=(k == DC - 1))
                        pb.append(pp)
                    m1, m2 = m1r.next(), m2r.next()
                    self.tt(m1[:], pb[0][:], gt[:, 0, j, :], MUL, [pb[0], gt], [m1])
                    self.tt(m2[:], pb[1][:], gt[:, 1, j, :], MUL, [pb[1], gt], [m2])
                    self.tt(m1[:], m1[:], m2[:], ADD, [m1, m2], [m1])
                    self.tt(m2[:], pb[2][:], gt[:, 2, j, :], MUL, [pb[2], gt], [m2])
                    self.tt(merged[:, dc, :], m1[:], m2[:], ADD, [m1, m2], [merged])
            for g in range(4):
                w = wbr.next()
                fw.dma("sp", w[:], self.wbf["w_out"][l, :, g * 256:(g + 1) * 256].rearrange("(k p) f -> p k f", p=128), writes=[w])
                for j in range(2):
                    dc = 2 * g + j
                    pp = prot.next()
                    for k in range(DC):
                        self.mm(pp[:], w[:, k, j * 128:(j + 1) * 128], merged[:, k, :], [w, merged], [pp], start=(k == 0), stop=(k == DC - 1))
                    self.tt(xT[:, dc, :], xT[:, dc, :], pp[:], ADD, [xT, pp], [xT])
            self.rmsnorm(xT, gains, 2, xn, sqrot, rsd, prot.next())
            self.ffn(l, "ffn2", xT, xn, gT, w13rot, w2rot, srot, prot)
            if not last:
                fw.dma("pool", scr["xT"][:, :, t0:t0 + 512].rearrange("c p t -> p c t"), xT[:], reads=[xT])
            else:
                for blk in range(4):
                    yt = ytr.next()
                    for g in range(2):
                        pp = prot.next()
                        for j in range(4):
                            c = g * 4 + j
                            self.mm(pp[:, j * 128:(j + 1) * 128], xT[:, c, blk * 128:(blk + 1) * 128], self.identf, [xT, self.cf], [pp])
                        if g:
                            self.acopy(yt[:, g * 512:(g + 1) * 512], pp[:], [pp], [yt])
                        else:
                            self.vcopy(yt[:, g * 512:(g + 1) * 512], pp[:], [pp], [yt])
                    fw.dma("pool", self.y_out[t0 + blk * 128:t0 + (blk + 1) * 128, :], yt[:], reads=[yt])


def _rope_tables(S):
    f32 = np.float32
    theta = f32(10000.0)

    def angles(pos, dim):
        inv = theta ** (-(np.arange(0, dim, 2, dtype=f32)) / f32(dim))
        ang = pos.astype(f32)[:, None] * inv[None, :].astype(f32)
        return np.cos(ang).astype(f32), np.sin(ang).astype(f32)

    tab = np.zeros((6, 128, S), f32)
    c, s = angles(np.arange(S), 128)
    tab[0, :64], tab[0, 64:] = c.T, c.T
    tab[1, :64], tab[1, 64:] = s.T, -s.T
    k = f32(128.0 ** -0.5)
    tab[2], tab[3] = tab[0] * k, tab[1] * k
    rows = np.repeat(np.arange(S // 64), 64)
    cols = np.tile(np.arange(64), S // 64)
    cr, sr = angles(rows, 64)
    cc, sc = angles(cols, 64)
    tab[4, 0:32], tab[4, 32:64], tab[4, 64:96], tab[4, 96:128] = cr.T, cr.T, cc.T, cc.T
    tab[5, 0:32], tab[5, 32:64], tab[5, 64:96], tab[5, 96:128] = sr.T, -sr.T, sc.T, -sc.T
    return tab


def _consts():
    f32 = np.float32
    p = np.arange(128)[:, None]
    f = np.arange(128)[None, :]
    cf = np.zeros((128, 9, 128), f32)
    cf[:, 0] = np.eye(128, dtype=f32)
    cf[:, 1] = 1.0
    cf[:, 2] = (p <= f)
    cf[:, 3] = (p >= f)
    cf[:, 4] = BIG * (p <= f)
    cf[:, 5] = BIG * (p >= f)
    cf[:, 6] = -BIG * (f < p)
    cf[:, 7] = -BIG * (f > p)
    cb = np.zeros((128, 2, 128), f32)
    cb[:, 0] = np.eye(128, dtype=f32)
    cb[:, 1] = 1.0
    return cf, cb.astype(ml_dtypes.bfloat16)


def _small_params(inp, L):
    f32 = np.float32
    A = lambda k: np.asarray(inp[k], dtype=f32)
    fm = lambda v: np.ascontiguousarray(v.reshape(L, 8, 128).transpose(0, 2, 1))
    out = {}
    out["gains_fm"] = np.ascontiguousarray(np.stack([fm(A("ffn1_norm")), fm(A("mix_norm")), fm(A("ffn2_norm"))], axis=2))
    out["conv_fm"] = np.ascontiguousarray(A("gdn_conv").reshape(L, 5, 16, 128).transpose(0, 3, 2, 1))
    rep = lambda v: np.ascontiguousarray(np.broadcast_to(v[:, None], (L, 128) + v.shape[1:]))
    out["gdnnorm_rep"] = rep(A("gdn_norm"))
    out["retnorm_rep"] = rep(A("ret_norm"))
    out["qk_gain_col"] = np.ascontiguousarray(np.stack([A("attn_q_norm"), A("attn_k_norm")], axis=2))
    out["qk_gain_rep"] = rep(np.stack([A("attn_q_norm"), A("attn_k_norm")], axis=1))
    out["alog_rep"] = rep(np.broadcast_to(A("gdn_A_log").reshape(L, 1, 8), (L, 4, 8)))
    out["dtb_rep"] = rep(np.broadcast_to(A("gdn_dt_bias").reshape(L, 1, 8), (L, 4, 8)))
    out["rlogit_rep"] = rep(A("ret_decay_logit").reshape(L, 8))
    return out


_CACHE = {}


def _program(S, L, dbg=(), phases=None):
    key = (S, L, tuple(dbg), None if phases is None else tuple(phases))
    if key not in _CACHE:
        nc = bass.Bass("TRN2", target_bir_lowering=False)
        K(nc, S, L, dbg, phases)
        _CACHE[key] = nc
    return _CACHE[key]


def run_cores(seqs, inp, L, dbg=(), phases=None, n_cores=None):
    S = seqs[0].shape[0]
    nc = _program(S, L, dbg, phases)
    cf, cb = _consts()
    common = {n: np.ascontiguousarray(np.asarray(inp[n], dtype=np.float32)) for n in WNAMES}
    common.update(_small_params(inp, L))
    common["c_f32"] = cf
    common["c_bf16"] = cb
    common["rope_tab"] = _rope_tables(S)
    in_maps = []
    for s in seqs:
        m = dict(common)
        m["x"] = np.ascontiguousarray(np.asarray(s, dtype=np.float32))
        in_maps.append(m)
    res = run_bass_kernel_spmd(nc, in_maps, core_ids=list(range(len(seqs))))
    return res.results


def kernel(**inputs):
    xp = np.asarray(inputs["x_prompt"], dtype=np.float32)
    xs = np.asarray(inputs["x_sample"], dtype=np.float32)
    L = np.asarray(inputs["w_in"]).shape[0]
    seqs = [xp[i] for i in range(xp.shape[0])] + [xs[i] for i in range(xs.shape[0])]
    n_real = len(seqs)
    while len(seqs) < 8:
        seqs.append(seqs[0])
    res = run_cores(seqs, inputs, L)
    ys = [np.asarray(res[i]["y"], dtype=np.float32) for i in range(n_real)]
    y_prompt = np.stack(ys[:xp.shape[0]], axis=0)
    y_sample = np.stack(ys[xp.shape[0]:], axis=0)
    return (y_prompt, y_sample)
```

```python
import contextlib
import numpy as np
import ml_dtypes
import concourse.bass as bass
import concourse.mybir as mybir
from concourse.bass_utils import run_bass_kernel_spmd

F32 = mybir.dt.float32
BF16 = mybir.dt.bfloat16
AF = mybir.ActivationFunctionType
ALU = mybir.AluOpType
AX = mybir.AxisListType

ENGS = ("pe", "act", "dve", "pool", "sp")
NSLOT = 12


class Buf:
    __slots__ = ("name", "w", "r")

    def __init__(self, name):
        self.name = name
        self.w = None
        self.r = []


class Op:
    __slots__ = ("eng", "fn", "dma", "deps", "sig", "sigval", "slot", "dmaval", "epoch")

    def __init__(self, eng, fn, dma):
        self.eng = eng
        self.fn = fn
        self.dma = dma
        self.deps = []
        self.sig = False
        self.sigval = 0
        self.slot = -1
        self.dmaval = 0
        self.epoch = 0


class T:
    __slots__ = ("t", "buf")

    def __init__(self, t, buf):
        self.t = t
        self.buf = buf

    def __getitem__(self, k):
        return self.t[k]


class Rot:
    def __init__(self, tiles):
        self.tiles = tiles
        self.i = 0

    def next(self):
        t = self.tiles[self.i % len(self.tiles)]
        self.i += 1
        return t


class FW:
    def __init__(self, nc):
        self.nc = nc
        self.ops = {e: [] for e in ENGS}
        self.ndma = {e: 0 for e in ENGS}
        self.bufs = []
        self.last = {e: None for e in ENGS}
        self.recent_dma = {e: [] for e in ENGS}
        self.base = (nc.sbuf_base + 63) // 64 * 64
        self.top = nc.sbuf_top
        self.ptr = self.base
        self.uid = 0
        self.epoch = 0

    def reset_arena(self, keep=None):
        self.ptr = self.base if keep is None else keep

    def sb(self, name, shape, dtype=F32):
        esz = 4 if dtype == F32 else 2
        n = 1
        for s in shape[1:]:
            n *= s
        nbytes = (n * esz + 63) // 64 * 64
        assert self.ptr + nbytes <= self.top, "SBUF arena overflow at %s (%d)" % (name, self.ptr + nbytes - self.top)
        self.uid += 1
        t = self.nc.alloc_sbuf_tensor_at("%s_%d" % (name, self.uid), list(shape), dtype, offset=self.ptr)
        self.ptr += nbytes
        b = Buf(name)
        self.bufs.append(b)
        return T(t, b)

    def ps(self, name, shape, dtype=F32):
        t = self.nc.alloc_psum_tensor(name, list(shape), dtype)
        b = Buf(name)
        self.bufs.append(b)
        return T(t, b)

    def _rec(self, op, reads, writes):
        raw, other = [], []
        for t in reads:
            b = t.buf
            if b.w is not None:
                raw.append(b.w)
        for t in writes:
            b = t.buf
            if b.w is not None:
                other.append(b.w)
            other.extend(b.r)
        seen = set()
        for lst, is_raw in ((raw, True), (other, False)):
            for d in lst:
                if d is op or id(d) in seen:
                    continue
                if (not d.dma) and (not op.dma) and d.eng == op.eng:
                    if op.eng == "pe" or not is_raw:
                        continue
                seen.add(id(d))
                op.deps.append(d)
                if not d.dma:
                    d.sig = True
        op.epoch = self.epoch
        for t in reads:
            r = t.buf.r
            if not op.dma:
                r[:] = [x for x in r if x.dma or x.eng != op.eng]
            r.append(op)
        for t in writes:
            t.buf.w = op
            t.buf.r = []
        self.ops[op.eng].append(op)
        if not op.dma:
            self.last[op.eng] = op
        return op

    def op(self, eng, fn, reads=(), writes=()):
        return self._rec(Op(eng, fn, False), reads, writes)

    def dma(self, eng, out, in_, reads=(), writes=()):
        o = Op(eng, (out, in_), True)
        n = self.ndma[eng]
        self.ndma[eng] = n + 1
        o.slot = n % NSLOT
        o.dmaval = 16 * (n // NSLOT + 1)
        self._rec(o, reads, writes)
        rd = self.recent_dma[eng]
        rd.append(o)
        if len(rd) > NSLOT:
            rd.pop(0)
        return o

    def barrier(self):
        lastc = [self.last[e] for e in ENGS if self.last[e] is not None]
        dmas = [d for e in ENGS for d in self.recent_dma[e]]
        for e in ENGS:
            o = Op(e, None, False)
            o.epoch = self.epoch
            for d in lastc:
                if d.eng != e:
                    o.deps.append(d)
                    d.sig = True
            o.deps.extend(dmas)
            self.ops[e].append(o)
        for b in self.bufs:
            b.w = None
            b.r = []
        self.epoch += 1

    def emit(self):
        nc = self.nc
        used = set()
        for e in ENGS:
            c = {}
            for o in self.ops[e]:
                if (not o.dma) and o.sig:
                    c[o.epoch] = c.get(o.epoch, 0) + 1
                    o.sigval = c[o.epoch]
                    used.add((e, o.epoch))
        with contextlib.ExitStack() as st:
            csem = {k: st.enter_context(nc.semaphore("c_%s_%d" % k)) for k in sorted(used)}
            dsem = {e: [st.enter_context(nc.semaphore("d_%s%d" % (e, i))) for i in range(NSLOT)]
                    for e in ENGS if self.ndma[e] > 0}
            block = st.enter_context(nc.Block())

            def run(e, eng):
                known = {}
                for o in self.ops[e]:
                    waits = {}
                    for d in o.deps:
                        if d.dma:
                            key = ("d", d.eng, d.slot)
                            v = d.dmaval
                        else:
                            key = ("c", d.eng, d.epoch)
                            v = d.sigval
                        if known.get(key, 0) >= v:
                            continue
                        if waits.get(key, 0) < v:
                            waits[key] = v
                    if o.dma and o.dmaval > 16:
                        key = ("d", e, o.slot)
                        v = o.dmaval - 16
                        if known.get(key, 0) < v and waits.get(key, 0) < v:
                            waits[key] = v
                    for key, v in waits.items():
                        sem = csem[(key[1], key[2])] if key[0] == "c" else dsem[key[1]][key[2]]
                        eng.wait_ge(sem, v)
                        known[key] = v
                    if o.dma:
                        out, in_ = o.fn
                        eng.dma_start(out=out, in_=in_).then_inc(dsem[e][o.slot], 16)
                    elif o.fn is not None:
                        ins = o.fn(eng)
                        if o.sig:
                            ins.then_inc(csem[(e, o.epoch)], 1)
                    elif o.sig:
                        eng.nop().then_inc(csem[(e, o.epoch)], 1)

            @block.tensor
            def _(eng):
                run("pe", eng)

            @block.scalar
            def _(eng):
                run("act", eng)

            @block.vector
            def _(eng):
                run("dve", eng)

            @block.gpsimd
            def _(eng):
                run("pool", eng)

            @block.sync
            def _(eng):
                run("sp", eng)

D = 1024
DC = 8
FF = 2816
FC = 22
NIN = 10768
EPS = 1e-6
QKV0, Z0, B0, A0, RQ0, RK0, RV0, RG0, AQ0, AK0, AV0, GT0 = (
    0, 2048, 3072, 3080, 3088, 3600, 4112, 5136, 6160, 7184, 7440, 7696)
BIG = 30000.0
WNAMES = ("ffn1_w1", "ffn1_w3", "ffn1_w2", "w_in", "w_branch_gdn", "w_branch_ret",
          "w_branch_attn", "w_out", "ffn2_w1", "ffn2_w3", "ffn2_w2")
WSHAPES = {"ffn1_w1": (D, FF), "ffn1_w3": (D, FF), "ffn1_w2": (FF, D), "w_in": (D, NIN),
           "w_branch_gdn": (D, D), "w_branch_ret": (D, D), "w_branch_attn": (D, D), "w_out": (D, D),
           "ffn2_w1": (D, FF), "ffn2_w3": (D, FF), "ffn2_w2": (FF, D)}
MUL, ADD, SUB = ALU.mult, ALU.add, ALU.subtract


class K:
    def __init__(self, nc, S, L, dbg=(), phases=None):
        self.nc = nc
        self.S = S
        self.L = L
        self.NT = S // 512
        self.fw = FW(nc)
        self.dbg = set(dbg)
        self.phases = phases
        fw = self.fw
        di = lambda n, s, dt=F32: nc.dram_tensor(n, list(s), dt, kind="ExternalInput").ap()
        self.x_in = di("x", (S, D))
        self.y_out = nc.dram_tensor("y", [S, D], F32, kind="ExternalOutput").ap()
        self.wsrc = {n: di(n, (L,) + WSHAPES[n]) for n in WNAMES}
        self.gains_d = di("gains_fm", (L, 128, 3, 8))
        self.conv_d = di("conv_fm", (L, 128, 16, 5))
        self.gdnnorm_d = di("gdnnorm_rep", (L, 128, 256))
        self.retnorm_d = di("retnorm_rep", (L, 128, 1024))
        self.qkcol_d = di("qk_gain_col", (L, 128, 2))
        self.qkrep_d = di("qk_gain_rep", (L, 128, 2, 128))
        self.alog_d = di("alog_rep", (L, 128, 4, 8))
        self.dtb_d = di("dtb_rep", (L, 128, 4, 8))
        self.rlogit_d = di("rlogit_rep", (L, 128, 8))
        self.cf_d = di("c_f32", (128, 9, 128))
        self.cb_d = di("c_bf16", (128, 2, 128), BF16)
        self.rope_d = di("rope_tab", (6, 128, S))
        self.scr = {}
        sc = self.scratch
        self.wbf = {n: sc("bf_" + n, (L,) + WSHAPES[n], BF16) for n in WNAMES}
        sc("xT", (DC, 128, S)); sc("x1T", (DC, 128, S))
        sc("qkv_pre", (16, 128, S + 4))
        sc("zs", (S, 1024), BF16); sc("gb", (S, 16))
        sc("rqT", (4, 128, S), BF16); sc("rkT", (4, 128, S), BF16)
        sc("rv", (S, 1024), BF16); sc("rgs", (S, 1024), BF16)
        sc("aqT", (8, 128, S), BF16); sc("akT", (2, 128, S), BF16); sc("av", (S, 256), BF16)
        sc("gatesT", (24, 128, S), BF16)
        sc("ofwd", (S, 1024)); sc("ofwd_r", (S, 1024))
        sc("gqT", (4, 128, S), BF16); sc("gkT", (4, 128, S), BF16); sc("gvT", (8, 128, S), BF16)
        sc("goT", (8, 128, S), BF16); sc("roT", (8, 128, S), BF16); sc("aoT", (8, 128, S), BF16)
        self.PP = [nc.alloc_psum_tensor("PP%d" % i, [128, 2, 512], F32) for i in range(4)]
        self.P = []
        for i in range(8):
            b = Buf("P%d" % i)
            fw.bufs.append(b)
            self.P.append(T(self.PP[i // 2][:, i % 2, :], b))
        self.build()

    def scratch(self, name, shape, dt=F32):
        kind = "ExternalOutput" if name in self.dbg else "Internal"
        t = self.nc.dram_tensor(name, list(shape), dt, kind=kind).ap()
        self.scr[name] = t
        return t

    def dump(self, name, tile, ap=None, dt=F32):
        if ("dbg_" + name) not in self.dbg or ("dbg_" + name) in self.scr:
            return
        ap = tile[:] if ap is None else ap
        d = self.nc.dram_tensor("dbg_" + name, [int(x) for x in ap.shape], dt, kind="ExternalOutput").ap()
        self.scr["dbg_" + name] = d
        self.fw.dma("pool", d, ap, reads=[tile])

    def mm(self, out, lhsT, rhs, reads, writes, start=True, stop=True):
        self.fw.op("pe", lambda e: e.matmul(out, lhsT, rhs, start=start, stop=stop), reads, writes)

    def act(self, out, in_, func, reads, writes, **kw):
        self.fw.op("act", lambda e: e.activation(out, in_, func, **kw), reads, writes)

    def amul(self, out, in_, m, reads, writes):
        self.fw.op("act", lambda e: e.mul(out, in_, m), reads, writes)

    def acopy(self, out, in_, reads, writes):
        self.fw.op("act", lambda e: e.copy(out, in_), reads, writes)

    def vcopy(self, out, in_, reads, writes):
        self.fw.op("dve", lambda e: e.tensor_copy(out, in_), reads, writes)

    def tt(self, out, a, b, op, reads, writes):
        self.fw.op("dve", lambda e: e.tensor_tensor(out, a, b, op=op), reads, writes)

    def ptt(self, out, a, b, op, reads, writes):
        self.fw.op("pool", lambda e: e.tensor_tensor(out, a, b, op=op), reads, writes)

    def ts(self, out, a, s1, op0, reads, writes):
        self.fw.op("dve", lambda e: e.tensor_scalar(out, a, s1, None, op0=op0), reads, writes)

    def stt(self, out, a, s, b, op0, op1, reads, writes):
        self.fw.op("dve", lambda e: e.scalar_tensor_tensor(out, a, s, b, op0=op0, op1=op1), reads, writes)

    def recip(self, out, in_, reads, writes):
        self.fw.op("dve", lambda e: e.reciprocal(out, in_), reads, writes)

    def memset(self, out, val, writes):
        self.fw.op("dve", lambda e: e.memset(out, val), (), writes)

    def build(self):
        fw = self.fw
        self.cf = fw.sb("cf", [128, 9, 128])
        self.cb = fw.sb("cb", [128, 2, 128], BF16)
        fw.dma("sp", self.cf[:], self.cf_d, writes=[self.cf])
        fw.dma("sp", self.cb[:], self.cb_d, writes=[self.cb])
        self.keep = fw.ptr
        self.identf = self.cf[:, 0, :]
        self.onesf = self.cf[:, 1, :]
        self.identb = self.cb[:, 0, :]
        self.onesb = self.cb[:, 1, :]
        ph = self.phases
        on = lambda p: ph is None or p in ph
        if on("cast"):
            self.cast_weights()
            fw.barrier()
        for l in range(self.L):
            if on("A"):
                self.phase_A(l)
                fw.barrier()
            if ph is None:
                fw.reset_arena(self.keep)
                g1 = self.sweep(l, "gdn", 0, reset=False)
                g2 = self.sweep(l, "ret", 0, reset=False)
                live = [g1, g2]
                while live:
                    for g in list(live):
                        try:
                            next(g)
                        except StopIteration:
                            live.remove(g)
                fw.barrier()
                for kind in ("gdn", "ret"):
                    for _ in self.sweep(l, kind, 1):
                        pass
                    fw.barrier()
            else:
                for kind in ("gdn", "ret"):
                    for dr in (0, 1):
                        if on(kind) or on(kind + str(dr)):
                            for _ in self.sweep(l, kind, dr):
                                pass
                            fw.barrier()
            if on("att"):
                self.attention(l)
                fw.barrier()
            if on("E"):
                self.phase_E(l)
                fw.barrier()
        fw.emit()

    def cast_weights(self):
        fw = self.fw
        for l in range(self.L):
            for n in WNAMES:
                R, C = WSHAPES[n]
                for r0 in range(0, R, 128):
                    fw.dma("pool", self.wbf[n][l, r0:r0 + 128, :], self.wsrc[n][l, r0:r0 + 128, :])
        z = self.cf[:, 8, 0:2]
        for c in range(16):
            fw.dma("sp", self.scr["qkv_pre"][c, :, 0:2], z, reads=[self.cf])
            fw.dma("sp", self.scr["qkv_pre"][c, :, self.S + 2:self.S + 4], z, reads=[self.cf])

    def rmsnorm(self, xT, gains, gi, xn, sqrot, rsd, ps):
        for c in range(DC):
            sq = sqrot.next()
            self.act(sq[:], xT[:, c, :], AF.Square, [xT], [sq])
            self.mm(ps[:], self.onesf, sq[:], [sq, self.cf], [ps], start=(c == 0), stop=(c == DC - 1))
        self.act(rsd[:], ps[:], AF.Ln, [ps], [rsd], bias=EPS, scale=1.0 / D)
        self.act(rsd[:], rsd[:], AF.Exp, [rsd], [rsd], scale=-0.5)
        for c in range(DC):
            self.stt(xn[:, c, :], xT[:, c, :], gains[:, gi, c:c + 1], rsd[:], MUL, MUL, [xT, rsd, gains], [xn])

    def ffn(self, l, pre, xT, xn, gT, w13rot, w2rot, srot, prot):
        fw = self.fw
        W1, W3, W2 = self.wbf[pre + "_w1"], self.wbf[pre + "_w3"], self.wbf[pre + "_w2"]
        for g in range(FC // 2):
            w = w13rot.next()
            f0 = g * 256
            fw.dma("sp", w[:, 0], W1[l, :, f0:f0 + 256].rearrange("(k p) f -> p k f", p=128), writes=[w])
            fw.dma("sp", w[:, 1], W3[l, :, f0:f0 + 256].rearrange("(k p) f -> p k f", p=128), writes=[w])
            for j in range(2):
                fc = g * 2 + j
                p1 = prot.next()
                p3 = prot.next()
                for k in range(DC):
                    for which, pp in ((0, p1), (1, p3)):
                        self.mm(pp[:], w[:, which, k, j * 128:(j + 1) * 128], xn[:, k, :], [w, xn], [pp],
                                start=(k == 0), stop=(k == DC - 1))
                s = srot.next()
                self.act(s[:], p1[:], AF.Silu, [p1], [s])
                self.tt(gT[:, fc, :], s[:], p3[:], MUL, [s, p3], [gT])
        for g in range(4):
            w = w2rot.next()
            d0 = g * 256
            fw.dma("sp", w[:], W2[l, :, d0:d0 + 256].rearrange("(k p) f -> p k f", p=128), writes=[w])
            pps = [prot.next(), prot.next()]
            for k in range(FC):
                for j in range(2):
                    self.mm(pps[j][:], w[:, k, j * 128:(j + 1) * 128], gT[:, k, :], [w, gT], [pps[j]], start=(k == 0), stop=(k == FC - 1))
            for j in range(2):
                dc = g * 2 + j
                self.stt(xT[:, dc, :], pps[j][:], 0.5, xT[:, dc, :], MUL, ADD, [pps[j], xT], [xT])

    def phase_A(self, l):
        fw, S, scr = self.fw, self.S, self.scr
        fw.reset_arena(self.keep)
        gains = fw.sb("gains", [128, 3, 8])
        fw.dma("sp", gains[:], self.gains_d[l], writes=[gains])
        qkcol = fw.sb("qkcol", [128, 2])
        fw.dma("sp", qkcol[:], self.qkcol_d[l], writes=[qkcol])
        alog = fw.sb("alog", [128, 4, 8]); dtb = fw.sb("dtb", [128, 4, 8])
        fw.dma("sp", alog[:], self.alog_d[l], writes=[alog])
        fw.dma("sp", dtb[:], self.dtb_d[l], writes=[dtb])
        self.act(alog[:], alog[:], AF.Exp, [alog], [alog])
        self.ts(alog[:], alog[:], -1.0, MUL, [alog], [alog])
        xrot = Rot([fw.sb("xT%d" % i, [128, DC, 512]) for i in range(2)])
        xn = fw.sb("xn", [128, DC, 512], BF16)
        gT = fw.sb("gT", [128, FC, 512], BF16)
        w13rot = Rot([fw.sb("w13_%d" % i, [128, 2, DC, 256], BF16) for i in range(2)])
        w2rot = Rot([fw.sb("w2_%d" % i, [128, FC, 256], BF16) for i in range(2)])
        wgrot = Rot([fw.sb("wg_%d" % i, [128, DC, 512], BF16) for i in range(2)])
        wsm = fw.sb("wsm", [128, DC, 16], BF16)
        srot = Rot([fw.sb("s_%d" % i, [128, 512]) for i in range(3)])
        sqrot = Rot([fw.sb("sq_%d" % i, [128, 512]) for i in range(3)])
        rsd = fw.sb("rsd", [128, 512])
        rsd2 = Rot([fw.sb("rsd2_%d" % i, [128, 512]) for i in range(2)])
        fst = Rot([fw.sb("fst_%d" % i, [128, 512]) for i in range(4)])
        bst = Rot([fw.sb("bst_%d" % i, [128, 512], BF16) for i in range(4)])
        tm = Rot([fw.sb("tm_%d" % i, [128, 4, 1024], BF16) for i in range(2)])
        tmv = fw.sb("tmv", [128, 4, 256], BF16)
        gbs = fw.sb("gbs", [128, 4, 16])
        gtmp = fw.sb("gtmp", [128, 4, 8])
        rope = fw.sb("rope", [128, 6, 512])
        xs = Rot([fw.sb("xs_%d" % i, [128, 512]) for i in range(3)])
        t1r = Rot([fw.sb("t1_%d" % i, [128, 512]) for i in range(2)])
        t2r = Rot([fw.sb("t2_%d" % i, [128, 512]) for i in range(2)])
        xin = fw.sb("xin", [128, 4, 256]) if l == 0 else None
        prot = Rot(self.P)
        Wi = self.wbf["w_in"]
        fw.dma("sp", wsm[:], Wi[l, :, B0:B0 + 16].rearrange("(k p) f -> p k f", p=128), writes=[wsm])

        def wload(c0, n):
            w = wgrot.next()
            fw.dma("sp", w[:, :, 0:n], Wi[l, :, c0:c0 + n].rearrange("(k p) f -> p k f", p=128), writes=[w])
            return w

        def fm_chunk(w, j, pp):
            for k in range(DC):
                self.mm(pp[:], w[:, k, j * 128:(j + 1) * 128], xn[:, k, :], [w, xn], [pp], start=(k == 0), stop=(k == DC - 1))

        def tm_block(w, n, blk, pp):
            for k in range(DC):
                self.mm(pp[:, 0:n], xn[:, k, blk * 128:(blk + 1) * 128], w[:, k, 0:n], [w, xn], [pp], start=(k == 0), stop=(k == DC - 1))

        def do_rope(src, ci, si, pairs, out):
            t1 = t1r.next(); t2 = t2r.next()
            self.tt(t1[:], src[:], rope[:, ci, :], MUL, [src, rope], [t1])
            for (dl, sl, n) in pairs:
                self.tt(t2[dl:dl + n, :], src[sl:sl + n, :], rope[sl:sl + n, si, :], MUL, [src, rope], [t2])
            self.ptt(out[:], t1[:], t2[:], ADD, [t1, t2], [out])

        PR = [(0, 64, 64), (64, 0, 64)]
        PA = [(0, 32, 32), (32, 0, 32), (64, 96, 32), (96, 64, 32)]

        for tt in range(self.NT):
            t0 = tt * 512
            xT = xrot.next()
            if l == 0:
                for dq in range(4):
                    fw.dma("sp", xin[:], self.x_in[t0:t0 + 512, dq * 256:(dq + 1) * 256].rearrange("(n p) d -> p n d", p=128),
                           writes=[xin])
                    for j in range(2):
                        pp = prot.next()
                        for blk in range(4):
                            self.mm(pp[:, blk * 128:(blk + 1) * 128], xin[:, blk, j * 128:(j + 1) * 128], self.identf, [xin, self.cf], [pp])
                        c = dq * 2 + j
                        if j:
                            self.acopy(xT[:, c, :], pp[:], [pp], [xT])
                        else:
                            self.vcopy(xT[:, c, :], pp[:], [pp], [xT])
            else:
                fw.dma("sp", xT[:], scr["xT"][:, :, t0:t0 + 512].rearrange("c p t -> p c t"), writes=[xT])
            fw.dma("sp", rope[:], self.rope_d[:, :, t0:t0 + 512].rearrange("r p t -> p r t"), writes=[rope])
            self.rmsnorm(xT, gains, 0, xn, sqrot, rsd, prot.next())
            self.ffn(l, "ffn1", xT, xn, gT, w13rot, w2rot, srot, prot)
            fw.dma("pool", scr["x1T"][:, :, t0:t0 + 512].rearrange("c p t -> p c t"), xT[:], reads=[xT])
            self.rmsnorm(xT, gains, 1, xn, sqrot, rsd, prot.next())
            for g in range(4):
                w = wload(QKV0 + g * 512, 512)
                for j in range(4):
                    pp = prot.next(); fm_chunk(w, j, pp)
                    st = fst.next()
                    if j % 2:
                        self.acopy(st[:], pp[:], [pp], [st])
                    else:
                        self.vcopy(st[:], pp[:], [pp], [st])
                    fw.dma("pool", scr["qkv_pre"][g * 4 + j, :, 2 + t0:2 + t0 + 512], st[:], reads=[st])
            for (c0, dst, silu) in ((Z0, "zs", True), (RV0, "rv", False), (RG0, "rgs", True)):
                tmt = tm.next()
                for g in range(2):
                    w = wload(c0 + g * 512, 512)
                    for blk in range(4):
                        pp = prot.next(); tm_block(w, 512, blk, pp)
                        if silu:
                            self.act(tmt[:, blk, g * 512:(g + 1) * 512], pp[:], AF.Silu, [pp], [tmt])
                        else:
                            self.vcopy(tmt[:, blk, g * 512:(g + 1) * 512], pp[:], [pp], [tmt])
                fw.dma("pool", scr[dst][t0:t0 + 512, :].rearrange("(n p) c -> p n c", p=128), tmt[:], reads=[tmt])
            w = wload(AV0, 256)
            for blk in range(4):
                pp = prot.next(); tm_block(w, 256, blk, pp)
                self.vcopy(tmv[:, blk, :], pp[:, 0:256], [pp], [tmv])
            fw.dma("pool", scr["av"][t0:t0 + 512, :].rearrange("(n p) c -> p n c", p=128), tmv[:], reads=[tmv])
            pp = prot.next()
            for blk in range(4):
                for k in range(DC):
                    self.mm(pp[:, blk * 16:(blk + 1) * 16], xn[:, k, blk * 128:(blk + 1) * 128], wsm[:, k, :], [wsm, xn], [pp],
                            start=(k == 0), stop=(k == DC - 1))
            pv = pp[:, 0:64].rearrange("p (n c) -> p n c", c=16)
            self.act(gbs[:, :, 0:8], pv[:, :, 0:8], AF.Sigmoid, [pp], [gbs])
            self.acopy(gtmp[:], pv[:, :, 8:16], [pp], [gtmp])
            self.tt(gtmp[:], gtmp[:], dtb[:], ADD, [gtmp, dtb], [gtmp])
            self.act(gtmp[:], gtmp[:], AF.Exp, [gtmp], [gtmp])
            self.act(gtmp[:], gtmp[:], AF.Ln, [gtmp], [gtmp], bias=1.0)
            self.tt(gbs[:, :, 8:16], gtmp[:], alog[:], MUL, [gtmp, alog], [gbs])
            fw.dma("pool", scr["gb"][t0:t0 + 512, :].rearrange("(n p) c -> p n c", p=128), gbs[:], reads=[gbs])
            for (c0, dst, ci) in ((RQ0, "rqT", 0), (RK0, "rkT", 2)):
                w = wload(c0, 512)
                for j in range(4):
                    pp = prot.next(); fm_chunk(w, j, pp)
                    x_sb = xs.next()
                    self.acopy(x_sb[:], pp[:], [pp], [x_sb])
                    o = bst.next()
                    do_rope(x_sb, ci, ci + 1, PR, o)
                    fw.dma("pool", scr[dst][j, :, t0:t0 + 512], o[:], reads=[o])
            jobs = []
            for g in range(2):
                jobs += [("aqT", AQ0 + g * 512, g * 4 + j, j, 0) for j in range(4)]
            jobs += [("akT", AK0, j, j, 1) for j in range(2)]
            wcur = {}
            pend = None

            def stage2(job, x_sb, sq):
                dst, c0, oc, j, gi = job
                p2 = prot.next()
                self.mm(p2[:], self.onesf, sq[:], [sq, self.cf], [p2])
                r = rsd2.next()
                self.act(r[:], p2[:], AF.Ln, [p2], [r], bias=EPS, scale=1.0 / 128)
                self.act(r[:], r[:], AF.Exp, [r], [r], scale=-0.5)
                self.stt(x_sb[:], x_sb[:], qkcol[:, gi:gi + 1], r[:], MUL, MUL, [x_sb, r, qkcol], [x_sb])
                o = bst.next()
                do_rope(x_sb, 4, 5, PA, o)
                fw.dma("pool", scr[dst][oc, :, t0:t0 + 512], o[:], reads=[o])

            for job in jobs:
                dst, c0, oc, j, gi = job
                if c0 not in wcur:
                    wcur[c0] = wload(c0, 512 if gi == 0 else 256)
                w = wcur[c0]
                pp = prot.next(); fm_chunk(w, j, pp)
                sq = sqrot.next(); x_sb = xs.next()
                self.act(sq[:], pp[:], AF.Square, [pp], [sq])
                self.acopy(x_sb[:], pp[:], [pp], [x_sb])
                if pend is not None:
                    stage2(*pend)
                pend = (job, x_sb, sq)
            stage2(*pend)
            for g in range(6):
                w = wload(GT0 + g * 512, 512)
                for j in range(4):
                    pp = prot.next(); fm_chunk(w, j, pp)
                    o = bst.next()
                    self.act(o[:], pp[:], AF.Sigmoid, [pp], [o])
                    fw.dma("pool", scr["gatesT"][g * 4 + j, :, t0:t0 + 512], o[:], reads=[o])

    def sweep(self, l, kind, dr, reset=True):
        fw, S, scr = self.fw, self.S, self.scr
        if reset:
            fw.reset_arena(self.keep)
        gdn = kind == "gdn"
        OFW = "ofwd" if gdn else "ofwd_r"
        cf, cb = self.cf, self.cb
        tri, pos, neg = cf[:, 2 + dr, :], cf[:, 4 + dr, :], cf[:, 6 + dr, :]
        identb, identf, onesf = self.identb, self.identf, self.onesf
        prot = Rot(self.P)
        H = range(4)
        GB = 2
        St = [fw.sb("S%d" % h, [128, 256]) for h in H]
        Sb = [fw.sb("Sb%d" % h, [128, 256], BF16) for h in H]
        for h in H:
            self.memset(St[h][:], 0.0, [St[h]])
            self.memset(Sb[h][:], 0.0, [Sb[h]])
        R2 = lambda n, shp, dt=F32, k=2: Rot([fw.sb("%s_%d" % (n, i), shp, dt) for i in range(k)])
        BH = lambda n, shp, dt=F32: [[fw.sb("%s_%d_%d" % (n, b, h), shp, dt) for h in H] for b in range(GB)]
        NB = GB + 1
        gcs, negc, egc, kds, egt, dtmp = (R2(n, [128, 4], F32, NB) for n in ("gcs", "negc", "egc", "kds", "egt", "dtmp"))
        Gbr = R2("Gb", [128, 128], F32, 4)
        gsbr = R2("gsb", [128, 8], F32, NB)
        ETb, ATb, kdb = BH("ET", [128, 128]), BH("AT", [128, 128], BF16), BH("kd", [128, 128], BF16)
        tmpBr = R2("tmpB", [128, 256], F32, 4)
        o_r = R2("o_sb", [128, 1024], F32, 2)
        if gdn:
            negb, bge = R2("negb", [128, 4], F32, NB), R2("bge", [128, 4], F32, NB)
            EAb = BH("EA", [128, 128])
            kbgb, vbb = BH("kbg", [128, 128], BF16), BH("vb", [128, 256], BF16)
            Qb = [BH("Qa", [128, 128]), BH("Qb", [128, 128])]
            QTb = [BH("QTa", [128, 128]), BH("QTb", [128, 128])]
            PTb_ = [BH("PTa", [128, 128]), BH("PTb", [128, 128])]
            PTh = BH("PTh", [128, 128], BF16)
            usb, wTb = BH("us", [128, 256]), BH("wT", [128, 128], BF16)
            vnr = [R2("vn%d" % h, [128, 256], BF16) for h in H]
            if dr == 0:
                xq = fw.sb("xq", [128, 16, 516])
                convw = fw.sb("convw", [128, 16, 5])
                fw.dma("sp", convw[:], self.conv_d[l], writes=[convw])
                accr = R2("acc", [128, 512], F32, 4)
                sqr = R2("sq", [128, 512], F32, 4)
                rsr = R2("rs", [128, 512], F32, 4)
            vT = fw.sb("vT", [128, 8, 512], BF16)
            gbt = fw.sb("gbt", [128, 4, 16])
        else:
            rvt = fw.sb("rvt", [128, 4, 1024], BF16)
            gl = fw.sb("gl", [128, 8])
            fw.dma("sp", gl[:], self.rlogit_d[l], writes=[gl])
            self.act(gl[:], gl[:], AF.Exp, [gl], [gl], scale=-1.0)
            self.act(gl[:], gl[:], AF.Ln, [gl], [gl], bias=1.0)
            self.ts(gl[:], gl[:], -1.0, MUL, [gl], [gl])
        qT = fw.sb("qT", [128, 4, 512], BF16)
        kT = fw.sb("kT", [128, 4, 512], BF16)
        if dr == 1:
            ofw = R2("ofw", [128, 1024], F32, 2)
            osum = fw.sb("osum", [128, 1024])
            gate = R2("gate", [128, 1024], BF16, 2)
            yb = R2("yb", [128, 1024], BF16, 2)
            yT = R2("yT", [128, 8, 512], BF16, 2)
            nrm = fw.sb("nrm", [128, 256 if gdn else 1024])
            fw.dma("sp", nrm[:], (self.gdnnorm_d if gdn else self.retnorm_d)[l], writes=[nrm])
            junk = R2("junk", [128, 256], F32, 2)
            st4 = R2("st4", [128, 4], F32, 4)
            cen = [fw.sb("cen%d" % h, [128, 256]) for h in H] if not gdn else None

        def decay_prep(g, gbuf, bi):
            pg = prot.next()
            self.mm(pg[:, 0:4], tri, g, [gbuf, cf], [pg])
            self.mm(pg[:, 4:8], onesf, g, [gbuf, cf], [pg])
            c_gcs, c_negc, c_egc, c_kds, c_egt, c_d = (r.next() for r in (gcs, negc, egc, kds, egt, dtmp))
            gsb = gsbr.next()
            self.acopy(gsb[:], pg[:, 0:8], [pg], [gsb])
            self.vcopy(c_gcs[:], gsb[:, 0:4], [gsb], [c_gcs])
            self.act(c_egc[:], gsb[:, 0:4], AF.Exp, [gsb], [c_egc])
            self.tt(c_d[:], gsb[:, 4:8], gsb[:, 0:4], SUB, [gsb], [c_d])
            self.act(c_kds[:], c_d[:], AF.Exp, [c_d], [c_kds])
            self.act(c_egt[:], gsb[:, 4:8], AF.Exp, [gsb], [c_egt])
            self.ts(c_negc[:], gsb[:, 0:4], -1.0, MUL, [gsb], [c_negc])
            Gbs = []
            for h in H:
                Gb = Gbr.next()
                self.ts(Gb[:], onesf, g[:, h:h + 1], MUL, [gbuf, cf], [Gb])
                Gbs.append(Gb)
            pcs, pas = [], []
            for h in H:
                pc = prot.next()
                self.mm(pc[:, 0:128], Gbs[h][:], tri, [Gbs[h], cf], [pc], start=True, stop=False)
                self.mm(pc[:, 0:128], identf, neg, [cf], [pc], start=False, stop=True)
                if gdn:
                    self.mm(pc[:, 128:256], Gbs[h][:], tri, [Gbs[h], cf], [pc], start=True, stop=False)
                    self.mm(pc[:, 128:256], identf, pos, [cf], [pc], start=False, stop=True)
                pcs.append(pc)
            for h in H:
                self.act(ETb[bi][h][:], pcs[h][:, 0:128], AF.Exp, [pcs[h], c_negc], [ETb[bi][h]], bias=c_negc[:, h:h + 1])
                if gdn:
                    self.act(EAb[bi][h][:], pcs[h][:, 128:256], AF.Exp, [pcs[h], c_gcs], [EAb[bi][h]], bias=c_gcs[:, h:h + 1], scale=-1.0)
            return dict(gcs=c_gcs, egc=c_egc, kds=c_kds, egt=c_egt, ET=ETb[bi], EA=EAb[bi] if gdn else None)

        if not gdn:
            dec_c = decay_prep(gl[:, dr * 4:dr * 4 + 4], gl, 0)

        def norm2(c, sl, sq):
            p2 = prot.next()
            self.mm(p2[:], onesf, sq[:], [sq, cf], [p2])
            r = rsr.next()
            self.act(r[:], p2[:], AF.Ln, [p2], [r], bias=EPS)
            self.act(r[:], r[:], AF.Exp, [r], [r], scale=-0.5)
            dst = qT if c < 4 else kT
            sc_ = (128.0 ** -0.5) if c < 4 else 1.0
            self.stt(dst[:, c % 4, :], sl[:], sc_, r[:], MUL, MUL, [sl, r], [dst])

        import os
        STOP = int(os.environ.get("SW_STOP", "99"))
        tiles = list(range(self.NT))
        blks = list(range(4))
        if dr == 1:
            tiles.reverse(); blks.reverse()
        for tt in tiles:
            t0 = tt * 512
            if gdn and dr == 1:
                fw.dma("sp", qT[:], scr["gqT"][:, :, t0:t0 + 512].rearrange("c p t -> p c t"), writes=[qT])
                fw.dma("sp", kT[:], scr["gkT"][:, :, t0:t0 + 512].rearrange("c p t -> p c t"), writes=[kT])
                fw.dma("sp", vT[:], scr["gvT"][:, :, t0:t0 + 512].rearrange("c p t -> p c t"), writes=[vT])
                fw.dma("sp", gbt[:], scr["gb"][t0:t0 + 512, :].rearrange("(n p) c -> p n c", p=128), writes=[gbt])
            elif gdn:
                for c4 in range(0, 16, 2):
                    fw.dma("sp", xq[:, c4:c4 + 2, :], scr["qkv_pre"][c4:c4 + 2, :, t0:t0 + 516].rearrange("c p t -> p c t"), writes=[xq])
                fw.dma("sp", gbt[:], scr["gb"][t0:t0 + 512, :].rearrange("(n p) c -> p n c", p=128), writes=[gbt])
                for cg in range(0, 16, 4):
                    accs = [accr.next() for _ in range(4)]
                    for i in range(4):
                        c = cg + i
                        self.amul(accs[i][:], xq[:, c, 0:512], convw[:, c, 0:1], [xq, convw], [accs[i]])
                    for w in range(1, 5):
                        for i in range(4):
                            c = cg + i
                            self.stt(accs[i][:], xq[:, c, w:w + 512], convw[:, c, w:w + 1], accs[i][:], MUL, ADD, [xq, convw, accs[i]], [accs[i]])
                    if cg >= 8:
                        for i in range(4):
                            self.act(vT[:, cg + i - 8, :], accs[i][:], AF.Silu, [accs[i]], [vT])
                    else:
                        sqs = []
                        for i in range(4):
                            self.act(accs[i][:], accs[i][:], AF.Silu, [accs[i]], [accs[i]])
                        for i in range(4):
                            sq = sqr.next()
                            self.act(sq[:], accs[i][:], AF.Square, [accs[i]], [sq])
                            sqs.append(sq)
                        for i in range(4):
                            norm2(cg + i, accs[i], sqs[i])
                fw.dma("pool", scr["gqT"][:, :, t0:t0 + 512].rearrange("c p t -> p c t"), qT[:], reads=[qT])
                fw.dma("pool", scr["gkT"][:, :, t0:t0 + 512].rearrange("c p t -> p c t"), kT[:], reads=[kT])
                fw.dma("pool", scr["gvT"][:, :, t0:t0 + 512].rearrange("c p t -> p c t"), vT[:], reads=[vT])
            else:
                fw.dma("sp", qT[:], scr["rqT"][:, :, t0:t0 + 512].rearrange("c p t -> p c t"), writes=[qT])
                fw.dma("sp", kT[:], scr["rkT"][:, :, t0:t0 + 512].rearrange("c p t -> p c t"), writes=[kT])
                fw.dma("sp", rvt[:], scr["rv"][t0:t0 + 512, :].rearrange("(n p) c -> p n c", p=128), writes=[rvt])
            if dr == 1:
                yTt = yT.next()
            for g0 in range(0, 4, GB):
                grp = blks[g0:g0 + GB]
                BHs = [(bi, blk, h) for bi, blk in enumerate(grp) for h in H]
                bsl = {blk: slice(blk * 128, (blk + 1) * 128) for blk in grp}
                dec, beta, c_negb, c_bge = {}, {}, {}, {}
                for bi, blk in enumerate(grp):
                    if gdn:
                        dec[blk] = decay_prep(gbt[:, blk, 8 + dr * 4:12 + dr * 4], gbt, bi)
                        beta[blk] = gbt[:, blk, dr * 4:dr * 4 + 4]
                        c_negb[blk], c_bge[blk] = negb.next(), bge.next()
                        self.ts(c_negb[blk][:], beta[blk], -1.0, MUL, [gbt], [c_negb[blk]])
                        self.tt(c_bge[blk][:], beta[blk], dec[blk]["egc"][:], MUL, [gbt, dec[blk]["egc"]], [c_bge[blk]])
                    else:
                        dec[blk] = dec_c
                AT, kd, kbg, vb, Q, QT, PT = {}, {}, {}, {}, {}, {}, {}
                for sub in range(0, len(BHs), 4):
                    SUBB = BHs[sub:sub + 4]
                    ps1 = {}
                    for (bi, blk, h) in SUBB:
                        bs = bsl[blk]
                        qb, kb = qT[:, h, bs], kT[:, h, bs]
                        pa = prot.next()
                        self.mm(pa[:, 0:128], kb, qb, [qT, kT], [pa])
                        if gdn:
                            self.mm(pa[:, 128:256], kb, kb, [kT], [pa])
                        pb = prot.next()
                        self.mm(pb[:, 0:128], kb, identb, [kT, cb], [pb])
                        if gdn:
                            for j in range(2):
                                self.mm(pb[:, 128 + j * 128:256 + j * 128], vT[:, 2 * h + j, bs], identb, [vT, cb], [pb])
                        ps1[(blk, h)] = (pa, pb)
                    for (bi, blk, h) in SUBB:
                        pa, pb = ps1[(blk, h)]
                        d = dec[blk]
                        AT[(blk, h)] = ATb[bi][h]
                        self.tt(ATb[bi][h][:], pa[:, 0:128], d["ET"][h][:], MUL, [pa, d["ET"][h]], [ATb[bi][h]])
                        kd[(blk, h)] = kdb[bi][h]
                        self.amul(kdb[bi][h][:], pb[:, 0:128], d["kds"][:, h:h + 1], [pb, d["kds"]], [kdb[bi][h]])
                        if gdn:
                            kbg[(blk, h)] = kbgb[bi][h]
                            self.amul(kbgb[bi][h][:], pb[:, 0:128], c_bge[blk][:, h:h + 1], [pb, c_bge[blk]], [kbgb[bi][h]])
                            vb[(blk, h)] = vbb[bi][h]
                            self.amul(vbb[bi][h][:], pb[:, 128:384], beta[blk][:, h:h + 1], [pb, gbt], [vbb[bi][h]])
                            Q[(blk, h)] = Qb[0][bi][h]
                            self.stt(Qb[0][bi][h][:], pa[:, 128:256], c_negb[blk][:, h:h + 1], d["EA"][h][:], MUL, MUL,
                                     [pa, c_negb[blk], d["EA"][h]], [Qb[0][bi][h]])
                if gdn:
                    psn = {}
                    for (bi, blk, h) in BHs:
                        pn = prot.next()
                        self.mm(pn[:, 0:128], Q[(blk, h)][:], identf, [Q[(blk, h)], cf], [pn])
                        psn[(blk, h)] = pn
                    for (bi, blk, h) in BHs:
                        pn = psn[(blk, h)]
                        QT[(blk, h)] = QTb[0][bi][h]; PT[(blk, h)] = PTb_[0][bi][h]
                        self.acopy(QT[(blk, h)][:], pn[:, 0:128], [pn], [QT[(blk, h)]])
                        self.tt(PT[(blk, h)][:], QT[(blk, h)][:], identf, ADD, [QT[(blk, h)], cf], [PT[(blk, h)]])
                    for lev in range(1, 7):
                        pi = lev % 2
                        psq = {}
                        for (bi, blk, h) in BHs:
                            k_ = (blk, h)
                            pq = prot.next()
                            self.mm(pq[:, 0:128], QT[k_][:], Q[k_][:], [QT[k_], Q[k_]], [pq])
                            if lev < 6:
                                self.mm(pq[:, 128:256], Q[k_][:], QT[k_][:], [QT[k_], Q[k_]], [pq])
                            psq[k_] = pq
                        for n_, (bi, blk, h) in enumerate(BHs):
                            k_ = (blk, h)
                            pq = psq[k_]
                            cp = self.acopy if (n_ % 2 == 0 or lev == 6) else self.vcopy
                            Qn = Qb[pi][bi][h]
                            cp(Qn[:], pq[:, 0:128], [pq], [Qn])
                            if lev < 6:
                                QTn = QTb[pi][bi][h]
                                cp(QTn[:], pq[:, 128:256], [pq], [QTn])
                                QT[k_] = QTn
                            Q[k_] = Qn
                        psp = {}
                        for (bi, blk, h) in BHs:
                            k_ = (blk, h)
                            pp = prot.next()
                            self.mm(pp[:, 0:128], Q[k_][:], PT[k_][:], [Q[k_], PT[k_]], [pp])
                            psp[k_] = pp
                        for (bi, blk, h) in BHs:
                            k_ = (blk, h)
                            PTn = PTb_[pi][bi][h]
                            self.tt(PTn[:], psp[k_][:, 0:128], PT[k_][:], ADD, [psp[k_], PT[k_]], [PTn])
                            PT[k_] = PTn
                    for n_, (bi, blk, h) in enumerate(BHs):
                        k_ = (blk, h)
                        if n_ % 2:
                            self.acopy(PTh[bi][h][:], PT[k_][:], [PT[k_]], [PTh[bi][h]])
                        else:
                            self.vcopy(PTh[bi][h][:], PT[k_][:], [PT[k_]], [PTh[bi][h]])
                    psu = {}
                    for (bi, blk, h) in BHs:
                        k_ = (blk, h)
                        pu = prot.next()
                        self.mm(pu[:, 0:256], PTh[bi][h][:], vb[k_][:], [PTh[bi][h], vb[k_]], [pu])
                        self.mm(pu[:, 256:384], kbg[k_][:], PTh[bi][h][:], [PTh[bi][h], kbg[k_]], [pu])
                        psu[k_] = pu
                    us, wT = {}, {}
                    for (bi, blk, h) in BHs:
                        k_ = (blk, h)
                        us[k_] = usb[bi][h]; wT[k_] = wTb[bi][h]
                        cp = self.acopy if h % 2 else self.vcopy
                        cp(us[k_][:], psu[k_][:, 0:256], [psu[k_]], [us[k_]])
                        cp(wT[k_][:], psu[k_][:, 256:384], [psu[k_]], [wT[k_]])
                for bi, blk in enumerate(grp):
                    bs = bsl[blk]
                    tok0 = t0 + blk * 128
                    d = dec[blk]
                    if dr == 1:
                        c_ofw = ofw.next()
                        fw.dma("sp", c_ofw[:], scr[OFW][tok0:tok0 + 128, :], writes=[c_ofw])
                        c_gate = gate.next()
                        fw.dma("sp", c_gate[:], scr["zs" if gdn else "rgs"][tok0:tok0 + 128, :], writes=[c_gate])
                    o_sb = o_r.next()
                    vnew = {}
                    if gdn:
                        pws = {}
                        for h in H:
                            pws[h] = prot.next()
                            self.mm(pws[h][:, 0:256], wT[(blk, h)][:], Sb[h][:], [wT[(blk, h)], Sb[h]], [pws[h]])
                        for h in H:
                            vn = vnr[h].next()
                            self.tt(vn[:], us[(blk, h)][:], pws[h][:, 0:256], SUB, [us[(blk, h)], pws[h]], [vn])
                            vnew[h] = (vn[:], vn)
                    else:
                        for h in H:
                            vnew[h] = (rvt[:, blk, h * 256:(h + 1) * 256], rvt)
                    poA, poB = {}, {}
                    for h in H:
                        vap, vbuf = vnew[h]
                        poA[h] = prot.next()
                        self.mm(poA[h][:, 0:256], qT[:, h, bs], Sb[h][:], [qT, Sb[h]], [poA[h]])
                        self.mm(poA[h][:, 256:512], kd[(blk, h)][:], vap, [kd[(blk, h)], vbuf], [poA[h]])
                        poB[h] = prot.next()
                        self.mm(poB[h][:, 0:256], AT[(blk, h)][:], vap, [AT[(blk, h)], vbuf], [poB[h]])
                    for h in H:
                        self.stt(St[h][:], St[h][:], d["egt"][:, h:h + 1], poA[h][:, 256:512], MUL, ADD, [St[h], poA[h], d["egt"]], [St[h]])
                        self.acopy(Sb[h][:], St[h][:], [St[h]], [Sb[h]])
                    for h in H:
                        tB = tmpBr.next()
                        self.acopy(tB[:], poB[h][:, 0:256], [poB[h]], [tB])
                        self.stt(o_sb[:, h * 256:(h + 1) * 256], poA[h][:, 0:256], d["egc"][:, h:h + 1], tB[:], MUL, ADD,
                                 [poA[h], tB, d["egc"]], [o_sb])
                    if dr == 0:
                        fw.dma("pool", scr[OFW][tok0:tok0 + 128, :], o_sb[:], reads=[o_sb])
                        continue
                    self.tt(osum[:], o_sb[:], c_ofw[:], ADD, [o_sb, c_ofw], [osum])
                    c_yb = yb.next()
                    s1, s2 = st4.next(), st4.next()
                    srcs = {}
                    if gdn:
                        for h in H:
                            srcs[h] = (osum, osum[:, h * 256:(h + 1) * 256])
                    else:
                        for h in H:
                            jk = junk.next()
                            self.act(jk[:], osum[:, h * 256:(h + 1) * 256], AF.Copy, [osum], [jk, s1], accum_out=s1[:, h:h + 1])
                        self.ts(s1[:], s1[:], -1.0 / 256, MUL, [s1], [s1])
                        for h in H:
                            self.ts(cen[h][:], osum[:, h * 256:(h + 1) * 256], s1[:, h:h + 1], ADD, [osum, s1], [cen[h]])
                            srcs[h] = (cen[h], cen[h][:])
                    for h in H:
                        jk = junk.next()
                        self.act(jk[:], srcs[h][1], AF.Square, [srcs[h][0]], [jk, s2], accum_out=s2[:, h:h + 1])
                    self.act(s2[:], s2[:], AF.Sqrt, [s2], [s2], bias=EPS, scale=1.0 / 256)
                    self.recip(s2[:], s2[:], [s2], [s2])
                    tBs = {}
                    for h in H:
                        hs = slice(h * 256, (h + 1) * 256)
                        nap = nrm[:, 0:256] if gdn else nrm[:, hs]
                        tBs[h] = tmpBr.next()
                        self.stt(tBs[h][:], srcs[h][1], s2[:, h:h + 1], nap, MUL, MUL, [srcs[h][0], s2, nrm], [tBs[h]])
                    for h in H:
                        hs = slice(h * 256, (h + 1) * 256)
                        self.tt(c_yb[:, hs], tBs[h][:], c_gate[:, hs], MUL, [tBs[h], c_gate], [c_yb])
                    pps = []
                    for g in range(2):
                        pp = prot.next()
                        for j in range(4):
                            c = g * 4 + j
                            self.mm(pp[:, j * 128:(j + 1) * 128], c_yb[:, c * 128:(c + 1) * 128], identb, [c_yb, cb], [pp])
                        pps.append(pp)
                    for g in range(2):
                        src3 = pps[g][:, 0:512].rearrange("p (c t) -> p c t", t=128)
                        if g:
                            self.acopy(yTt[:, g * 4:g * 4 + 4, bs], src3, [pps[g]], [yTt])
                        else:
                            self.vcopy(yTt[:, g * 4:g * 4 + 4, bs], src3, [pps[g]], [yTt])
                yield
            if dr == 1:
                fw.dma("pool", scr["goT" if gdn else "roT"][:, :, t0:t0 + 512].rearrange("c p t -> p c t"), yTt[:], reads=[yTt])
        yield

    def attention(self, l):
        fw, S, scr = self.fw, self.S, self.scr
        fw.reset_arena(self.keep)
        NK = S // 128
        qkrep = fw.sb("qkrep", [128, 2, 128])
        fw.dma("sp", qkrep[:], self.qkrep_d[l], writes=[qkrep])
        mx = fw.sb("mx", [128, 2]); nb = fw.sb("nb", [128, 1])
        self.act(qkrep[:], qkrep[:], AF.Abs, [qkrep], [qkrep])
        fw.op("dve", lambda e: e.reduce_max(mx[:], qkrep[:], axis=AX.X), [qkrep], [mx])
        self.stt(nb[:], mx[:, 0:1], -(128.0 ** 0.5), mx[:, 1:2], MUL, MUL, [mx], [nb])
        KT = fw.sb("KT", [128, S], BF16)
        Vt = fw.sb("Vt", [128, NK, 128], BF16)
        qrot = Rot([fw.sb("qt%d" % i, [128, 512], BF16) for i in range(3)])
        prob = Rot([fw.sb("pr%d" % i, [128, 2, 512], BF16) for i in range(4)])
        psr = Rot([fw.sb("prs%d" % i, [128, 512], BF16) for i in range(4)])
        rdn = Rot([fw.sb("rdn%d" % i, [128, 512]) for i in range(2)])
        orot = Rot([fw.sb("ot%d" % i, [128, 512], BF16) for i in range(2)])
        P = self.P
        sgrp = [(self.PP[0], P[0], P[1]), (self.PP[1], P[2], P[3]), (self.PP[2], P[4], P[5])]
        porot = Rot(P[6:7]); pdrot = Rot(P[7:8])
        sc = 128.0 ** -0.5
        assert NK % 2 == 0
        NP = NK // 2
        LA = 2
        for g in range(2):
            fw.dma("sp", KT[:], scr["akT"][g], writes=[KT])
            nstep = min(NK, 8)
            for n0 in range(0, NK, nstep):
                fw.dma("sp", Vt[:, n0:n0 + nstep, :],
                       scr["av"][n0 * 128:(n0 + nstep) * 128, g * 128:(g + 1) * 128].rearrange("(n p) e -> p n e", p=128), writes=[Vt])
            jobs = [(hq, tq, kp) for hq in range(4 * g, 4 * g + 4) for tq in range(self.NT) for kp in range(NP)]
            state = {}

            def tile_state(hq, tq):
                key = (hq, tq)
                if key not in state:
                    qt = qrot.next()
                    fw.dma("sp", qt[:], scr["aqT"][hq, :, tq * 512:tq * 512 + 512], writes=[qt])
                    state[key] = (qt, porot.next(), pdrot.next())
                return state[key]

            pend = []
            denq = []
            for i in range(len(jobs) + LA):
                if i < len(jobs):
                    hq, tq, kp = jobs[i]
                    qt, po, pd = tile_state(hq, tq)
                    PPt, b0, b1 = sgrp[i % 3]
                    for j, bk in ((0, b0), (1, b1)):
                        kt = 2 * kp + j
                        self.mm(bk[:], KT[:, kt * 128:(kt + 1) * 128], qt[:], [KT, qt], [bk])
                    pr = prob.next()
                    self.act(pr[:], PPt[:], AF.Exp, [b0, b1, nb], [pr], bias=nb[:, 0:1], scale=sc)
                    pend.append((hq, tq, kp, pr, po, pd))
                if i >= LA:
                    hq, tq, kp, pr, po, pd = pend.pop(0)
                    prs = psr.next()
                    self.tt(prs[:], pr[:, 0, :], pr[:, 1, :], ADD, [pr], [prs])
                    for j in range(2):
                        kt = 2 * kp + j
                        self.mm(po[:], Vt[:, kt, :], pr[:, j, :], [Vt, pr], [po], start=(kt == 0), stop=(kt == NK - 1))
                    if denq:
                        dprs, dpd, dkp = denq.pop(0)
                        self.mm(dpd[:], self.onesb, dprs[:], [self.cb, dprs], [dpd], start=(dkp == 0), stop=(dkp == NP - 1))
                    denq.append((prs, pd, kp))
                    if kp == NP - 1:
                        dprs, dpd, dkp = denq.pop(0)
                        self.mm(dpd[:], self.onesb, dprs[:], [self.cb, dprs], [dpd], start=(dkp == 0), stop=(dkp == NP - 1))
                    if kp == NP - 1:
                        r = rdn.next()
                        self.recip(r[:], pd[:], [pd], [r])
                        ot = orot.next()
                        self.tt(ot[:], po[:], r[:], MUL, [po, r], [ot])
                        fw.dma("pool", scr["aoT"][hq, :, tq * 512:tq * 512 + 512], ot[:], reads=[ot])
                        del state[(hq, tq)]

    def phase_E(self, l):
        fw, S, scr = self.fw, self.S, self.scr
        fw.reset_arena(self.keep)
        last = l == self.L - 1
        gains = fw.sb("gains", [128, 3, 8])
        fw.dma("sp", gains[:], self.gains_d[l], writes=[gains])
        xrot = Rot([fw.sb("xT%d" % i, [128, DC, 512]) for i in range(2)])
        xn = fw.sb("xn", [128, DC, 512], BF16)
        gT = fw.sb("gT", [128, FC, 512], BF16)
        w13rot = Rot([fw.sb("w13_%d" % i, [128, 2, DC, 256], BF16) for i in range(2)])
        w2rot = Rot([fw.sb("w2_%d" % i, [128, FC, 256], BF16) for i in range(2)])
        wbr = Rot([fw.sb("wbr_%d" % i, [128, DC, 256], BF16) for i in range(4)])
        srot = Rot([fw.sb("s_%d" % i, [128, 512]) for i in range(3)])
        sqrot = Rot([fw.sb("sq_%d" % i, [128, 512]) for i in range(3)])
        rsd = fw.sb("rsd", [128, 512])
        bins = [fw.sb("bin_%d" % i, [128, DC, 512], BF16) for i in range(3)]
        gtr = Rot([fw.sb("gt_%d" % i, [128, 3, 2, 512], BF16) for i in range(2)])
        merged = fw.sb("merged", [128, DC, 512], BF16)
        m1r = Rot([fw.sb("m1_%d" % i, [128, 512]) for i in range(2)])
        m2r = Rot([fw.sb("m2_%d" % i, [128, 512]) for i in range(2)])
        ytr = Rot([fw.sb("yt_%d" % i, [128, 1024]) for i in range(2)]) if last else None
        prot = Rot(self.P)
        bnames = ("w_branch_gdn", "w_branch_ret", "w_branch_attn")
        bsrc = ("goT", "roT", "aoT")
        for tt in range(self.NT):
            t0 = tt * 512
            xT = xrot.next()
            fw.dma("sp", xT[:], scr["x1T"][:, :, t0:t0 + 512].rearrange("c p t -> p c t"), writes=[xT])
            for b in range(3):
                fw.dma("sp", bins[b][:], scr[bsrc[b]][:, :, t0:t0 + 512].rearrange("c p t -> p c t"), writes=[bins[b]])
            for g in range(4):
                ws = []
                for b in range(3):
                    w = wbr.next()
                    fw.dma("sp", w[:], self.wbf[bnames[b]][l, :, g * 256:(g + 1) * 256].rearrange("(k p) f -> p k f", p=128), writes=[w])
                    ws.append(w)
                gt = gtr.next()
                for b in range(3):
                    fw.dma("sp", gt[:, b], scr["gatesT"][b * 8 + 2 * g:b * 8 + 2 * g + 2, :, t0:t0 + 512].rearrange("c p t -> p c t"), writes=[gt])
                for j in range(2):
                    dc = 2 * g + j
                    pb = []
                    for b in range(3):
                        pp = prot.next()
                        for k in range(DC):
                            self.mm(pp[:], ws[b][:, k, j * 128:(j + 1) * 128], bins[b][:, k, :], [ws[b], bins[b]], [pp],
                                    start=(k == 0), stop=(k == DC - 1))
                        pb.append(pp)
                    m1, m2 = m1r.next(), m2r.next()
                    self.tt(m1[:], pb[0][:], gt[:, 0, j, :], MUL, [pb[0], gt], [m1])
                    self.tt(m2[:], pb[1][:], gt[:, 1, j, :], MUL, [pb[1], gt], [m2])
                    self.tt(m1[:], m1[:], m2[:], ADD, [m1, m2], [m1])
                    self.tt(m2[:], pb[2][:], gt[:, 2, j, :], MUL, [pb[2], gt], [m2])
                    self.tt(merged[:, dc, :], m1[:], m2[:], ADD, [m1, m2], [merged])
            for g in range(4):
                w = wbr.next()
                fw.dma("sp", w[:], self.wbf["w_out"][l, :, g * 256:(g + 1) * 256].rearrange("(k p) f -> p k f", p=128), writes=[w])
                for j in range(2):
                    dc = 2 * g + j
                    pp = prot.next()
                    for k in range(DC):
                        self.mm(pp[:], w[:, k, j * 128:(j + 1) * 128], merged[:, k, :], [w, merged], [pp], start=(k == 0), stop=(k == DC - 1))
                    self.tt(xT[:, dc, :], xT[:, dc, :], pp[:], ADD, [xT, pp], [xT])
            self.rmsnorm(xT, gains, 2, xn, sqrot, rsd, prot.next())
            self.ffn(l, "ffn2", xT, xn, gT, w13rot, w2rot, srot, prot)
            if not last:
                fw.dma("pool", scr["xT"][:, :, t0:t0 + 512].rearrange("c p t -> p c t"), xT[:], reads=[xT])
            else:
                for blk in range(4):
                    yt = ytr.next()
                    for g in range(2):
                        pp = prot.next()
                        for j in range(4):
                            c = g * 4 + j
                            self.mm(pp[:, j * 128:(j + 1) * 128], xT[:, c, blk * 128:(blk + 1) * 128], self.identf, [xT, self.cf], [pp])
                        if g:
                            self.acopy(yt[:, g * 512:(g + 1) * 512], pp[:], [pp], [yt])
                        else:
                            self.vcopy(yt[:, g * 512:(g + 1) * 512], pp[:], [pp], [yt])
                    fw.dma("pool", self.y_out[t0 + blk * 128:t0 + (blk + 1) * 128, :], yt[:], reads=[yt])


def _rope_tables(S):
    f32 = np.float32
    theta = f32(10000.0)

    def angles(pos, dim):
        inv = theta ** (-(np.arange(0, dim, 2, dtype=f32)) / f32(dim))
        ang = pos.astype(f32)[:, None] * inv[None, :].astype(f32)
        return np.cos(ang).astype(f32), np.sin(ang).astype(f32)

    tab = np.zeros((6, 128, S), f32)
    c, s = angles(np.arange(S), 128)
    tab[0, :64], tab[0, 64:] = c.T, c.T
    tab[1, :64], tab[1, 64:] = s.T, -s.T
    k = f32(128.0 ** -0.5)
    tab[2], tab[3] = tab[0] * k, tab[1] * k
    rows = np.repeat(np.arange(S // 64), 64)
    cols = np.tile(np.arange(64), S // 64)
    cr, sr = angles(rows, 64)
    cc, sc = angles(cols, 64)
    tab[4, 0:32], tab[4, 32:64], tab[4, 64:96], tab[4, 96:128] = cr.T, cr.T, cc.T, cc.T
    tab[5, 0:32], tab[5, 32:64], tab[5, 64:96], tab[5, 96:128] = sr.T, -sr.T, sc.T, -sc.T
    return tab


def _consts():
    f32 = np.float32
    p = np.arange(128)[:, None]
    f = np.arange(128)[None, :]
    cf = np.zeros((128, 9, 128), f32)
    cf[:, 0] = np.eye(128, dtype=f32)
    cf[:, 1] = 1.0
    cf[:, 2] = (p <= f)
    cf[:, 3] = (p >= f)
    cf[:, 4] = BIG * (p <= f)
    cf[:, 5] = BIG * (p >= f)
    cf[:, 6] = -BIG * (f < p)
    cf[:, 7] = -BIG * (f > p)
    cb = np.zeros((128, 2, 128), f32)
    cb[:, 0] = np.eye(128, dtype=f32)
    cb[:, 1] = 1.0
    return cf, cb.astype(ml_dtypes.bfloat16)


def _small_params(inp, L):
    f32 = np.float32
    A = lambda k: np.asarray(inp[k], dtype=f32)
    fm = lambda v: np.ascontiguousarray(v.reshape(L, 8, 128).transpose(0, 2, 1))
    out = {}
    out["gains_fm"] = np.ascontiguousarray(np.stack([fm(A("ffn1_norm")), fm(A("mix_norm")), fm(A("ffn2_norm"))], axis=2))
    out["conv_fm"] = np.ascontiguousarray(A("gdn_conv").reshape(L, 5, 16, 128).transpose(0, 3, 2, 1))
    rep = lambda v: np.ascontiguousarray(np.broadcast_to(v[:, None], (L, 128) + v.shape[1:]))
    out["gdnnorm_rep"] = rep(A("gdn_norm"))
    out["retnorm_rep"] = rep(A("ret_norm"))
    out["qk_gain_col"] = np.ascontiguousarray(np.stack([A("attn_q_norm"), A("attn_k_norm")], axis=2))
    out["qk_gain_rep"] = rep(np.stack([A("attn_q_norm"), A("attn_k_norm")], axis=1))
    out["alog_rep"] = rep(np.broadcast_to(A("gdn_A_log").reshape(L, 1, 8), (L, 4, 8)))
    out["dtb_rep"] = rep(np.broadcast_to(A("gdn_dt_bias").reshape(L, 1, 8), (L, 4, 8)))
    out["rlogit_rep"] = rep(A("ret_decay_logit").reshape(L, 8))
    return out


_CACHE = {}


def _program(S, L, dbg=(), phases=None):
    key = (S, L, tuple(dbg), None if phases is None else tuple(phases))
    if key not in _CACHE:
        nc = bass.Bass("TRN2", target_bir_lowering=False)
        K(nc, S, L, dbg, phases)
        _CACHE[key] = nc
    return _CACHE[key]


def run_cores(seqs, inp, L, dbg=(), phases=None, n_cores=None):
    S = seqs[0].shape[0]
    nc = _program(S, L, dbg, phases)
    cf, cb = _consts()
    common = {n: np.ascontiguousarray(np.asarray(inp[n], dtype=np.float32)) for n in WNAMES}
    common.update(_small_params(inp, L))
    common["c_f32"] = cf
    common["c_bf16"] = cb
    common["rope_tab"] = _rope_tables(S)
    in_maps = []
    for s in seqs:
        m = dict(common)
        m["x"] = np.ascontiguousarray(np.asarray(s, dtype=np.float32))
        in_maps.append(m)
    res = run_bass_kernel_spmd(nc, in_maps, core_ids=list(range(len(seqs))))
    return res.results


def kernel(**inputs):
    xp = np.asarray(inputs["x_prompt"], dtype=np.float32)
    xs = np.asarray(inputs["x_sample"], dtype=np.float32)
    L = np.asarray(inputs["w_in"]).shape[0]
    seqs = [xp[i] for i in range(xp.shape[0])] + [xs[i] for i in range(xs.shape[0])]
    n_real = len(seqs)
    while len(seqs) < 8:
        seqs.append(seqs[0])
    res = run_cores(seqs, inputs, L)
    ys = [np.asarray(res[i]["y"], dtype=np.float32) for i in range(n_real)]
    y_prompt = np.stack(ys[:xp.shape[0]], axis=0)
    y_sample = np.stack(ys[xp.shape[0]:], axis=0)
    return (y_prompt, y_sample)
```
